# Optimizing a Trainium2 kernel written in Bass

```python
import math
import jax, jax.numpy as jnp
from jax import lax
import numpy as np

D_MODEL = 2048
BATCH = 16
SEQ = 2048
DEPTH = 1
DEC_BATCH = 4
DEC_SEQ = 4096
PAST_LEN = 128

HEAD_DIM = 128
HEADS_PER_GROUP = 4
DILATED_GROUPS = ((128, 1), (512, 4), (2048, 16))
N_GROUPS = len(DILATED_GROUPS)
N_ATTN_HEADS = N_GROUPS * HEADS_PER_GROUP
ATTN_W = N_ATTN_HEADS * HEAD_DIM
ATTN_OUT_W = HEADS_PER_GROUP * HEAD_DIM
NUM_BUCKETS = 32
REL_MAX_DIST = 1024
LRU_W = 1536
LRU_BLOCKS = 12
LRU_BLOCK_W = LRU_W // LRU_BLOCKS
LRU_C = 8.0
CONV_W = 4
D_FF = 4 * D_MODEL
NORM_EPS = 1e-6
NEG_INF = -1e30
D_IN = 3 * ATTN_W + 2 * LRU_W + 2 * D_MODEL
SPLIT_POINTS = (ATTN_W, 2 * ATTN_W, 3 * ATTN_W, 3 * ATTN_W + LRU_W,
                3 * ATTN_W + 2 * LRU_W, 3 * ATTN_W + 2 * LRU_W + D_MODEL)

kernel_name = "hybrid_dilated_attn_rglru_encoder"


def _rmsnorm(x, g):
    xf = x.astype(jnp.float32)
    y = xf * lax.rsqrt(jnp.mean(xf * xf, axis=-1, keepdims=True) + NORM_EPS)
    return (y * g.astype(jnp.float32)).astype(x.dtype)


def _t5_bucket(rel):
    half = NUM_BUCKETS // 2
    max_exact = half // 2
    n = jnp.abs(rel)
    nf = jnp.maximum(n, 1).astype(jnp.float32)
    large = max_exact + (jnp.log(nf / max_exact) / math.log(REL_MAX_DIST / max_exact)
                         * (half - max_exact)).astype(jnp.int32)
    large = jnp.minimum(large, half - 1)
    return jnp.where(rel > 0, half, 0) + jnp.where(n < max_exact, n, large)


def _dilated_group(q, k, v, bias_tab, window, dil):
    B, S, H, hd = q.shape
    L = S // dil
    C = window // (2 * dil)
    N = B * dil

    def fold(t):
        return t.reshape(B, L, dil, H, hd).transpose(0, 2, 1, 3, 4).reshape(N, L, H, hd)

    q, k, v = fold(q), fold(k), fold(v)
    nb = -(-L // C)
    Lp = nb * C
    qb = jnp.pad(q, ((0, 0), (0, Lp - L), (0, 0), (0, 0))).reshape(N, nb, C, H, hd)

    def band(t):
        tp = jnp.pad(t, ((0, 0), (C, Lp - L + C), (0, 0), (0, 0))).reshape(N, nb + 2, C, H, hd)
        return jnp.concatenate([tp[:, :-2], tp[:, 1:-1], tp[:, 2:]], axis=2)

    kb, vb = band(k), band(v)
    s = jnp.einsum('nbqhd,nbkhd->nbhqk', qb, kb).astype(jnp.float32) * (HEAD_DIM ** -0.5)
    qi = jnp.arange(C, dtype=jnp.int32)[:, None]
    ki = jnp.arange(3 * C, dtype=jnp.int32)[None, :] - C
    rel = ki - qi
    key_idx = jnp.arange(nb, dtype=jnp.int32)[:, None, None] * C + ki[None]
    valid = (jnp.abs(rel) <= C)[None] & (key_idx >= 0) & (key_idx < L)
    bias = bias_tab.astype(jnp.float32)[_t5_bucket(rel * dil)].transpose(2, 0, 1)
    s = jnp.where(valid[None, :, None], s + bias[None, None], NEG_INF)
    m = jnp.max(s, axis=-1)
    p = jnp.exp(s - m[..., None])
    l = jnp.sum(p, axis=-1)
    o = jnp.einsum('nbhqk,nbkhd->nbqhd', p, vb.astype(jnp.float32))
    o = o / l.transpose(0, 1, 3, 2)[..., None]
    lse = (m + jnp.log(l)).transpose(0, 1, 3, 2)
    o = o.reshape(N, Lp, H, hd)[:, :L]
    lse = lse.reshape(N, Lp, H)[:, :L]
    o = o.reshape(B, dil, L, H, hd).transpose(0, 2, 1, 3, 4).reshape(B, S, H, hd)
    lse = lse.reshape(B, dil, L, H).transpose(0, 2, 1, 3).reshape(B, S, H)
    return o, lse


def _centred_conv(x, w, b):
    S = x.shape[1]
    left = CONV_W // 2
    xp = jnp.pad(x, ((0, 0), (left, CONV_W - 1 - left), (0, 0)))
    y = b
    for t in range(CONV_W):
        y = y + xp[:, t:t + S] * w[t]
    return y


def _block_diag(x, w, b):
    B, S, _ = x.shape
    xb = x.reshape(B, S, LRU_BLOCKS, LRU_BLOCK_W)
    return jnp.einsum('bsni,nij->bsnj', xb, w.astype(jnp.float32)).reshape(B, S, LRU_W) + b.astype(jnp.float32)


def _lin_combine(e1, e2):
    a1, b1 = e1
    a2, b2 = e2
    return a1 * a2, a2 * b1 + b2


def _rglru(x, wa, ba, wx, bx, lam):
    r = jax.nn.sigmoid(_block_diag(x, wa, ba))
    i = jax.nn.sigmoid(_block_diag(x, wx, bx))
    log_a = -LRU_C * r * jax.nn.softplus(-lam.astype(jnp.float32))
    a = jnp.exp(log_a)
    u = jnp.sqrt(-jnp.expm1(2.0 * log_a)) * (i * x)
    _, h = lax.associative_scan(_lin_combine, (a, u), axis=1)
    return h


def _mixer(u, rel_bias, w_in, conv_w, conv_b, lru_wa, lru_ba, lru_wx, lru_bx, lru_lambda,
           w_attn_o, w_rnn_o, w_out):
    B, S, _ = u.shape
    z = u @ w_in
    q, k, v, rx, ry, g_attn, g_rnn = jnp.split(z, SPLIT_POINTS, axis=-1)
    q = q.reshape(B, S, N_ATTN_HEADS, HEAD_DIM)
    k = k.reshape(B, S, N_ATTN_HEADS, HEAD_DIM)
    v = v.reshape(B, S, N_ATTN_HEADS, HEAD_DIM)
    outs, lses = [], []
    for g, (window, dil) in enumerate(DILATED_GROUPS):
        hs = slice(g * HEADS_PER_GROUP, (g + 1) * HEADS_PER_GROUP)
        o, lse = _dilated_group(q[:, :, hs], k[:, :, hs], v[:, :, hs], rel_bias[:, hs], window, dil)
        outs.append(o)
        lses.append(lse)
    wts = jax.nn.softmax(jnp.stack(lses), axis=0)
    attn = jnp.sum(wts[..., None] * jnp.stack(outs), axis=0).reshape(B, S, ATTN_OUT_W).astype(u.dtype)

    xc = _centred_conv(rx.astype(jnp.float32), conv_w.astype(jnp.float32), conv_b.astype(jnp.float32))
    h_fwd = _rglru(xc, lru_wa[0], lru_ba[0], lru_wx[0], lru_bx[0], lru_lambda[0])
    h_bwd = jnp.flip(_rglru(jnp.flip(xc, axis=1), lru_wa[1], lru_ba[1], lru_wx[1], lru_bx[1],
                            lru_lambda[1]), axis=1)
    rnn = ((h_fwd + h_bwd) * jax.nn.gelu(ry.astype(jnp.float32))).astype(u.dtype)

    merged = jax.nn.sigmoid(g_attn) * (attn @ w_attn_o) + jax.nn.sigmoid(g_rnn) * (rnn @ w_rnn_o)
    return merged @ w_out


def _mlp(u, w1, w2):
    return jnp.square(jax.nn.relu(u @ w1)) @ w2


def setup_inputs(seed: int = 0) -> dict:
    key = jax.random.key(seed)
    ks = jax.random.split(key, 24)
    f32 = jnp.float32
    nrm = lambda k, shape, scale: jax.random.normal(k, shape, f32) * scale
    u = jax.random.uniform(ks[12], (DEPTH, 2, LRU_W), f32, minval=0.9, maxval=0.999)
    a0 = u ** (1.0 / LRU_C)
    lru_lambda = jnp.log(a0) - jnp.log1p(-a0)
    return {
        "x_prompt": nrm(ks[0], (BATCH, SEQ, D_MODEL), 1.0),
        "x_sample": nrm(ks[1], (DEC_BATCH, DEC_SEQ, D_MODEL), 1.0),
        "rel_bias": nrm(ks[2], (NUM_BUCKETS, N_ATTN_HEADS), 0.5),
        "norm_mix_g": 1.0 + nrm(ks[3], (DEPTH, D_MODEL), 0.05),
        "w_in": nrm(ks[4], (DEPTH, D_MODEL, D_IN), D_MODEL ** -0.5),
        "conv_w": nrm(ks[5], (DEPTH, CONV_W, LRU_W), 0.5),
        "conv_b": nrm(ks[6], (DEPTH, LRU_W), 0.05),
        "lru_wa": nrm(ks[7], (DEPTH, 2, LRU_BLOCKS, LRU_BLOCK_W, LRU_BLOCK_W), LRU_BLOCK_W ** -0.5),
        "lru_ba": nrm(ks[8], (DEPTH, 2, LRU_W), 0.1),
        "lru_wx": nrm(ks[9], (DEPTH, 2, LRU_BLOCKS, LRU_BLOCK_W, LRU_BLOCK_W), LRU_BLOCK_W ** -0.5),
        "lru_bx": nrm(ks[10], (DEPTH, 2, LRU_W), 0.1),
        "lru_lambda": lru_lambda,
        "w_attn_o": nrm(ks[13], (DEPTH, ATTN_OUT_W, D_MODEL), ATTN_OUT_W ** -0.5),
        "w_rnn_o": nrm(ks[14], (DEPTH, LRU_W, D_MODEL), LRU_W ** -0.5),
        "w_out": nrm(ks[15], (DEPTH, D_MODEL, D_MODEL), D_MODEL ** -0.5),
        "norm_mlp_g": 1.0 + nrm(ks[16], (DEPTH, D_MODEL), 0.05),
        "w_mlp_in": nrm(ks[17], (DEPTH, D_MODEL, D_FF), D_MODEL ** -0.5),
        "w_mlp_out": nrm(ks[18], (DEPTH, D_FF, D_MODEL), D_FF ** -0.5),
        "norm_final_g": 1.0 + nrm(ks[19], (D_MODEL,), 0.05),
    }


def reference(x_prompt, x_sample, rel_bias, norm_mix_g, w_in, conv_w, conv_b, lru_wa, lru_ba,
              lru_wx, lru_bx, lru_lambda, w_attn_o, w_rnn_o, w_out, norm_mlp_g, w_mlp_in,
              w_mlp_out, norm_final_g):
    def trunk(x):
        for l in range(DEPTH):
            x = x + _mixer(_rmsnorm(x, norm_mix_g[l]), rel_bias, w_in[l], conv_w[l], conv_b[l],
                           lru_wa[l], lru_ba[l], lru_wx[l], lru_bx[l], lru_lambda[l],
                           w_attn_o[l], w_rnn_o[l], w_out[l])
            x = x + _mlp(_rmsnorm(x, norm_mlp_g[l]), w_mlp_in[l], w_mlp_out[l])
        return _rmsnorm(x, norm_final_g)

    y_prompt = trunk(x_prompt)
    y_sample = trunk(x_sample)
    return (y_prompt, y_sample)
```

```python
import contextlib
import numpy as np
import concourse.bass as bass
import concourse.mybir as mybir
from concourse.bass_utils import run_bass_kernel_spmd

F32 = mybir.dt.float32
BF16 = mybir.dt.bfloat16
AF = mybir.ActivationFunctionType
ALU = mybir.AluOpType

D = 2048
T = 6144
NCH = 12
CH = 512
D_IN = 11776
DFF = 8192
NEG = -30000.0
EPS = 1e-6
SAME_ENGINE_WAITS = True
OPTS = {}


class Buf:
    __slots__ = ("name", "w", "r", "dsem", "dcount", "excl")

    def __init__(self, name):
        self.name = name
        self.w = {}
        self.r = {}
        self.dsem = None
        self.dcount = 0
        self.excl = name.startswith("ps_")


class Eng:
    def __init__(self, name, sem):
        self.name = name
        self.sem = sem
        self.count = 0
        self.waited = {}
        self.prog = []


class Emitter:
    def __init__(self, nc, stack):
        self.nc = nc
        self.stack = stack
        self.engs = {}
        for n in ("pe", "act", "dve", "pool", "sp"):
            self.engs[n] = Eng(n, stack.enter_context(nc.semaphore("sem_" + n)))
        self.nbuf = 0

    def buf(self, name=None):
        self.nbuf += 1
        return Buf(name or f"b{self.nbuf}")

    def _dsem(self, b):
        if b.dsem is None:
            b.dsem = self.stack.enter_context(self.nc.semaphore("ds_" + b.name))
        return b.dsem

    def _deps(self, eng, reads, writes):
        deps = {}

        def add(tok):
            if tok is None:
                return
            s, v = tok
            if deps.get(s, 0) < v:
                deps[s] = v

        for b in reads:
            for s, v in b.w.items():
                add((s, v))
            if b.excl:
                for s, v in b.r.items():
                    if s is not eng.sem:
                        add((s, v))
        for b in writes:
            for s, v in b.w.items():
                add((s, v))
            for s, v in b.r.items():
                add((s, v))
        for s, v in deps.items():
            if s is eng.sem and (eng.name == "pe" or not SAME_ENGINE_WAITS):
                continue
            if eng.waited.get(s, 0) < v:
                eng.waited[s] = v
                eng.prog.append(lambda h, s=s, v=v: h.wait_ge(s, v))

    def _mark(self, tok, reads, writes):
        s, v = tok
        for b in reads:
            if b.r.get(s, 0) < v:
                b.r[s] = v
        for b in writes:
            if b.w.get(s, 0) < v:
                b.w[s] = v
            b.r = {}

    def op(self, engname, fn, reads=(), writes=()):
        eng = self.engs[engname]
        self._deps(eng, reads, writes)
        eng.count += 1
        cnt = eng.count
        sem = eng.sem
        eng.prog.append(lambda h, fn=fn, sem=sem: fn(h).then_inc(sem, 1))
        self._mark((sem, cnt), reads, writes)

    def mm_group(self, fns, reads=(), writes=()):
        eng = self.engs["pe"]
        self._deps(eng, reads, writes)
        eng.count += 1
        cnt = eng.count
        sem = eng.sem
        for f in fns[:-1]:
            eng.prog.append(lambda h, f=f: f(h))
        eng.prog.append(lambda h, f=fns[-1], sem=sem: f(h).then_inc(sem, 1))
        self._mark((sem, cnt), reads, writes)

    def dma(self, out, in_, reads=(), writes=(), sembuf=None, q="sp", **kw):
        eng = self.engs[q]
        self._deps(eng, reads, writes)
        b = sembuf or (writes[0] if writes else reads[0])
        s = self._dsem(b)
        b.dcount += 16
        v = b.dcount
        eng.prog.append(lambda h, out=out, in_=in_, s=s, kw=kw: h.dma_start(out=out, in_=in_, **kw).then_inc(s, 16))
        self._mark((s, v), reads, writes)

    def flush(self, final_waits=()):
        nc = self.nc
        progs = {n: e.prog for n, e in self.engs.items()}
        for e in self.engs.values():
            e.prog = []
        with nc.Block() as block:
            @block.tensor
            def _(h):
                for t in progs["pe"]:
                    t(h)

            @block.scalar
            def _(h):
                for t in progs["act"]:
                    t(h)

            @block.vector
            def _(h):
                for t in progs["dve"]:
                    t(h)

            @block.gpsimd
            def _(h):
                for t in progs["pool"]:
                    t(h)

            @block.sync
            def _(h):
                for t in progs["sp"]:
                    t(h)
                for b in final_waits:
                    for s, v in list(b.w.items()) + list(b.r.items()):
                        h.wait_ge(s, v)


def r3(ap, pat, **kw):
    return ap.rearrange(pat, **kw)


def build(debug=None):
    nc = bass.Bass("TRN2", target_bir_lowering=False)
    dt = nc.dram_tensor
    x_d = dt("x", [T, D], F32, kind="ExternalInput").ap()
    w_in_d = dt("w_in", [D, D_IN], F32, kind="ExternalInput").ap()
    w_ao_d = dt("w_attn_o", [512, D], F32, kind="ExternalInput").ap()
    w_ro_d = dt("w_rnn_o", [1536, D], F32, kind="ExternalInput").ap()
    w_out_d = dt("w_out", [D, D], F32, kind="ExternalInput").ap()
    w1_d = dt("w_mlp_in", [D, DFF], F32, kind="ExternalInput").ap()
    w2_d = dt("w_mlp_out", [DFF, D], F32, kind="ExternalInput").ap()
    lwa_d = dt("lru_wa", [24, 128, 128], F32, kind="ExternalInput").ap()
    lwx_d = dt("lru_wx", [24, 128, 128], F32, kind="ExternalInput").ap()
    gains_d = dt("gains", [3, D], F32, kind="ExternalInput").ap()
    pc_d = dt("pc", [128, 136], F32, kind="ExternalInput").ap()
    bt_d = dt("bt", [128, 12 * 256], F32, kind="ExternalInput").ap()
    ident_d = dt("ident", [128, 128], BF16, kind="ExternalInput").ap()
    flag_d = dt("flag", [128, 1], F32, kind="ExternalInput").ap()
    gT_d = dt("gT", [128, 16], F32, kind="ExternalInput").ap()
    y_d = dt("y", [T, D], F32, kind="ExternalOutput").ap()

    uT_d = dt("uT_s", [NCH, 128, 16 * CH], BF16, kind="Internal").ap()
    art_d = dt("art_s", [NCH, 128, 16 * CH], BF16, kind="Internal").ap()
    wA_d = dt("wA_s", [12, 128, 6144], BF16, kind="Internal").ap()
    wL_d = dt("wL_s", [12, 128, 4096], BF16, kind="Internal").ap()
    wG_d = dt("wG_s", [8, 128, 8192], BF16, kind="Internal").ap()
    wAO_d = dt("wAO_s", [1, 128, 8192], BF16, kind="Internal").ap()
    wRO_d = dt("wRO_s", [4, 128, 6144], BF16, kind="Internal").ap()
    wO_d = dt("wO_s", [4, 128, 8192], BF16, kind="Internal").ap()
    w1_s = dt("w1_s", [16, 128, 8192], BF16, kind="Internal").ap()
    w2_s = dt("w2_s", [16, 128, 8192], BF16, kind="Internal").ap()

    dbg = {}
    if debug:
        for name, shape, dty in debug:
            dbg[name] = dt("dbg_" + name, shape, dty, kind="ExternalOutput").ap()

    with contextlib.ExitStack() as gstack:
        em = Emitter(nc, gstack)
        sb = lambda st, name, shape, dty: st.enter_context(nc.sbuf_tensor("s_" + name, shape, dty))
        ps = lambda st, name, shape, dty: st.enter_context(nc.psum_tensor("p_" + name, shape, dty))

        ident = sb(gstack, "ident", [128, 128], BF16)
        pc = sb(gstack, "pc", [128, 136], F32)
        lw = sb(gstack, "lw", [128, 48 * 128], BF16)
        b_ident, b_pc, b_lw = em.buf("ident"), em.buf("pc"), em.buf("lw")
        em.dma(ident[:], ident_d[:, :], writes=[b_ident])
        em.dma(pc[:], pc_d[:, :], writes=[b_pc])
        fl = sb(gstack, "fl", [128, 2], F32)
        b_fl = em.buf("fl")
        em.dma(fl[:, 0:1], flag_d[:, :], writes=[b_fl])
        em.op("dve", lambda h: h.tensor_scalar(out=fl[:, 1:2], in0=fl[:, 0:1], scalar1=-1.0, scalar2=-NEG,
                                               op0=ALU.add, op1=ALU.mult), reads=[b_fl], writes=[b_fl])
        ones = sb(gstack, "ones", [128, 128], BF16)
        b_ones = em.buf("ones")
        em.op("dve", lambda h: h.memset(ones[:], 1.0), writes=[b_ones])

        with contextlib.ExitStack() as st:
            stg = [sb(st, f"stg{i}", [128, 8192], F32) for i in range(2)]
            slb = [sb(st, f"slb{i}", [128, 8192], BF16) for i in range(2)]
            b_stg = [em.buf(f"stg{i}") for i in range(2)]
            b_slb = [em.buf(f"slb{i}") for i in range(2)]
            b_wd = em.buf("wdram")
            state = {"n": 0, "ci": 0}

            def cast(out, in_, rb, wb):
                state["ci"] += 1
                if state["ci"] % 2 == 0:
                    em.op("act", lambda h: h.activation(out=out, in_=in_, func=AF.Copy), reads=[rb], writes=[wb])
                else:
                    em.op("dve", lambda h: h.tensor_copy(out=out, in_=in_), reads=[rb], writes=[wb])

            def prep(loads, casts, nel, dst, sbuf_dst=None):
                i = state["n"] % 2
                state["n"] += 1
                S, L = stg[i], slb[i]
                for vf, src in loads:
                    em.dma(vf(S), src, writes=[b_stg[i]])
                if sbuf_dst is not None:
                    for of, inf in casts:
                        cast(of(sbuf_dst[0]), inf(S), b_stg[i], sbuf_dst[1])
                    return
                for of, inf in casts:
                    cast(of(L), inf(S), b_stg[i], b_slb[i])
                em.dma(dst, L[:, 0:nel], reads=[b_slb[i]], writes=[b_wd], sembuf=b_slb[i])

            def rows(w, nkc):
                return w.rearrange("(kc p) c -> p kc c", p=128)

            win_v = rows(w_in_d, 16)

            def stat_slab(src_v, nkc, c0, dst):
                n = nkc * 512
                loads = [(lambda S: S[:, 0:n].rearrange("p (kc c) -> p kc c", c=512), src_v[:, :, c0:c0 + 512])]
                casts = []
                for cb in range(4):
                    casts.append((
                        lambda L, cb=cb: L[:, cb * nkc * 128:(cb + 1) * nkc * 128].rearrange("p (kc c) -> p kc c", c=128),
                        lambda S, cb=cb: S[:, 0:n].rearrange("p (kc c) -> p kc c", c=512)[:, :, cb * 128:(cb + 1) * 128]))
                prep(loads, casts, n, dst)

            def mov_slab(src_v, c0, dst):
                loads = [(lambda S: S[:, :].rearrange("p (kc c) -> p kc c", c=512), src_v[:, :, c0:c0 + 512])]
                casts = [(lambda L, j=j: L[:, j * 2048:(j + 1) * 2048], lambda S, j=j: S[:, j * 2048:(j + 1) * 2048])
                         for j in range(4)]
                prep(loads, casts, 8192, dst)

            def cols_slab(col_offs, dst):
                nb = len(col_offs)
                loads = [(lambda S, j=j: S[:, j * 2048:(j + 1) * 2048].rearrange("p (kc c) -> p kc c", c=128),
                          win_v[:, :, c0:c0 + 128]) for j, c0 in enumerate(col_offs)]
                casts = [(lambda L, j=j: L[:, j * 2048:(j + 1) * 2048], lambda S, j=j: S[:, j * 2048:(j + 1) * 2048])
                         for j in range(nb)]
                prep(loads, casts, nb * 2048, dst)

            loads = [(lambda S: S[:, 0:3072].rearrange("p (n o) -> p n o", o=128), lwa_d.rearrange("n i o -> i n o")),
                     (lambda S: S[:, 3072:6144].rearrange("p (n o) -> p n o", o=128), lwx_d.rearrange("n i o -> i n o"))]
            casts = [(lambda L, j=j: L[:, j * 3072:(j + 1) * 3072], lambda S, j=j: S[:, j * 3072:(j + 1) * 3072])
                     for j in range(2)]
            prep(loads, casts, 6144, None, sbuf_dst=(lw, b_lw))

            for h in range(4):
                for g in range(3):
                    head = g * 4 + h
                    cols_slab([head * 128, 1536 + head * 128, 3072 + head * 128], wA_d[h * 3 + g, :, :])
            for c in range(12):
                cols_slab([4608 + c * 128, 6144 + c * 128], wL_d[c, :, :])
            for jb in range(4):
                stat_slab(win_v, 16, 7680 + jb * 512, wG_d[jb, :, :])
            for jb in range(4):
                stat_slab(win_v, 16, 9728 + jb * 512, wG_d[4 + jb, :, :])
            ao_v = rows(w_ao_d, 4)
            loads = [(lambda S: S[:, :].rearrange("p (kc c) -> p kc c", c=2048), ao_v)]
            casts = []
            for q in range(4):
                casts.append((
                    lambda L, q=q: L[:, q * 2048:(q + 1) * 2048].rearrange("p (cb kc c) -> p cb kc c", kc=4, c=128),
                    lambda S, q=q: S[:, :].rearrange("p (kc c) -> p kc c", c=2048)[:, :, q * 512:(q + 1) * 512]
                    .rearrange("p kc (cb c) -> p cb kc c", c=128)))
            prep(loads, casts, 8192, wAO_d[0, :, :])
            ro_v = rows(w_ro_d, 12)
            for jb in range(4):
                stat_slab(ro_v, 12, jb * 512, wRO_d[jb, :, :])
            wo_v = rows(w_out_d, 16)
            for fb in range(4):
                mov_slab(wo_v, fb * 512, wO_d[fb, :, :])
            w1_v = rows(w1_d, 16)
            for j in range(16):
                stat_slab(w1_v, 16, j * 512, w1_s[j, :, :])
            w2_v = w2_d.rearrange("(kg kc p) c -> kg p kc c", p=128, kc=16)
            for fb in range(4):
                for kg in range(4):
                    mov_slab(w2_v[kg], fb * 512, w2_s[fb * 4 + kg, :, :])

            if debug and "wA" in dbg:
                em.dma(dbg["wA"][:, :], wA_d[5, :, :], reads=[b_wd], sembuf=b_wd)
                em.dma(dbg["w1"][:, :], w1_s[3, :, :], reads=[b_wd], sembuf=b_wd)
                em.dma(dbg["wAO"][:, :], wAO_d[0, :, :], reads=[b_wd], sembuf=b_wd)

            gbc = sb(st, "gbc", [128, D], F32)
            b_gbc = em.buf("gbc")
            em.dma(gbc[:], gains_d[0].partition_broadcast(128), writes=[b_gbc])
            xt = [sb(st, f"xt{i}", [128, D], F32) for i in range(3)]
            b_xt = [em.buf(f"xt{i}") for i in range(3)]
            junk = sb(st, "junk", [128, D], BF16)
            b_junk = em.buf("junk")
            stt = sb(st, "stt", [128, 8], F32)
            b_stt = [em.buf(f"stt{i}") for i in range(4)]
            ub = [sb(st, f"ub{i}", [128, D], BF16) for i in range(2)]
            b_ub = [em.buf(f"ub{i}") for i in range(2)]
            tp = [ps(st, f"tp{i}", [128, D], BF16) for i in range(2)]
            b_tp = [em.buf(f"ps_tp{i}") for i in range(2)]
            uTc = [sb(st, f"uTc{i}", [128, 16 * CH], BF16) for i in range(2)]
            b_uTc = [em.buf(f"uTc{i}") for i in range(2)]
            b_uTd = em.buf("uTd")
            for i in range(T // 128):
                c, j = divmod(i, 4)
                X, bX = xt[i % 3], b_xt[i % 3]
                k = i % 4
                ssa, rsa, bS = stt[:, 2 * k:2 * k + 1], stt[:, 2 * k + 1:2 * k + 2], b_stt[k]
                U, bU = ub[i % 2], b_ub[i % 2]
                P, bP = tp[i % 2], b_tp[i % 2]
                UT, bUT = uTc[c % 2], b_uTc[c % 2]
                em.dma(X[:], x_d[i * 128:(i + 1) * 128, :], writes=[bX])
                em.op("act", lambda h, X=X, ssa=ssa: h.activation(out=junk[:], in_=X[:], func=AF.Square, accum_out=ssa),
                      reads=[bX], writes=[b_junk, bS])
                em.op("dve", lambda h, ssa=ssa, rsa=rsa: h.tensor_scalar(out=rsa, in0=ssa, scalar1=1.0 / D, scalar2=EPS,
                                                                       op0=ALU.mult, op1=ALU.add), reads=[bS], writes=[bS])
                em.op("act", lambda h, rsa=rsa: h.activation(out=rsa, in_=rsa, func=AF.Sqrt), reads=[bS], writes=[bS])
                em.op("dve", lambda h, rsa=rsa: h.reciprocal(out=rsa, in_=rsa), reads=[bS], writes=[bS])
                em.op("dve", lambda h, X=X, U=U, rsa=rsa: h.scalar_tensor_tensor(out=U[:], in0=X[:], scalar=rsa, in1=gbc[:],
                                                                                op0=ALU.mult, op1=ALU.mult),
                      reads=[bX, bS, b_gbc], writes=[bU])
                em.mm_group([lambda h, kc=kc, P=P, U=U: h.transpose(out=P[:, kc * 128:(kc + 1) * 128],
                                                                   in_=U[:, kc * 128:(kc + 1) * 128], identity=ident[:])
                             for kc in range(16)], reads=[bU, b_ident], writes=[bP])
                em.op("act", lambda h, P=P, UT=UT, j=j: h.activation(
                    out=UT[:, :].rearrange("p (kc t) -> p kc t", t=CH)[:, :, j * 128:(j + 1) * 128],
                    in_=P[:, :].rearrange("p (kc t) -> p kc t", t=128), func=AF.Copy), reads=[bP], writes=[bUT])
                if j == 3:
                    em.dma(uT_d[c, :, :], UT[:, :], reads=[bUT], writes=[b_uTd], sembuf=bUT)
            if debug and "uT" in dbg:
                em.dma(dbg["uT"][:, :, :], uT_d[:, :, :], reads=[b_uTd], sembuf=b_uTd)
            em.flush(final_waits=[b_wd, b_uTd])

        UNITS = [(0, 8, True), (8, 4, False)]
        DILS = (1, 4, 16)
        SCALE = 128.0 ** -0.5
        alt = {"n": 0}

        def evac_eng():
            alt["n"] += 1
            return "act" if alt["n"] % 2 == 0 else "dve"

        def copy_op(eng, out, in_):
            if eng == "act":
                return lambda h: h.activation(out=out, in_=in_, func=AF.Copy)
            return lambda h: h.tensor_copy(out=out, in_=in_)

        with contextlib.ExitStack() as st:
            BT = sb(st, "BT", [128, 12 * 256], F32)
            b_BT = em.buf("BT")
            em.dma(BT[:, :], bt_d[:, :], writes=[b_BT])
            wsl = [sb(st, f"wsl{i}", [128, 6144], BF16) for i in range(2)]
            b_wsl = [em.buf(f"wsl{i}") for i in range(2)]
            uTs = [sb(st, f"uTs{i}", [128, 8192], BF16) for i in range(2)]
            b_uTs = [em.buf(f"uTs{i}") for i in range(2)]
            qkv = [sb(st, f"qkv{i}", [128, 4096], BF16) for i in range(3)]
            b_qkv = [em.buf(f"qkv{i}") for i in range(3)]
            vtok = sb(st, "vtok", [128, 4096], BF16)
            b_vtok = em.buf("vtok")
            acc = [sb(st, f"acc{i}", [128, 4096], F32) for i in range(2)]
            b_acc = [em.buf(f"acc{i}") for i in range(2)]
            PT = [sb(st, f"PT{i}", [128, 512], BF16) for i in range(4)]
            b_PT = [em.buf(f"PT{i}") for i in range(4)]
            tmpS = [sb(st, f"tmpS{i}", [128, 512], F32) for i in range(3)]
            b_tmpS = [em.buf(f"tmpS{i}") for i in range(3)]
            attnT = sb(st, "attnT", [128, 4096], BF16)
            b_attnT = em.buf("attnT")
            Wb = [ps(st, f"Wb{i}", [128, 512], F32) for i in range(2)]
            b_Wb = [em.buf(f"ps_Wb{i}") for i in range(2)]
            Ob = [ps(st, f"Ob{i}", [128, 512], F32) for i in range(2)]
            b_Ob = [em.buf(f"ps_Ob{i}") for i in range(2)]
            Lb = [ps(st, f"Lb{i}", [128, 512], F32) for i in range(2)]
            b_Lb = [em.buf(f"ps_Lb{i}") for i in range(2)]
            VTb = ps(st, "VTb", [128, 1024], BF16)
            b_VTb = em.buf("ps_VTb")
            b_art = em.buf("art")
            cnt = {"w": 0, "u": 0, "W": 0, "pt": 0, "ts": 0, "reg": 0}

            for (c0, nch, joint) in (UNITS if not OPTS.get("skip_a1") else []):
                ulen = nch * CH
                for hs in range(4):
                    for g, d in enumerate(DILS):
                        head = g * 4 + hs
                        Lj = ulen // d
                        wi = cnt["w"] % 2
                        cnt["w"] += 1
                        WS, bWS = wsl[wi], b_wsl[wi]
                        em.dma(WS[:, :], wA_d[hs * 3 + g, :, :], reads=[b_wd], writes=[bWS])
                        for cu in range(nch):
                            ui = cnt["u"] % 2
                            cnt["u"] += 1
                            UT, bUT = uTs[ui], b_uTs[ui]
                            em.dma(UT[:, :], uT_d[c0 + cu, :, :], reads=[b_uTd], writes=[bUT])
                            for blk in range(3):
                                bi = cnt["W"] % 2
                                cnt["W"] += 1
                                bank, bB = Wb[bi], b_Wb[bi]
                                em.mm_group([lambda h, kc=kc, bank=bank, WS=WS, UT=UT, blk=blk: h.matmul(
                                    out=bank[:, :], lhsT=WS[:, blk * 2048 + kc * 128: blk * 2048 + (kc + 1) * 128],
                                    rhs=UT[:, kc * 512:(kc + 1) * 512], start=(kc == 0), stop=(kc == 15))
                                    for kc in range(16)], reads=[bWS, bUT], writes=[bB])
                                dst = qkv[blk][:, 0:ulen].rearrange("p (r l) -> p r l", l=Lj)[:, :, cu * (512 // d):(cu + 1) * (512 // d)]
                                src = bank[:, :].rearrange("p (l r) -> p r l", r=d)
                                e = evac_eng()
                                em.op(e, copy_op(e, dst, src), reads=[bB], writes=[b_qkv[blk]])
                        nkb = ulen // 128
                        for b0 in range(0, nkb, 8):
                            em.mm_group([lambda h, b=b, b0=b0: h.transpose(out=VTb[:, (b - b0) * 128:(b - b0 + 1) * 128],
                                                                         in_=qkv[2][:, b * 128:(b + 1) * 128], identity=ident[:])
                                         for b in range(b0, b0 + 8)], reads=[b_qkv[2], b_ident], writes=[b_VTb])
                            e = evac_eng()
                            em.op(e, copy_op(e, vtok[:, b0 * 128:(b0 + 8) * 128], VTb[:, :]), reads=[b_VTb], writes=[b_vtok])
                        Lseg = Lj // 2 if joint else Lj
                        for R in range(ulen // 512):
                            ri = cnt["reg"] % 2
                            cnt["reg"] += 1
                            OB, bO, LB, bL = Ob[ri], b_Ob[ri], Lb[ri], b_Lb[ri]
                            pieces = []
                            for b in range(nkb):
                                r = (128 * b) // Lj
                                lb = 128 * b - r * Lj
                                qlo, qhi = max(0, lb - 64), min(Lj, lb + 192)
                                cuts = [qlo, qhi]
                                if joint and qlo < Lseg < qhi:
                                    cuts = [qlo, Lseg, qhi]
                                for a_, b_ in zip(cuts[:-1], cuts[1:]):
                                    fa, fb_ = r * Lj + a_, r * Lj + b_
                                    fa2, fb2 = max(fa, 512 * R), min(fb_, 512 * R + 512)
                                    if fa2 >= fb2:
                                        continue
                                    cross = joint and ((a_ >= Lseg) != (lb >= Lseg))
                                    pieces.append((b, fa2, fb2, (fa2 - r * Lj) - (lb - 64), cross))
                            for pi, (b, fa, fb_, j0, cross) in enumerate(pieces):
                                n = fb_ - fa
                                bi = cnt["W"] % 2
                                cnt["W"] += 1
                                bank, bB = Wb[bi], b_Wb[bi]
                                em.mm_group([lambda h, bank=bank, b=b, fa=fa, fb_=fb_, n=n: h.matmul(
                                    out=bank[:, 0:n], lhsT=qkv[1][:, b * 128:(b + 1) * 128], rhs=qkv[0][:, fa:fb_],
                                    start=True, stop=True)], reads=[b_qkv[0], b_qkv[1]], writes=[bB])
                                ti = cnt["ts"] % 3
                                cnt["ts"] += 1
                                TS, bTS = tmpS[ti], b_tmpS[ti]
                                em.op("dve", lambda h, TS=TS, bank=bank, n=n, j0=j0, head=head: h.scalar_tensor_tensor(
                                    out=TS[:, 0:n], in0=bank[:, 0:n], scalar=SCALE, in1=BT[:, head * 256 + j0: head * 256 + j0 + n],
                                    op0=ALU.mult, op1=ALU.add), reads=[bB, b_BT], writes=[bTS])
                                pti = cnt["pt"] % 4
                                cnt["pt"] += 1
                                P_, bP_ = PT[pti], b_PT[pti]
                                if cross:
                                    em.op("act", lambda h, P_=P_, TS=TS, n=n: h.activation(out=P_[:, 0:n], in_=TS[:, 0:n], func=AF.Exp,
                                                                                          bias=fl[:, 1:2]), reads=[bTS, b_fl], writes=[bP_])
                                else:
                                    em.op("act", lambda h, P_=P_, TS=TS, n=n: h.activation(out=P_[:, 0:n], in_=TS[:, 0:n], func=AF.Exp),
                                          reads=[bTS], writes=[bP_])
                                o0 = fa - 512 * R
                                first, last = (pi == 0), (pi == len(pieces) - 1)
                                em.mm_group([
                                    lambda h, OB=OB, b=b, P_=P_, n=n, o0=o0, first=first, last=last: h.matmul(
                                        out=OB[:, o0:o0 + n], lhsT=vtok[:, b * 128:(b + 1) * 128], rhs=P_[:, 0:n],
                                        start=first, stop=last, skip_group_check=True),
                                    lambda h, LB=LB, P_=P_, n=n, o0=o0, first=first, last=last: h.matmul(
                                        out=LB[:, o0:o0 + n], lhsT=ones[:, :], rhs=P_[:, 0:n],
                                        start=first, stop=last, skip_group_check=True)],
                                    reads=[b_vtok, bP_, b_ones], writes=[bO, bL])
                            for (A, bA, BK, bBK) in ((acc[0], b_acc[0], OB, bO), (acc[1], b_acc[1], LB, bL)):
                                Av = A[:, 0:ulen].rearrange("p (l r) -> p r l", r=d)
                                f = 512 * R
                                while f < 512 * R + 512:
                                    r = f // Lj
                                    l0 = f - r * Lj
                                    n = min(Lj - l0, 512 * R + 512 - f)
                                    dst = Av[:, r, l0:l0 + n]
                                    src = BK[:, f - 512 * R: f - 512 * R + n]
                                    if g == 0:
                                        e = evac_eng()
                                        em.op(e, copy_op(e, dst, src), reads=[bBK], writes=[bA])
                                    else:
                                        em.op("dve", lambda h, dst=dst, src=src: h.tensor_tensor(out=dst, in0=dst, in1=src, op=ALU.add),
                                              reads=[bBK, bA], writes=[bA])
                                    f += n
                    em.op("dve", lambda h, ulen=ulen: h.reciprocal(out=acc[1][:, 0:ulen], in_=acc[1][:, 0:ulen]),
                          reads=[b_acc[1]], writes=[b_acc[1]])
                    em.op("dve", lambda h, ulen=ulen: h.tensor_tensor(out=attnT[:, 0:ulen], in0=acc[0][:, 0:ulen], in1=acc[1][:, 0:ulen],
                                                                     op=ALU.mult), reads=[b_acc[0], b_acc[1]], writes=[b_attnT])
                    em.dma(art_d[c0:c0 + nch, :, :].rearrange("c p (f t) -> p c f t", t=512)[:, :, hs, :],
                           attnT[:, 0:ulen].rearrange("p (c t) -> p c t", t=512), reads=[b_attnT], writes=[b_art], sembuf=b_attnT)
            em.flush(final_waits=[b_art])

        with contextlib.ExitStack() as st:
            cf = sb(st, "cf", [128, 24], F32)
            b_cf = em.buf("cf")
            em.op("act", lambda h: h.activation(out=cf[:, :], in_=pc[:, 108:132], func=AF.Exp, scale=-1.0), reads=[b_pc], writes=[b_cf])
            em.op("act", lambda h: h.activation(out=cf[:, :], in_=cf[:, :], func=AF.Ln, bias=1.0), reads=[b_cf], writes=[b_cf])
            em.op("dve", lambda h: h.tensor_scalar(out=cf[:, :], in0=cf[:, :], scalar1=-8.0, scalar2=None, op0=ALU.mult),
                  reads=[b_cf], writes=[b_cf])
            carry = sb(st, "carry", [128, 2], F32)
            b_carry = em.buf("carry")
            dg = sb(st, "dg", [128, 512], BF16)
            b_dg = em.buf("dg")
            wsl = [sb(st, f"wl{i}", [128, 4096], BF16) for i in range(2)]
            b_wsl = [em.buf(f"wl{i}") for i in range(2)]
            uTs = [sb(st, f"uTl{i}", [128, 8192], BF16) for i in range(2)]
            b_uTs = [em.buf(f"uTl{i}") for i in range(2)]
            rxs = [sb(st, f"rxs{i}", [128, 2052], BF16) for i in range(2)]
            b_rxs = [em.buf(f"rxs{i}") for i in range(2)]
            gg = sb(st, "gg", [128, 4096], BF16)
            b_gg = em.buf("gg")
            xc = [sb(st, f"xc{i}", [128, 2048], F32) for i in range(2)]
            b_xc = [em.buf(f"xc{i}") for i in range(2)]
            xcb = [sb(st, f"xcb{i}", [128, 2048], BF16) for i in range(2)]
            b_xcb = [em.buf(f"xcb{i}") for i in range(2)]
            hf = [sb(st, f"hf{i}", [128, 2048], F32) for i in range(2)]
            b_hf = [em.buf(f"hf{i}") for i in range(2)]
            hb = sb(st, "hb", [128, 2048], F32)
            b_hb = em.buf("hb")
            ra = sb(st, "ra", [128, 2048], F32)
            b_ra = em.buf("ra")
            iu = sb(st, "iu", [128, 2048], F32)
            b_iu = em.buf("iu")
            tt = sb(st, "tt", [128, 2048], F32)
            b_tt = em.buf("tt")
            outb = sb(st, "outb", [128, 4096], BF16)
            b_outb = em.buf("outb")
            Wb = [ps(st, f"Wl{i}", [128, 512], F32) for i in range(2)]
            b_Wb = [em.buf(f"ps_Wl{i}") for i in range(2)]
            Cb = [ps(st, f"Cb{i}", [128, 512], F32) for i in range(2)]
            b_Cb = [em.buf(f"ps_Cb{i}") for i in range(2)]
            Gb = [ps(st, f"Gb{i}", [128, 512], F32) for i in range(2)]
            b_Gb = [em.buf(f"ps_Gb{i}") for i in range(2)]
            cnt = {"w": 0, "u": 0, "W": 0, "C": 0, "G": 0}

            def gates_and_elem(dr, s, c):
                for j in range(4):
                    for which, dstT, bD, bcol in ((0, ra, b_ra, 60), (1, iu, b_iu, 84)):
                        gi = cnt["G"] % 2
                        cnt["G"] += 1
                        bank, bB = Gb[gi], b_Gb[gi]
                        w0 = (which * 24 + dr * 12 + c) * 128
                        em.mm_group([lambda h, bank=bank, w0=w0, s=s, j=j: h.matmul(
                            out=bank[:, :], lhsT=lw[:, w0:w0 + 128], rhs=xcb[s][:, j * 512:(j + 1) * 512], start=True, stop=True)],
                            reads=[b_lw, b_xcb[s]], writes=[bB])
                        col = bcol + dr * 12 + c
                        em.op("act", lambda h, bank=bank, dstT=dstT, j=j, col=col: h.activation(
                            out=dstT[:, j * 512:(j + 1) * 512], in_=bank[:, :], func=AF.Sigmoid, bias=pc[:, col:col + 1]),
                            reads=[bB, b_pc], writes=[bD])
                ccol = dr * 12 + c
                em.op("act", lambda h, ccol=ccol: h.activation(out=ra[:, :], in_=ra[:, :], func=AF.Exp, scale=cf[:, ccol:ccol + 1]),
                      reads=[b_ra, b_cf], writes=[b_ra])
                em.op("dve", lambda h: h.tensor_tensor(out=tt[:, :], in0=ra[:, :], in1=ra[:, :], op=ALU.mult), reads=[b_ra], writes=[b_tt])
                em.op("act", lambda h: h.activation(out=tt[:, :], in_=tt[:, :], func=AF.Sqrt, scale=-1.0, bias=1.0),
                      reads=[b_tt], writes=[b_tt])
                em.op("dve", lambda h, s=s: h.tensor_tensor(out=iu[:, :], in0=iu[:, :], in1=xc[s][:, :], op=ALU.mult),
                      reads=[b_iu, b_xc[s]], writes=[b_iu])
                em.op("dve", lambda h: h.tensor_tensor(out=iu[:, :], in0=iu[:, :], in1=tt[:, :], op=ALU.mult),
                      reads=[b_iu, b_tt], writes=[b_iu])

            for (c0, nch, joint) in OPTS.get("a2_units", UNITS):
                nseg = nch // 4
                for c in range(OPTS.get("a2_chunks", 12)):
                    wi = cnt["w"] % 2
                    cnt["w"] += 1
                    WS, bWS = wsl[wi], b_wsl[wi]
                    em.dma(WS[:, :], wL_d[c, :, :], reads=[b_wd], writes=[bWS])
                    for t_ in range(4):
                        em.op("dve", lambda h, t_=t_, c=c: h.tensor_scalar(out=dg[:, t_ * 128:(t_ + 1) * 128], in0=ident[:, :],
                                                                            scalar1=pc[:, t_ * 12 + c: t_ * 12 + c + 1], scalar2=None,
                                                                            op0=ALU.mult), reads=[b_ident, b_pc], writes=[b_dg])
                    for cu in range(nch):
                        ui = cnt["u"] % 2
                        cnt["u"] += 1
                        UT, bUT = uTs[ui], b_uTs[ui]
                        em.dma(UT[:, :], uT_d[c0 + cu, :, :], reads=[b_uTd], writes=[bUT])
                        s, j = divmod(cu, 4)
                        for blk in range(2):
                            bi = cnt["W"] % 2
                            cnt["W"] += 1
                            bank, bB = Wb[bi], b_Wb[bi]
                            em.mm_group([lambda h, kc=kc, bank=bank, WS=WS, UT=UT, blk=blk: h.matmul(
                                out=bank[:, :], lhsT=WS[:, blk * 2048 + kc * 128: blk * 2048 + (kc + 1) * 128],
                                rhs=UT[:, kc * 512:(kc + 1) * 512], start=(kc == 0), stop=(kc == 15))
                                for kc in range(16)], reads=[bWS, bUT], writes=[bB])
                            if blk == 0:
                                em.op("dve", lambda h, bank=bank, s=s, j=j: h.tensor_copy(
                                    out=rxs[s][:, 2 + j * 512: 2 + (j + 1) * 512], in_=bank[:, :]), reads=[bB], writes=[b_rxs[s]])
                            else:
                                em.op("act", lambda h, bank=bank, cu=cu: h.activation(
                                    out=gg[:, cu * 512:(cu + 1) * 512], in_=bank[:, :], func=AF.Gelu_apprx_tanh), reads=[bB], writes=[b_gg])
                    if OPTS.get("a2_stage", 9) < 1:
                        continue
                    em.op("dve", lambda h: h.memset(rxs[0][:, 0:2], 0.0), writes=[b_rxs[0]])
                    em.op("dve", lambda h, nseg=nseg: h.memset(rxs[nseg - 1][:, 2050:2052], 0.0), writes=[b_rxs[nseg - 1]])
                    if joint:
                        em.op("dve", lambda h: h.tensor_scalar(out=rxs[0][:, 2050:2051], in0=rxs[1][:, 2:3], scalar1=fl[:, 0:1],
                                                               scalar2=None, op0=ALU.mult), reads=[b_rxs[1], b_fl], writes=[b_rxs[0]])
                        em.op("dve", lambda h: h.tensor_scalar(out=rxs[1][:, 0:2], in0=rxs[0][:, 2048:2050], scalar1=fl[:, 0:1],
                                                               scalar2=None, op0=ALU.mult), reads=[b_rxs[0], b_fl], writes=[b_rxs[1]])
                    if OPTS.get("a2_stage", 9) < 2:
                        continue
                    for s in range(nseg):
                        for j in range(4):
                            ci = cnt["C"] % 2
                            cnt["C"] += 1
                            bank, bB = Cb[ci], b_Cb[ci]
                            em.mm_group([lambda h, bank=bank, t_=t_, s=s, j=j: h.matmul(
                                out=bank[:, :], lhsT=dg[:, t_ * 128:(t_ + 1) * 128], rhs=rxs[s][:, j * 512 + t_: j * 512 + t_ + 512],
                                start=(t_ == 0), stop=(t_ == 3)) for t_ in range(4)], reads=[b_dg, b_rxs[s]], writes=[bB])
                            em.op("act", lambda h, bank=bank, s=s, j=j, c=c: h.activation(
                                out=xc[s][:, j * 512:(j + 1) * 512], in_=bank[:, :], func=AF.Identity, bias=pc[:, 48 + c:49 + c]),
                                reads=[bB, b_pc], writes=[b_xc[s]])
                            em.op("dve", lambda h, s=s, j=j: h.tensor_copy(
                                out=xcb[s][:, j * 512:(j + 1) * 512], in_=xc[s][:, j * 512:(j + 1) * 512]),
                                reads=[b_xc[s]], writes=[b_xcb[s]])
                    if OPTS.get("a2_stage", 9) < 3:
                        continue
                    for s in range(nseg):
                        gates_and_elem(0, s, c)
                        init = 0.0 if s == 0 else carry[:, 0:1]
                        em.op("dve", lambda h, s=s, init=init: h.tensor_tensor_scan(out=hf[s][:, :], data0=ra[:, :], data1=iu[:, :],
                                                                                  initial=init, op0=ALU.mult, op1=ALU.add),
                              reads=[b_ra, b_iu, b_carry], writes=[b_hf[s]])
                        if joint and s == 0:
                            em.op("dve", lambda h: h.tensor_scalar(out=carry[:, 0:1], in0=hf[0][:, 2047:2048], scalar1=fl[:, 0:1],
                                                                   scalar2=None, op0=ALU.mult), reads=[b_hf[0], b_fl], writes=[b_carry])
                    if OPTS.get("a2_stage", 9) < 4:
                        continue
                    for s in reversed(range(nseg)):
                        gates_and_elem(1, s, c)
                        init = 0.0 if s == nseg - 1 else carry[:, 1:2]
                        em.op("dve", lambda h, init=init: h.tensor_tensor_scan(out=hb[:, ::-1], data0=ra[:, ::-1], data1=iu[:, ::-1],
                                                                             initial=init, op0=ALU.mult, op1=ALU.add),
                              reads=[b_ra, b_iu, b_carry], writes=[b_hb])
                        if joint and s == 1:
                            em.op("dve", lambda h: h.tensor_scalar(out=carry[:, 1:2], in0=hb[:, 0:1], scalar1=fl[:, 0:1],
                                                                   scalar2=None, op0=ALU.mult), reads=[b_hb, b_fl], writes=[b_carry])
                        em.op("dve", lambda h, s=s: h.tensor_tensor(out=hb[:, :], in0=hb[:, :], in1=hf[s][:, :], op=ALU.add),
                              reads=[b_hb, b_hf[s]], writes=[b_hb])
                        em.op("dve", lambda h, s=s: h.tensor_tensor(out=outb[:, s * 2048:(s + 1) * 2048], in0=hb[:, :],
                                                                   in1=gg[:, s * 2048:(s + 1) * 2048], op=ALU.mult),
                              reads=[b_hb, b_gg], writes=[b_outb])
                    em.dma(art_d[c0:c0 + nch, :, :].rearrange("c p (f t) -> p c f t", t=512)[:, :, 4 + c, :],
                           outb[:, 0:nch * 512].rearrange("p (c t) -> p c t", t=512), reads=[b_outb], writes=[b_art], sembuf=b_outb)
            if debug and "art" in dbg:
                em.dma(dbg["art"][:, :, :], art_d[:, :, :], reads=[b_art], sembuf=b_art)
            em.flush(final_waits=[b_art])

        with contextlib.ExitStack() as st:
            gfin = sb(st, "gfin", [128, D], F32)
            b_gfin = em.buf("gfin")
            em.dma(gfin[:], gains_d[2].partition_broadcast(128), writes=[b_gfin])
            gmT = sb(st, "gmT", [128, 16], F32)
            b_gmT = em.buf("gmT")
            em.dma(gmT[:], gT_d[:, :], writes=[b_gmT])
            NS = 4
            ring = [sb(st, f"ring{i}", [128, 8192], BF16) for i in range(NS)]
            b_ring = [em.buf(f"ring{i}") for i in range(NS)]
            AU = [sb(st, f"AU{i}", [128, 8192], BF16) for i in range(2)]
            b_AU = [em.buf(f"AU{i}") for i in range(2)]
            arT = sb(st, "arT", [128, 8192], BF16)
            b_arT = em.buf("arT")
            MH = sb(st, "MH", [128, 8192], BF16)
            b_MH = em.buf("MH")
            x1 = sb(st, "x1", [128, 4 * D], F32)
            b_x1 = [em.buf(f"x1_{i}") for i in range(4)]
            u2 = [sb(st, f"u2_{i}", [128, D], BF16) for i in range(2)]
            b_u2 = [em.buf(f"u2_{i}") for i in range(2)]
            sg = [sb(st, f"sg{i}", [128, 512], BF16) for i in range(4)]
            b_sg = [em.buf(f"sg{i}") for i in range(4)]
            tm = [sb(st, f"tm{i}", [128, 512], F32) for i in range(4)]
            b_tm = [em.buf(f"tm{i}") for i in range(4)]
            stt2 = sb(st, "stt2", [128, 8], F32)
            b_stt2 = [em.buf(f"stt2_{i}") for i in range(4)]
            PB = [ps(st, f"PB{i}", [128, 512], F32) for i in range(8)]
            b_PB = [em.buf(f"ps_PB{i}") for i in range(8)]
            b_y = em.buf("y")

            seq = []
            for ci in range(NCH):
                for jb in range(4):
                    seq.append([(0, 8192, wG_d[jb, :, :])])
                    seq.append([(0, 8192, wG_d[4 + jb, :, :])])
                    seq.append([(0, 6144, wRO_d[jb, :, :]), (6144, 8192, wAO_d[0, :, jb * 2048:(jb + 1) * 2048])])
                for fb in range(4):
                    seq.append([(0, 8192, wO_d[fb, :, :])])
                for qd in range(4):
                    for jj in range(4):
                        seq.append([(0, 8192, w1_s[qd * 4 + jj, :, :])])
                    for fb in range(4):
                        seq.append([(0, 8192, w2_s[fb * 4 + qd, :, :])])
            if OPTS.get("b_chunks") is not None:
                seq = seq[:48 * OPTS["b_chunks"]]
            ss_ = {"issued": 0, "next": 0}

            def slab_issue_upto(n):
                while ss_["issued"] < min(n, len(seq)):
                    k = ss_["issued"]
                    for lo, hi, src in seq[k]:
                        em.dma(ring[k % NS][:, lo:hi], src, reads=[b_wd], writes=[b_ring[k % NS]])
                    ss_["issued"] += 1

            def slab_next():
                k = ss_["next"]
                ss_["next"] += 1
                slab_issue_upto(k + 1)
                return ring[k % NS], b_ring[k % NS]

            def slab_prefetch():
                slab_issue_upto(ss_["next"] + NS)

            def rms_stats(src_ap, k, rb):
                ssa, rsa, bS = stt2[:, 2 * k:2 * k + 1], stt2[:, 2 * k + 1:2 * k + 2], b_stt2[k]
                jk = u2[k % 2]
                em.op("act", lambda h: h.activation(out=jk[:, :], in_=src_ap, func=AF.Square, accum_out=ssa),
                      reads=[rb], writes=[b_u2[k % 2], bS])
                em.op("dve", lambda h: h.tensor_scalar(out=rsa, in0=ssa, scalar1=1.0 / D, scalar2=EPS, op0=ALU.mult, op1=ALU.add),
                      reads=[bS], writes=[bS])
                em.op("act", lambda h: h.activation(out=rsa, in_=rsa, func=AF.Sqrt), reads=[bS], writes=[bS])
                em.op("dve", lambda h: h.reciprocal(out=rsa, in_=rsa), reads=[bS], writes=[bS])
                return rsa, bS

            nchunks = OPTS.get("b_chunks", NCH)
            em.dma(AU[0][:, :], uT_d[0, :, :], reads=[b_uTd], writes=[b_AU[0]])
            em.dma(arT[:, :], art_d[0, :, :], reads=[b_art], writes=[b_arT])
            for ci in range(nchunks):
                UT, bUT = AU[ci % 2], b_AU[ci % 2]
                em.dma(x1[:, :].rearrange("p (t d) -> p t d", d=D), x_d[ci * 512:(ci + 1) * 512, :].rearrange("(t p) d -> p t d", p=128),
                       writes=b_x1)
                for jb in range(4):
                    GA, bGA = slab_next()
                    GR, bGR = slab_next()
                    RO, bRO = slab_next()
                    for cb in range(4):
                        j = jb * 4 + cb
                        pb = 4 * (j % 2)
                        em.mm_group([lambda h, kc=kc, pb=pb, cb=cb, GA=GA, UT=UT: h.matmul(
                            out=PB[pb][:, :], lhsT=GA[:, cb * 2048 + kc * 128: cb * 2048 + (kc + 1) * 128],
                            rhs=UT[:, kc * 512:(kc + 1) * 512], start=(kc == 0), stop=(kc == 15)) for kc in range(16)],
                            reads=[bGA, bUT], writes=[b_PB[pb]])
                        em.mm_group([lambda h, kc=kc, pb=pb, cb=cb, GR=GR, UT=UT: h.matmul(
                            out=PB[pb + 1][:, :], lhsT=GR[:, cb * 2048 + kc * 128: cb * 2048 + (kc + 1) * 128],
                            rhs=UT[:, kc * 512:(kc + 1) * 512], start=(kc == 0), stop=(kc == 15)) for kc in range(16)],
                            reads=[bGR, bUT], writes=[b_PB[pb + 1]])
                        em.mm_group([lambda h, kc=kc, pb=pb, cb=cb, RO=RO: h.matmul(
                            out=PB[pb + 2][:, :], lhsT=RO[:, 6144 + cb * 512 + kc * 128: 6144 + cb * 512 + (kc + 1) * 128],
                            rhs=arT[:, kc * 512:(kc + 1) * 512], start=(kc == 0), stop=(kc == 3)) for kc in range(4)],
                            reads=[bRO, b_arT], writes=[b_PB[pb + 2]])
                        em.mm_group([lambda h, kc=kc, pb=pb, cb=cb, RO=RO: h.matmul(
                            out=PB[pb + 3][:, :], lhsT=RO[:, cb * 1536 + kc * 128: cb * 1536 + (kc + 1) * 128],
                            rhs=arT[:, (4 + kc) * 512:(5 + kc) * 512], start=(kc == 0), stop=(kc == 11)) for kc in range(12)],
                            reads=[bRO, b_arT], writes=[b_PB[pb + 3]])
                        q2 = 2 * (j % 2)
                        em.op("act", lambda h, pb=pb, q2=q2: h.activation(out=sg[q2][:, :], in_=PB[pb][:, :], func=AF.Sigmoid),
                              reads=[b_PB[pb]], writes=[b_sg[q2]])
                        em.op("act", lambda h, pb=pb, q2=q2: h.activation(out=sg[q2 + 1][:, :], in_=PB[pb + 1][:, :], func=AF.Sigmoid),
                              reads=[b_PB[pb + 1]], writes=[b_sg[q2 + 1]])
                        em.op("dve", lambda h, pb=pb, q2=q2: h.tensor_tensor(out=tm[q2][:, :], in0=PB[pb + 2][:, :], in1=sg[q2][:, :],
                                                                            op=ALU.mult), reads=[b_PB[pb + 2], b_sg[q2]], writes=[b_tm[q2]])
                        em.op("dve", lambda h, pb=pb, q2=q2: h.tensor_tensor(out=tm[q2 + 1][:, :], in0=PB[pb + 3][:, :], in1=sg[q2 + 1][:, :],
                                                                            op=ALU.mult), reads=[b_PB[pb + 3], b_sg[q2 + 1]], writes=[b_tm[q2 + 1]])
                        em.op("dve", lambda h, j=j, q2=q2: h.tensor_tensor(out=MH[:, j * 512:(j + 1) * 512], in0=tm[q2][:, :], in1=tm[q2 + 1][:, :],
                                                                          op=ALU.add), reads=[b_tm[q2], b_tm[q2 + 1]], writes=[b_MH])
                    slab_prefetch()
                if "mT" in dbg and ci == 0:
                    em.dma(dbg["mT"][:, :], MH[:, :], reads=[b_MH], writes=[b_y], sembuf=b_MH)
                for fb in range(4):
                    WO, bWO = slab_next()
                    pb = 4 * (fb % 2)
                    for kc in range(16):
                        em.mm_group([lambda h, kc=kc, tt=tt, pb=pb, WO=WO: h.matmul(
                            out=PB[pb + tt][:, :], lhsT=MH[:, kc * 512 + tt * 128: kc * 512 + (tt + 1) * 128],
                            rhs=WO[:, kc * 512:(kc + 1) * 512], start=(kc == 0), stop=(kc == 15)) for tt in range(4)],
                            reads=[bWO, b_MH], writes=[b_PB[pb + tt] for tt in range(4)])
                    for tt in range(4):
                        dst = x1[:, tt * D + fb * 512: tt * D + (fb + 1) * 512]
                        em.op("dve", lambda h, dst=dst, pb=pb, tt=tt: h.tensor_tensor(out=dst, in0=PB[pb + tt][:, :], in1=dst, op=ALU.add),
                              reads=[b_PB[pb + tt], b_x1[tt]], writes=[b_x1[tt]])
                    slab_prefetch()
                if "x1" in dbg:
                    em.dma(dbg["x1"][ci * 512:(ci + 1) * 512, :].rearrange("(t p) d -> p t d", p=128),
                           x1[:, :].rearrange("p (t d) -> p t d", d=D), reads=b_x1, writes=[b_y], sembuf=b_x1[0])
                if ci + 1 < nchunks:
                    em.dma(AU[(ci + 1) % 2][:, :], uT_d[ci + 1, :, :], reads=[b_uTd], writes=[b_AU[(ci + 1) % 2]])
                    em.dma(arT[:, :], art_d[ci + 1, :, :], reads=[b_art], writes=[b_arT])
                for tt in range(4):
                    rsa, bS = rms_stats(x1[:, tt * D:(tt + 1) * D], tt, b_x1[tt])
                    U2, bU2 = u2[tt % 2], b_u2[tt % 2]
                    em.op("dve", lambda h, U2=U2, tt=tt, rsa=rsa: h.tensor_scalar(out=U2[:, :], in0=x1[:, tt * D:(tt + 1) * D], scalar1=rsa,
                                                                                scalar2=None, op0=ALU.mult), reads=[b_x1[tt], bS], writes=[bU2])
                    pbk = 2 * (tt % 2)
                    for half in range(2):
                        bank = PB[pbk + half][:, :].bitcast(BF16)
                        em.mm_group([lambda h, kc=kc, bank=bank, U2=U2, half=half: h.transpose(
                            out=bank[:, kc * 128:(kc + 1) * 128], in_=U2[:, (half * 8 + kc) * 128:(half * 8 + kc + 1) * 128], identity=ident[:])
                            for kc in range(8)], reads=[bU2, b_ident], writes=[b_PB[pbk + half]])
                        em.op("dve", lambda h, bank=bank, half=half, tt=tt, UT=UT: h.tensor_tensor(
                            out=UT[:, half * 4096:(half + 1) * 4096].rearrange("p (kc t) -> p kc t", t=512)[:, :, tt * 128:(tt + 1) * 128],
                            in0=bank.rearrange("p (kc t) -> p kc t", t=128),
                            in1=gmT[:, half * 8:(half + 1) * 8].unsqueeze(2).broadcast_to([128, 8, 128]), op=ALU.mult),
                            reads=[b_PB[pbk + half], b_gmT], writes=[bUT])
                for qd in range(4):
                    for jj in range(4):
                        W1, bW1 = slab_next()
                        for cb in range(4):
                            hk = jj * 4 + cb
                            pb = hk % 2
                            em.mm_group([lambda h, kc=kc, pb=pb, cb=cb, W1=W1, UT=UT: h.matmul(
                                out=PB[pb][:, :], lhsT=W1[:, cb * 2048 + kc * 128: cb * 2048 + (kc + 1) * 128],
                                rhs=UT[:, kc * 512:(kc + 1) * 512], start=(kc == 0), stop=(kc == 15)) for kc in range(16)],
                                reads=[bW1, bUT], writes=[b_PB[pb]])
                            q2 = hk % 4
                            em.op("act", lambda h, pb=pb, q2=q2: h.activation(out=tm[q2][:, :], in_=PB[pb][:, :], func=AF.Relu),
                                  reads=[b_PB[pb]], writes=[b_tm[q2]])
                            em.op("dve", lambda h, hk=hk, q2=q2: h.tensor_tensor(out=MH[:, hk * 512:(hk + 1) * 512], in0=tm[q2][:, :], in1=tm[q2][:, :],
                                                                                op=ALU.mult), reads=[b_tm[q2]], writes=[b_MH])
                        slab_prefetch()
                    for fb in range(4):
                        W2, bW2 = slab_next()
                        for kc in range(16):
                            em.mm_group([lambda h, kc=kc, tt=tt, W2=W2: h.matmul(
                                out=PB[4 + tt][:, :], lhsT=MH[:, kc * 512 + tt * 128: kc * 512 + (tt + 1) * 128],
                                rhs=W2[:, kc * 512:(kc + 1) * 512], start=(kc == 0), stop=(kc == 15)) for tt in range(4)],
                                reads=[bW2, b_MH], writes=[b_PB[4 + tt] for tt in range(4)])
                        for tt in range(4):
                            dst = x1[:, tt * D + fb * 512: tt * D + (fb + 1) * 512]
                            em.op("dve", lambda h, dst=dst, tt=tt: h.tensor_tensor(out=dst, in0=PB[4 + tt][:, :], in1=dst, op=ALU.add),
                                  reads=[b_PB[4 + tt], b_x1[tt]], writes=[b_x1[tt]])
                        slab_prefetch()
                for tt in range(4):
                    rsa, bS = rms_stats(x1[:, tt * D:(tt + 1) * D], tt, b_x1[tt])
                    em.op("dve", lambda h, tt=tt, rsa=rsa: h.scalar_tensor_tensor(out=x1[:, tt * D:(tt + 1) * D], in0=x1[:, tt * D:(tt + 1) * D],
                                                                                scalar=rsa, in1=gfin[:, :], op0=ALU.mult, op1=ALU.mult),
                          reads=[b_x1[tt], bS, b_gfin], writes=[b_x1[tt]])
                em.dma(y_d[ci * 512:(ci + 1) * 512, :].rearrange("(t p) d -> p t d", p=128), x1[:, :].rearrange("p (t d) -> p t d", d=D),
                       reads=b_x1, writes=[b_y], sembuf=b_x1[0])
            em.flush(final_waits=[b_y])
    return nc


def _t5_bucket_np(rel):
    import math
    rel = np.asarray(rel, np.int64)
    half, max_exact = 16, 8
    n = np.abs(rel)
    nf = np.maximum(n, 1).astype(np.float64)
    large = max_exact + (np.log(nf / max_exact) / math.log(1024 / max_exact) * (half - max_exact)).astype(np.int64)
    large = np.minimum(large, half - 1)
    return np.where(rel > 0, half, 0) + np.where(n < max_exact, n, large)


def _make_bt(rel_bias):
    k = np.arange(128)[:, None]
    j = np.arange(256)[None, :]
    rel = k - j + 64
    valid = np.abs(rel) <= 64
    bt = np.full((128, 12, 256), NEG, np.float32)
    for g, d in enumerate((1, 4, 16)):
        bk = _t5_bucket_np(np.clip(rel, -64, 64) * d)
        for h in range(4):
            head = g * 4 + h
            bt[:, head, :] = np.where(valid, rel_bias[bk, head], np.float32(NEG))
    return np.ascontiguousarray(bt.reshape(128, 12 * 256))


def _make_pc(inputs):
    f = lambda a: np.asarray(a, dtype=np.float32)
    pc = np.zeros((128, 136), np.float32)
    cw = f(inputs["conv_w"])[0]
    for t in range(4):
        pc[:, t * 12:(t + 1) * 12] = cw[t].reshape(12, 128).T
    pc[:, 48:60] = f(inputs["conv_b"])[0].reshape(12, 128).T
    for i, nm in enumerate(["lru_ba", "lru_bx", "lru_lambda"]):
        a = f(inputs[nm])[0]
        for dr in range(2):
            pc[:, 60 + i * 24 + dr * 12: 60 + i * 24 + (dr + 1) * 12] = a[dr].reshape(12, 128).T
    return pc


_NC_CACHE = {}


def kernel(**inputs):
    import ml_dtypes
    f = lambda a: np.ascontiguousarray(np.asarray(a, dtype=np.float32))
    xp = f(inputs["x_prompt"])
    xs = f(inputs["x_sample"])
    common = {
        "w_in": f(inputs["w_in"])[0],
        "w_attn_o": f(inputs["w_attn_o"])[0],
        "w_rnn_o": f(inputs["w_rnn_o"])[0],
        "w_out": f(inputs["w_out"])[0],
        "w_mlp_in": f(inputs["w_mlp_in"])[0],
        "w_mlp_out": f(inputs["w_mlp_out"])[0],
        "lru_wa": f(inputs["lru_wa"])[0].reshape(24, 128, 128),
        "lru_wx": f(inputs["lru_wx"])[0].reshape(24, 128, 128),
        "gains": np.ascontiguousarray(np.stack([f(inputs["norm_mix_g"])[0], f(inputs["norm_mlp_g"])[0], f(inputs["norm_final_g"])])),
        "gT": np.ascontiguousarray(f(inputs["norm_mlp_g"])[0].reshape(16, 128).T),
        "pc": _make_pc(inputs),
        "bt": _make_bt(f(inputs["rel_bias"])),
        "ident": np.eye(128, dtype=np.float32).astype(ml_dtypes.bfloat16),
    }
    in_maps = []
    for c in range(8):
        m = dict(common)
        if c < 4:
            m["x"] = np.ascontiguousarray(np.concatenate([xs[c], xp[c]], axis=0))
            m["flag"] = np.ones((128, 1), np.float32)
        else:
            b0 = 4 + 3 * (c - 4)
            m["x"] = np.ascontiguousarray(xp[b0:b0 + 3].reshape(T, D))
            m["flag"] = np.zeros((128, 1), np.float32)
        in_maps.append(m)
    if "nc" not in _NC_CACHE:
        _NC_CACHE["nc"] = build()
    res = run_bass_kernel_spmd(_NC_CACHE["nc"], in_maps, core_ids=list(range(8)))
    y_prompt = np.empty((16, 2048, D), np.float32)
    y_sample = np.empty((4, 4096, D), np.float32)
    for c in range(8):
        y = res.results[c]["y"]
        if c < 4:
            y_sample[c] = y[:4096]
            y_prompt[c] = y[4096:]
        else:
            b0 = 4 + 3 * (c - 4)
            y_prompt[b0:b0 + 3] = y.reshape(3, 2048, D)
    return (y_prompt, y_sample)
```

```python
import contextlib
import numpy as np
import concourse.bass as bass
import concourse.mybir as mybir
from concourse.bass_utils import run_bass_kernel_spmd

F32 = mybir.dt.float32
BF16 = mybir.dt.bfloat16
AF = mybir.ActivationFunctionType
ALU = mybir.AluOpType

D = 2048
T = 6144
NCH = 12
CH = 512
D_IN = 11776
DFF = 8192
NEG = -30000.0
EPS = 1e-6
SAME_ENGINE_WAITS = True
STQ = "pool"
OPTS = {}


class Buf:
    __slots__ = ("name", "w", "r", "dsem", "dcount", "excl")

    def __init__(self, name):
        self.name = name
        self.w = {}
        self.r = {}
        self.dsem = None
        self.dcount = 0
        self.excl = name.startswith("ps_")


class Eng:
    def __init__(self, name, sem):
        self.name = name
        self.sem = sem
        self.count = 0
        self.waited = {}
        self.prog = []


class Emitter:
    def __init__(self, nc, stack):
        self.nc = nc
        self.stack = stack
        self.engs = {}
        for n in ("pe", "act", "dve", "pool", "sp"):
            self.engs[n] = Eng(n, stack.enter_context(nc.semaphore("sem_" + n)))
        self.nbuf = 0

    def buf(self, name=None):
        self.nbuf += 1
        return Buf(name or f"b{self.nbuf}")

    def _dsem(self, b):
        if b.dsem is None:
            b.dsem = self.stack.enter_context(self.nc.semaphore("ds_" + b.name))
        return b.dsem

    def _deps(self, eng, reads, writes):
        deps = {}

        def add(tok):
            if tok is None:
                return
            s, v = tok
            if deps.get(s, 0) < v:
                deps[s] = v

        for b in reads:
            for s, v in b.w.items():
                add((s, v))
            if b.excl:
                for s, v in b.r.items():
                    if s is not eng.sem:
                        add((s, v))
        for b in writes:
            for s, v in b.w.items():
                add((s, v))
            for s, v in b.r.items():
                add((s, v))
        for s, v in deps.items():
            if s is eng.sem and (eng.name == "pe" or not SAME_ENGINE_WAITS):
                continue
            if eng.waited.get(s, 0) < v:
                eng.waited[s] = v
                eng.prog.append(lambda h, s=s, v=v: h.wait_ge(s, v))

    def _mark(self, tok, reads, writes):
        s, v = tok
        for b in reads:
            if b.r.get(s, 0) < v:
                b.r[s] = v
        for b in writes:
            if b.w.get(s, 0) < v:
                b.w[s] = v
            b.r = {}

    def op(self, engname, fn, reads=(), writes=()):
        eng = self.engs[engname]
        self._deps(eng, reads, writes)
        eng.count += 1
        cnt = eng.count
        sem = eng.sem
        eng.prog.append(lambda h, fn=fn, sem=sem: fn(h).then_inc(sem, 1))
        self._mark((sem, cnt), reads, writes)

    def mm_group(self, fns, reads=(), writes=()):
        eng = self.engs["pe"]
        self._deps(eng, reads, writes)
        eng.count += 1
        cnt = eng.count
        sem = eng.sem
        for f in fns[:-1]:
            eng.prog.append(lambda h, f=f: f(h))
        eng.prog.append(lambda h, f=fns[-1], sem=sem: f(h).then_inc(sem, 1))
        self._mark((sem, cnt), reads, writes)

    def dma(self, out, in_, reads=(), writes=(), sembuf=None, q="sp", **kw):
        eng = self.engs[q]
        self._deps(eng, reads, writes)
        b = sembuf or (writes[0] if writes else reads[0])
        s = self._dsem(b)
        b.dcount += 16
        v = b.dcount
        eng.prog.append(lambda h, out=out, in_=in_, s=s, kw=kw: h.dma_start(out=out, in_=in_, **kw).then_inc(s, 16))
        self._mark((s, v), reads, writes)

    def flush(self, final_waits=()):
        nc = self.nc
        progs = {n: e.prog for n, e in self.engs.items()}
        for e in self.engs.values():
            e.prog = []
        with nc.Block() as block:
            @block.tensor
            def _(h):
                for t in progs["pe"]:
                    t(h)

            @block.scalar
            def _(h):
                for t in progs["act"]:
                    t(h)

            @block.vector
            def _(h):
                for t in progs["dve"]:
                    t(h)

            @block.gpsimd
            def _(h):
                for t in progs["pool"]:
                    t(h)

            @block.sync
            def _(h):
                for t in progs["sp"]:
                    t(h)
                for b in final_waits:
                    for s, v in list(b.w.items()) + list(b.r.items()):
                        h.wait_ge(s, v)


def r3(ap, pat, **kw):
    return ap.rearrange(pat, **kw)


def build(debug=None):
    nc = bass.Bass("TRN2", target_bir_lowering=False)
    dt = nc.dram_tensor
    x_d = dt("x", [T, D], F32, kind="ExternalInput").ap()
    w_in_d = dt("w_in", [D, D_IN], F32, kind="ExternalInput").ap()
    w_ao_d = dt("w_attn_o", [512, D], F32, kind="ExternalInput").ap()
    w_ro_d = dt("w_rnn_o", [1536, D], F32, kind="ExternalInput").ap()
    w_out_d = dt("w_out", [D, D], F32, kind="ExternalInput").ap()
    w1_d = dt("w_mlp_in", [D, DFF], F32, kind="ExternalInput").ap()
    w2_d = dt("w_mlp_out", [DFF, D], F32, kind="ExternalInput").ap()
    lwa_d = dt("lru_wa", [24, 128, 128], F32, kind="ExternalInput").ap()
    lwx_d = dt("lru_wx", [24, 128, 128], F32, kind="ExternalInput").ap()
    gains_d = dt("gains", [3, D], F32, kind="ExternalInput").ap()
    pc_d = dt("pc", [128, 136], F32, kind="ExternalInput").ap()
    bt_d = dt("bt", [128, 12 * 256], F32, kind="ExternalInput").ap()
    ident_d = dt("ident", [128, 128], BF16, kind="ExternalInput").ap()
    flag_d = dt("flag", [128, 1], F32, kind="ExternalInput").ap()
    gT_d = dt("gT", [128, 16], F32, kind="ExternalInput").ap()
    y_d = dt("y", [T, D], F32, kind="ExternalOutput").ap()

    uT_d = dt("uT_s", [NCH, 128, 16 * CH], BF16, kind="Internal").ap()
    art_d = dt("art_s", [NCH, 128, 16 * CH], BF16, kind="Internal").ap()
    wA_d = dt("wA_s", [12, 128, 6144], BF16, kind="Internal").ap()
    wL_d = dt("wL_s", [12, 128, 4096], BF16, kind="Internal").ap()
    wG_d = dt("wG_s", [8, 128, 8192], BF16, kind="Internal").ap()
    wAO_d = dt("wAO_s", [1, 128, 8192], BF16, kind="Internal").ap()
    wRO_d = dt("wRO_s", [4, 128, 6144], BF16, kind="Internal").ap()
    wO_d = dt("wO_s", [4, 128, 8192], BF16, kind="Internal").ap()
    w1_s = dt("w1_s", [16, 128, 8192], BF16, kind="Internal").ap()
    w2_s = dt("w2_s", [16, 128, 8192], BF16, kind="Internal").ap()

    dbg = {}
    if debug:
        for name, shape, dty in debug:
            dbg[name] = dt("dbg_" + name, shape, dty, kind="ExternalOutput").ap()

    with contextlib.ExitStack() as gstack:
        em = Emitter(nc, gstack)
        sb = lambda st, name, shape, dty: st.enter_context(nc.sbuf_tensor("s_" + name, shape, dty))
        ps = lambda st, name, shape, dty: st.enter_context(nc.psum_tensor("p_" + name, shape, dty))

        ident = sb(gstack, "ident", [128, 128], BF16)
        pc = sb(gstack, "pc", [128, 136], F32)
        lw = sb(gstack, "lw", [128, 48 * 128], BF16)
        b_ident, b_pc, b_lw = em.buf("ident"), em.buf("pc"), em.buf("lw")
        em.dma(ident[:], ident_d[:, :], writes=[b_ident])
        em.dma(pc[:], pc_d[:, :], writes=[b_pc])
        fl = sb(gstack, "fl", [128, 2], F32)
        b_fl = em.buf("fl")
        em.dma(fl[:, 0:1], flag_d[:, :], writes=[b_fl])
        em.op("dve", lambda h: h.tensor_scalar(out=fl[:, 1:2], in0=fl[:, 0:1], scalar1=-1.0, scalar2=-NEG,
                                               op0=ALU.add, op1=ALU.mult), reads=[b_fl], writes=[b_fl])
        ones = sb(gstack, "ones", [128, 128], BF16)
        b_ones = em.buf("ones")
        em.op("dve", lambda h: h.memset(ones[:], 1.0), writes=[b_ones])

        with contextlib.ExitStack() as st:
            stg = [sb(st, f"stg{i}", [128, 8192], F32) for i in range(2)]
            slb = [sb(st, f"slb{i}", [128, 8192], BF16) for i in range(2)]
            b_stg = [em.buf(f"stg{i}") for i in range(2)]
            b_slb = [em.buf(f"slb{i}") for i in range(2)]
            b_wd = em.buf("wdram")
            state = {"n": 0, "ci": 0}

            def cast(out, in_, rb, wb):
                state["ci"] += 1
                if state["ci"] % 2 == 0:
                    em.op("act", lambda h: h.activation(out=out, in_=in_, func=AF.Copy), reads=[rb], writes=[wb])
                else:
                    em.op("dve", lambda h: h.tensor_copy(out=out, in_=in_), reads=[rb], writes=[wb])

            def prep(loads, casts, nel, dst, sbuf_dst=None):
                i = state["n"] % 2
                state["n"] += 1
                S, L = stg[i], slb[i]
                for vf, src in loads:
                    em.dma(vf(S), src, writes=[b_stg[i]])
                if sbuf_dst is not None:
                    for of, inf in casts:
                        cast(of(sbuf_dst[0]), inf(S), b_stg[i], sbuf_dst[1])
                    return
                for of, inf in casts:
                    cast(of(L), inf(S), b_stg[i], b_slb[i])
                em.dma(dst, L[:, 0:nel], reads=[b_slb[i]], writes=[b_wd], sembuf=b_slb[i], q=STQ)

            def rows(w, nkc):
                return w.rearrange("(kc p) c -> p kc c", p=128)

            win_v = rows(w_in_d, 16)

            def stat_slab(src_v, nkc, c0, dst):
                n = nkc * 512
                loads = [(lambda S: S[:, 0:n].rearrange("p (kc c) -> p kc c", c=512), src_v[:, :, c0:c0 + 512])]
                casts = []
                for cb in range(4):
                    casts.append((
                        lambda L, cb=cb: L[:, cb * nkc * 128:(cb + 1) * nkc * 128].rearrange("p (kc c) -> p kc c", c=128),
                        lambda S, cb=cb: S[:, 0:n].rearrange("p (kc c) -> p kc c", c=512)[:, :, cb * 128:(cb + 1) * 128]))
                prep(loads, casts, n, dst)

            def mov_slab(src_v, c0, dst):
                loads = [(lambda S: S[:, :].rearrange("p (kc c) -> p kc c", c=512), src_v[:, :, c0:c0 + 512])]
                casts = [(lambda L, j=j: L[:, j * 2048:(j + 1) * 2048], lambda S, j=j: S[:, j * 2048:(j + 1) * 2048])
                         for j in range(4)]
                prep(loads, casts, 8192, dst)

            def cols_slab(col_offs, dst):
                nb = len(col_offs)
                loads = [(lambda S, j=j: S[:, j * 2048:(j + 1) * 2048].rearrange("p (kc c) -> p kc c", c=128),
                          win_v[:, :, c0:c0 + 128]) for j, c0 in enumerate(col_offs)]
                casts = [(lambda L, j=j: L[:, j * 2048:(j + 1) * 2048], lambda S, j=j: S[:, j * 2048:(j + 1) * 2048])
                         for j in range(nb)]
                prep(loads, casts, nb * 2048, dst)

            loads = [(lambda S: S[:, 0:3072].rearrange("p (n o) -> p n o", o=128), lwa_d.rearrange("n i o -> i n o")),
                     (lambda S: S[:, 3072:6144].rearrange("p (n o) -> p n o", o=128), lwx_d.rearrange("n i o -> i n o"))]
            casts = [(lambda L, j=j: L[:, j * 3072:(j + 1) * 3072], lambda S, j=j: S[:, j * 3072:(j + 1) * 3072])
                     for j in range(2)]
            prep(loads, casts, 6144, None, sbuf_dst=(lw, b_lw))

            for h in range(4):
                for g in range(3):
                    head = g * 4 + h
                    cols_slab([head * 128, 1536 + head * 128, 3072 + head * 128], wA_d[h * 3 + g, :, :])
            for c in range(12):
                cols_slab([4608 + c * 128, 6144 + c * 128], wL_d[c, :, :])
            for jb in range(4):
                stat_slab(win_v, 16, 7680 + jb * 512, wG_d[jb, :, :])
            for jb in range(4):
                stat_slab(win_v, 16, 9728 + jb * 512, wG_d[4 + jb, :, :])
            ao_v = rows(w_ao_d, 4)
            loads = [(lambda S: S[:, :].rearrange("p (kc c) -> p kc c", c=2048), ao_v)]
            casts = []
            for q in range(4):
                casts.append((
                    lambda L, q=q: L[:, q * 2048:(q + 1) * 2048].rearrange("p (cb kc c) -> p cb kc c", kc=4, c=128),
                    lambda S, q=q: S[:, :].rearrange("p (kc c) -> p kc c", c=2048)[:, :, q * 512:(q + 1) * 512]
                    .rearrange("p kc (cb c) -> p cb kc c", c=128)))
            prep(loads, casts, 8192, wAO_d[0, :, :])
            ro_v = rows(w_ro_d, 12)
            for jb in range(4):
                stat_slab(ro_v, 12, jb * 512, wRO_d[jb, :, :])
            wo_v = rows(w_out_d, 16)
            for fb in range(4):
                mov_slab(wo_v, fb * 512, wO_d[fb, :, :])
            w1_v = rows(w1_d, 16)
            for j in range(16):
                stat_slab(w1_v, 16, j * 512, w1_s[j, :, :])
            w2_v = w2_d.rearrange("(kg kc p) c -> kg p kc c", p=128, kc=16)
            for fb in range(4):
                for kg in range(4):
                    mov_slab(w2_v[kg], fb * 512, w2_s[fb * 4 + kg, :, :])

            if debug and "wA" in dbg:
                em.dma(dbg["wA"][:, :], wA_d[5, :, :], reads=[b_wd], sembuf=b_wd)
                em.dma(dbg["w1"][:, :], w1_s[3, :, :], reads=[b_wd], sembuf=b_wd)
                em.dma(dbg["wAO"][:, :], wAO_d[0, :, :], reads=[b_wd], sembuf=b_wd)

            gbc = sb(st, "gbc", [128, D], F32)
            b_gbc = em.buf("gbc")
            em.dma(gbc[:], gains_d[0].partition_broadcast(128), writes=[b_gbc])
            xt = [sb(st, f"xt{i}", [128, D], F32) for i in range(3)]
            b_xt = [em.buf(f"xt{i}") for i in range(3)]
            junk = sb(st, "junk", [128, D], BF16)
            b_junk = em.buf("junk")
            stt = sb(st, "stt", [128, 8], F32)
            b_stt = [em.buf(f"stt{i}") for i in range(4)]
            ub = [sb(st, f"ub{i}", [128, D], BF16) for i in range(2)]
            b_ub = [em.buf(f"ub{i}") for i in range(2)]
            tp = [ps(st, f"tp{i}", [128, D], BF16) for i in range(2)]
            b_tp = [em.buf(f"ps_tp{i}") for i in range(2)]
            uTc = [sb(st, f"uTc{i}", [128, 16 * CH], BF16) for i in range(2)]
            b_uTc = [em.buf(f"uTc{i}") for i in range(2)]
            b_uTd = em.buf("uTd")
            for i in range(T // 128):
                c, j = divmod(i, 4)
                X, bX = xt[i % 3], b_xt[i % 3]
                k = i % 4
                ssa, rsa, bS = stt[:, 2 * k:2 * k + 1], stt[:, 2 * k + 1:2 * k + 2], b_stt[k]
                U, bU = ub[i % 2], b_ub[i % 2]
                P, bP = tp[i % 2], b_tp[i % 2]
                UT, bUT = uTc[c % 2], b_uTc[c % 2]
                em.dma(X[:], x_d[i * 128:(i + 1) * 128, :], writes=[bX])
                em.op("act", lambda h, X=X, ssa=ssa: h.activation(out=junk[:], in_=X[:], func=AF.Square, accum_out=ssa),
                      reads=[bX], writes=[b_junk, bS])
                em.op("dve", lambda h, ssa=ssa, rsa=rsa: h.tensor_scalar(out=rsa, in0=ssa, scalar1=1.0 / D, scalar2=EPS,
                                                                       op0=ALU.mult, op1=ALU.add), reads=[bS], writes=[bS])
                em.op("act", lambda h, rsa=rsa: h.activation(out=rsa, in_=rsa, func=AF.Sqrt), reads=[bS], writes=[bS])
                em.op("dve", lambda h, rsa=rsa: h.reciprocal(out=rsa, in_=rsa), reads=[bS], writes=[bS])
                em.op("dve", lambda h, X=X, U=U, rsa=rsa: h.scalar_tensor_tensor(out=U[:], in0=X[:], scalar=rsa, in1=gbc[:],
                                                                                op0=ALU.mult, op1=ALU.mult),
                      reads=[bX, bS, b_gbc], writes=[bU])
                em.mm_group([lambda h, kc=kc, P=P, U=U: h.transpose(out=P[:, kc * 128:(kc + 1) * 128],
                                                                   in_=U[:, kc * 128:(kc + 1) * 128], identity=ident[:])
                             for kc in range(16)], reads=[bU, b_ident], writes=[bP])
                em.op("act", lambda h, P=P, UT=UT, j=j: h.activation(
                    out=UT[:, :].rearrange("p (kc t) -> p kc t", t=CH)[:, :, j * 128:(j + 1) * 128],
                    in_=P[:, :].rearrange("p (kc t) -> p kc t", t=128), func=AF.Copy), reads=[bP], writes=[bUT])
                if j == 3:
                    em.dma(uT_d[c, :, :], UT[:, :], reads=[bUT], writes=[b_uTd], sembuf=bUT, q=STQ)
            if debug and "uT" in dbg:
                em.dma(dbg["uT"][:, :, :], uT_d[:, :, :], reads=[b_uTd], sembuf=b_uTd)
            em.flush(final_waits=[b_wd, b_uTd])

        UNITS = [(0, 8, True), (8, 4, False)]
        DILS = (1, 4, 16)
        SCALE = 128.0 ** -0.5
        alt = {"n": 0}

        def evac_eng():
            alt["n"] += 1
            return "act" if alt["n"] % 2 == 0 else "dve"

        def copy_op(eng, out, in_):
            if eng == "act":
                return lambda h: h.activation(out=out, in_=in_, func=AF.Copy)
            return lambda h: h.tensor_copy(out=out, in_=in_)

        with contextlib.ExitStack() as st:
            BT = sb(st, "BT", [128, 12 * 256], F32)
            b_BT = em.buf("BT")
            em.dma(BT[:, :], bt_d[:, :], writes=[b_BT])
            wsl = [sb(st, f"wsl{i}", [128, 6144], BF16) for i in range(2)]
            b_wsl = [em.buf(f"wsl{i}") for i in range(2)]
            uTs = [sb(st, f"uTs{i}", [128, 8192], BF16) for i in range(2)]
            b_uTs = [em.buf(f"uTs{i}") for i in range(2)]
            qkv_sets = [[sb(st, f"qkv{z}_{i}", [128, 4096], BF16) for i in range(3)] for z in range(2)]
            b_qkv_sets = [[em.buf(f"qkv{z}_{i}") for i in range(3)] for z in range(2)]
            vtok_sets = [sb(st, f"vtok{z}", [128, 4096], BF16) for z in range(2)]
            b_vtok_sets = [em.buf(f"vtok{z}") for z in range(2)]
            acc = [sb(st, f"acc{i}", [128, 4096], F32) for i in range(2)]
            b_acc = [em.buf(f"acc{i}") for i in range(2)]
            PT = [sb(st, f"PT{i}", [128, 512], BF16) for i in range(4)]
            b_PT = [em.buf(f"PT{i}") for i in range(4)]
            tmpS = [sb(st, f"tmpS{i}", [128, 512], F32) for i in range(4)]
            b_tmpS = [em.buf(f"tmpS{i}") for i in range(4)]
            attnT = sb(st, "attnT", [128, 4096], BF16)
            b_attnT = em.buf("attnT")
            Wb = [ps(st, f"Wb{i}", [128, 512], F32) for i in range(4)]
            b_Wb = [em.buf(f"ps_Wb{i}") for i in range(4)]
            Ob = [ps(st, f"Ob{i}", [128, 512], F32) for i in range(2)]
            b_Ob = [em.buf(f"ps_Ob{i}") for i in range(2)]
            Lb = [ps(st, f"Lb{i}", [128, 512], F32) for i in range(2)]
            b_Lb = [em.buf(f"ps_Lb{i}") for i in range(2)]
            VTb = Wb[1][:, :].bitcast(BF16)
            b_VTb = b_Wb[1]
            b_art = em.buf("art")
            cnt = {"w": 0, "u": 0, "W": 0, "pt": 0, "ts": 0, "reg": 0, "ip": 0, "zq": 0}

            def a1_ctx(c0, nch, joint, hs, g, zq):
                d = DILS[g]
                ulen = nch * CH
                head = g * 4 + hs
                Lj = ulen // d
                return dict(c0=c0, nch=nch, joint=joint, hs=hs, g=g, d=d, ulen=ulen, head=head, Lj=Lj,
                            qkv=qkv_sets[zq], b_qkv=b_qkv_sets[zq], vtok=vtok_sets[zq], b_vtok=b_vtok_sets[zq], nkb=ulen // 128)

            def gen_inproj(cx):
                c0, nch, joint, hs, g, d, ulen, head, Lj = (cx[k] for k in ('c0', 'nch', 'joint', 'hs', 'g', 'd', 'ulen', 'head', 'Lj'))
                qkv, b_qkv, vtok, b_vtok, nkb = cx['qkv'], cx['b_qkv'], cx['vtok'], cx['b_vtok'], cx['nkb']
                wi = cnt["w"] % 2
                cnt["w"] += 1
                WS, bWS = wsl[wi], b_wsl[wi]
                em.dma(WS[:, :], wA_d[hs * 3 + g, :, :], reads=[b_wd], writes=[bWS])
                for cu in range(nch):
                    ui = cnt["u"] % 2
                    cnt["u"] += 1
                    UT, bUT = uTs[ui], b_uTs[ui]
                    em.dma(UT[:, :], uT_d[c0 + cu, :, :], reads=[b_uTd], writes=[bUT])
                    for blk in range(3):
                        bi = cnt["ip"] % 2
                        cnt["ip"] += 1
                        bank, bB = Wb[bi], b_Wb[bi]
                        em.mm_group([lambda h, kc=kc, bank=bank, WS=WS, UT=UT, blk=blk: h.matmul(
                            out=bank[:, :], lhsT=WS[:, blk * 2048 + kc * 128: blk * 2048 + (kc + 1) * 128],
                            rhs=UT[:, kc * 512:(kc + 1) * 512], start=(kc == 0), stop=(kc == 15))
                            for kc in range(16)], reads=[bWS, bUT], writes=[bB])
                        dst = qkv[blk][:, 0:ulen].rearrange("p (r l) -> p r l", l=Lj)[:, :, cu * (512 // d):(cu + 1) * (512 // d)]
                        src = bank[:, :].rearrange("p (l r) -> p r l", r=d)
                        e = evac_eng()
                        em.op(e, copy_op(e, dst, src), reads=[bB], writes=[b_qkv[blk]])
                    yield
                nkb = ulen // 128
                for b0 in range(0, nkb, 8):
                    em.mm_group([lambda h, b=b, b0=b0, Vt=qkv[2]: h.transpose(out=VTb[:, (b - b0) * 128:(b - b0 + 1) * 128],
                                                                 in_=Vt[:, b * 128:(b + 1) * 128], identity=ident[:])
                                 for b in range(b0, b0 + 8)], reads=[b_qkv[2], b_ident], writes=[b_VTb])
                    e = evac_eng()
                    em.op(e, copy_op(e, vtok[:, b0 * 128:(b0 + 8) * 128], VTb[:, :]), reads=[b_VTb], writes=[b_vtok])
                yield

            def gen_attn(cx):
                c0, nch, joint, hs, g, d, ulen, head, Lj = (cx[k] for k in ('c0', 'nch', 'joint', 'hs', 'g', 'd', 'ulen', 'head', 'Lj'))
                qkv, b_qkv, vtok, b_vtok, nkb = cx['qkv'], cx['b_qkv'], cx['vtok'], cx['b_vtok'], cx['nkb']
                Lseg = Lj // 2 if joint else Lj
                for R in range(ulen // 512):
                    ri = cnt["reg"] % 2
                    cnt["reg"] += 1
                    OB, bO, LB, bL = Ob[ri], b_Ob[ri], Lb[ri], b_Lb[ri]
                    pieces = []
                    for b in range(nkb):
                        r = (128 * b) // Lj
                        lb = 128 * b - r * Lj
                        qlo, qhi = max(0, lb - 64), min(Lj, lb + 192)
                        cuts = [qlo, qhi]
                        if joint and qlo < Lseg < qhi:
                            cuts = [qlo, Lseg, qhi]
                        for a_, b_ in zip(cuts[:-1], cuts[1:]):
                            fa, fb_ = r * Lj + a_, r * Lj + b_
                            fa2, fb2 = max(fa, 512 * R), min(fb_, 512 * R + 512)
                            if fa2 >= fb2:
                                continue
                            cross = joint and ((a_ >= Lseg) != (lb >= Lseg))
                            pieces.append((b, fa2, fb2, (fa2 - r * Lj) - (lb - 64), cross))
                    LOOK = 2
                    held = {}

                    def emit_S(pi):
                        (b, fa, fb_, j0, cross) = pieces[pi]
                        n = fb_ - fa
                        bi = cnt["W"] % 4
                        cnt["W"] += 1
                        bank, bB = Wb[bi], b_Wb[bi]
                        em.mm_group([lambda h, bank=bank, b=b, fa=fa, fb_=fb_, n=n, Kt=qkv[1], Qt=qkv[0]: h.matmul(
                            out=bank[:, 0:n], lhsT=Kt[:, b * 128:(b + 1) * 128], rhs=Qt[:, fa:fb_],
                            start=True, stop=True)], reads=[b_qkv[0], b_qkv[1]], writes=[bB])
                        ti = cnt["ts"] % 4
                        cnt["ts"] += 1
                        TS, bTS = tmpS[ti], b_tmpS[ti]
                        em.op("dve", lambda h, TS=TS, bank=bank, n=n, j0=j0, head=head: h.scalar_tensor_tensor(
                            out=TS[:, 0:n], in0=bank[:, 0:n], scalar=SCALE, in1=BT[:, head * 256 + j0: head * 256 + j0 + n],
                            op0=ALU.mult, op1=ALU.add), reads=[bB, b_BT], writes=[bTS])
                        pti = cnt["pt"] % 4
                        cnt["pt"] += 1
                        P_, bP_ = PT[pti], b_PT[pti]
                        if cross:
                            em.op("act", lambda h, P_=P_, TS=TS, n=n: h.activation(out=P_[:, 0:n], in_=TS[:, 0:n], func=AF.Exp,
                                                                                  bias=fl[:, 1:2]), reads=[bTS, b_fl], writes=[bP_])
                        else:
                            em.op("act", lambda h, P_=P_, TS=TS, n=n: h.activation(out=P_[:, 0:n], in_=TS[:, 0:n], func=AF.Exp),
                                  reads=[bTS], writes=[bP_])
                        held[pi] = (P_, bP_)

                    def emit_PV(pi):
                        (b, fa, fb_, j0, cross) = pieces[pi]
                        n = fb_ - fa
                        P_, bP_ = held.pop(pi)
                        o0 = fa - 512 * R
                        first, last = (pi == 0), (pi == len(pieces) - 1)
                        em.mm_group([
                            lambda h, OB=OB, b=b, P_=P_, n=n, o0=o0, first=first, last=last, vtok=vtok: h.matmul(
                                out=OB[:, o0:o0 + n], lhsT=vtok[:, b * 128:(b + 1) * 128], rhs=P_[:, 0:n],
                                start=first, stop=last, skip_group_check=True),
                            lambda h, LB=LB, P_=P_, n=n, o0=o0, first=first, last=last: h.matmul(
                                out=LB[:, o0:o0 + n], lhsT=ones[:, :], rhs=P_[:, 0:n],
                                start=first, stop=last, skip_group_check=True)],
                            reads=[b_vtok, bP_, b_ones], writes=[bO, bL])

                    for pi in range(len(pieces) + LOOK):
                        if pi < len(pieces):
                            emit_S(pi)
                        if pi - LOOK >= 0:
                            emit_PV(pi - LOOK)
                    for (A, bA, BK, bBK) in ((acc[0], b_acc[0], OB, bO), (acc[1], b_acc[1], LB, bL)):
                        Av = A[:, 0:ulen].rearrange("p (l r) -> p r l", r=d)
                        f = 512 * R
                        while f < 512 * R + 512:
                            r = f // Lj
                            l0 = f - r * Lj
                            n = min(Lj - l0, 512 * R + 512 - f)
                            dst = Av[:, r, l0:l0 + n]
                            src = BK[:, f - 512 * R: f - 512 * R + n]
                            if g == 0:
                                e = evac_eng()
                                em.op(e, copy_op(e, dst, src), reads=[bBK], writes=[bA])
                            else:
                                em.op("dve", lambda h, dst=dst, src=src: h.tensor_tensor(out=dst, in0=dst, in1=src, op=ALU.add),
                                      reads=[bBK, bA], writes=[bA])
                            f += n
                    yield
                if g == 2:
                    em.op("dve", lambda h, ulen=ulen: h.reciprocal(out=acc[1][:, 0:ulen], in_=acc[1][:, 0:ulen]),
                          reads=[b_acc[1]], writes=[b_acc[1]])
                    em.op("dve", lambda h, ulen=ulen: h.tensor_tensor(out=attnT[:, 0:ulen], in0=acc[0][:, 0:ulen], in1=acc[1][:, 0:ulen],
                                                                     op=ALU.mult), reads=[b_acc[0], b_acc[1]], writes=[b_attnT])
                    em.dma(art_d[c0:c0 + nch, :, :].rearrange("c p (f t) -> p c f t", t=512)[:, :, hs, :],
                           attnT[:, 0:ulen].rearrange("p (c t) -> p c t", t=512), reads=[b_attnT], writes=[b_art], sembuf=b_attnT, q=STQ)
                yield

            passes = []
            for (c0, nch, joint) in (UNITS if not OPTS.get('skip_a1') else []):
                for hs in range(4):
                    for g in range(3):
                        passes.append(a1_ctx(c0, nch, joint, hs, g, len(passes) % 2))
            if passes:
                for _ in gen_inproj(passes[0]):
                    pass
            for i, cx in enumerate(passes):
                ga = gen_attn(cx)
                gi = gen_inproj(passes[i + 1]) if i + 1 < len(passes) else iter(())
                da = di = False
                while not (da and di):
                    if not da:
                        try:
                            next(ga)
                        except StopIteration:
                            da = True
                    if not di:
                        try:
                            next(gi)
                        except StopIteration:
                            di = True
            em.flush(final_waits=[b_art])

        with contextlib.ExitStack() as st:
            cf = sb(st, "cf", [128, 24], F32)
            b_cf = em.buf("cf")
            em.op("act", lambda h: h.activation(out=cf[:, :], in_=pc[:, 108:132], func=AF.Exp, scale=-1.0), reads=[b_pc], writes=[b_cf])
            em.op("act", lambda h: h.activation(out=cf[:, :], in_=cf[:, :], func=AF.Ln, bias=1.0), reads=[b_cf], writes=[b_cf])
            em.op("dve", lambda h: h.tensor_scalar(out=cf[:, :], in0=cf[:, :], scalar1=-8.0, scalar2=None, op0=ALU.mult),
                  reads=[b_cf], writes=[b_cf])
            carry = sb(st, "carry", [128, 2], F32)
            b_carry = em.buf("carry")
            dg = sb(st, "dg", [128, 512], BF16)
            b_dg = em.buf("dg")
            NCG = OPTS.get("ncg", 1)
            wsl = [sb(st, f"wl{i}", [128, 4096], BF16) for i in range(NCG)]
            b_wsl = [em.buf(f"wl{i}") for i in range(NCG)]
            uTs = [sb(st, f"uTl{i}", [128, 8192], BF16) for i in range(2)]
            b_uTs = [em.buf(f"uTl{i}") for i in range(2)]
            rxs_sets = [[[sb(st, f"rxs{z}_{k}_{i}", [128, 2052], BF16) for i in range(2)] for k in range(NCG)] for z in range(2)]
            b_rxs_sets = [[[em.buf(f"rxs{z}_{k}_{i}") for i in range(2)] for k in range(NCG)] for z in range(2)]
            gg_sets = [[sb(st, f"gg{z}_{k}", [128, 4096], BF16) for k in range(NCG)] for z in range(2)]
            b_gg_sets = [[em.buf(f"gg{z}_{k}") for k in range(NCG)] for z in range(2)]
            xc = [sb(st, f"xc{i}", [128, 2048], F32) for i in range(2)]
            b_xc = [em.buf(f"xc{i}") for i in range(2)]
            xcb = [sb(st, f"xcb{i}", [128, 2048], BF16) for i in range(2)]
            b_xcb = [em.buf(f"xcb{i}") for i in range(2)]
            hf = [sb(st, f"hf{i}", [128, 2048], F32) for i in range(2)]
            b_hf = [em.buf(f"hf{i}") for i in range(2)]
            hb = sb(st, "hb", [128, 2048], F32)
            b_hb = em.buf("hb")
            NSET = OPTS.get("nset", 2)
            ra_l = [sb(st, f"ra{i}", [128, 2048], F32) for i in range(NSET)]
            b_ra_l = [em.buf(f"ra{i}") for i in range(NSET)]
            iu_l = [sb(st, f"iu{i}", [128, 2048], F32) for i in range(NSET)]
            b_iu_l = [em.buf(f"iu{i}") for i in range(NSET)]
            tt_l = [sb(st, f"tt{i}", [128, 2048], F32) for i in range(NSET)]
            b_tt_l = [em.buf(f"tt{i}") for i in range(NSET)]
            Wb = [ps(st, f"Wl{i}", [128, 512], F32) for i in range(2)]
            b_Wb = [em.buf(f"ps_Wl{i}") for i in range(2)]
            Cb = [ps(st, f"Cb{i}", [128, 512], F32) for i in range(2)]
            b_Cb = [em.buf(f"ps_Cb{i}") for i in range(2)]
            Gb = [ps(st, f"Gb{i}", [128, 512], F32) for i in range(2)]
            b_Gb = [em.buf(f"ps_Gb{i}") for i in range(2)]
            cnt = {"w": 0, "u": 0, "W": 0, "C": 0, "G": 0, "it": 0, "z": 0}

            def gates_and_elem(dr, s, c):
                si = cnt["it"] % NSET
                cnt["it"] += 1
                ra, b_ra, iu, b_iu, tt, b_tt = ra_l[si], b_ra_l[si], iu_l[si], b_iu_l[si], tt_l[si], b_tt_l[si]
                for j in range(4):
                    for which, dstT, bD, bcol in ((0, ra, b_ra, 60), (1, iu, b_iu, 84)):
                        gi = cnt["G"] % 2
                        cnt["G"] += 1
                        bank, bB = Gb[gi], b_Gb[gi]
                        w0 = (which * 24 + dr * 12 + c) * 128
                        em.mm_group([lambda h, bank=bank, w0=w0, s=s, j=j: h.matmul(
                            out=bank[:, :], lhsT=lw[:, w0:w0 + 128], rhs=xcb[s][:, j * 512:(j + 1) * 512], start=True, stop=True)],
                            reads=[b_lw, b_xcb[s]], writes=[bB])
                        col = bcol + dr * 12 + c
                        em.op("act", lambda h, bank=bank, dstT=dstT, j=j, col=col: h.activation(
                            out=dstT[:, j * 512:(j + 1) * 512], in_=bank[:, :], func=AF.Sigmoid, bias=pc[:, col:col + 1]),
                            reads=[bB, b_pc], writes=[bD])
                ccol = dr * 12 + c
                em.op("act", lambda h, ccol=ccol, ra=ra: h.activation(out=ra[:, :], in_=ra[:, :], func=AF.Exp, scale=cf[:, ccol:ccol + 1]),
                      reads=[b_ra, b_cf], writes=[b_ra])
                em.op("act", lambda h, ra=ra, tt=tt: h.activation(out=tt[:, :], in_=ra[:, :], func=AF.Square), reads=[b_ra], writes=[b_tt])
                em.op("act", lambda h, tt=tt: h.activation(out=tt[:, :], in_=tt[:, :], func=AF.Sqrt, scale=-1.0, bias=1.0),
                      reads=[b_tt], writes=[b_tt])
                em.op("dve", lambda h, s=s, iu=iu: h.tensor_tensor(out=iu[:, :], in0=iu[:, :], in1=xc[s][:, :], op=ALU.mult),
                      reads=[b_iu, b_xc[s]], writes=[b_iu])
                em.op("dve", lambda h, iu=iu, tt=tt: h.tensor_tensor(out=iu[:, :], in0=iu[:, :], in1=tt[:, :], op=ALU.mult),
                      reads=[b_iu, b_tt], writes=[b_iu])
                return ra, b_ra, iu, b_iu

            def gen_inproj2(c0, nch, joint, chunks, rxs_all, b_rxs_all, gg_all, b_gg_all):
                for k, c in enumerate(chunks):
                    em.dma(wsl[k][:, :], wL_d[c, :, :], reads=[b_wd], writes=[b_wsl[k]])
                for cu in range(nch):
                    ui = cnt["u"] % 2
                    cnt["u"] += 1
                    UT, bUT = uTs[ui], b_uTs[ui]
                    em.dma(UT[:, :], uT_d[c0 + cu, :, :], reads=[b_uTd], writes=[bUT])
                    s, j = divmod(cu, 4)
                    for k, c in enumerate(chunks):
                        WS, bWS = wsl[k], b_wsl[k]
                        for blk in range(2):
                            bi = cnt["W"] % 2
                            cnt["W"] += 1
                            bank, bB = Wb[bi], b_Wb[bi]
                            em.mm_group([lambda h, kc=kc, bank=bank, WS=WS, UT=UT, blk=blk: h.matmul(
                                out=bank[:, :], lhsT=WS[:, blk * 2048 + kc * 128: blk * 2048 + (kc + 1) * 128],
                                rhs=UT[:, kc * 512:(kc + 1) * 512], start=(kc == 0), stop=(kc == 15))
                                for kc in range(16)], reads=[bWS, bUT], writes=[bB])
                            if blk == 0:
                                em.op("dve", lambda h, bank=bank, j=j, dstR=rxs_all[k][s]: h.tensor_copy(
                                    out=dstR[:, 2 + j * 512: 2 + (j + 1) * 512], in_=bank[:, :]), reads=[bB], writes=[b_rxs_all[k][s]])
                            else:
                                em.op("act", lambda h, bank=bank, cu=cu, dstG=gg_all[k]: h.activation(
                                    out=dstG[:, cu * 512:(cu + 1) * 512], in_=bank[:, :], func=AF.Gelu_apprx_tanh), reads=[bB], writes=[b_gg_all[k]])
                    yield


            def lru_chunk(c, rxs, b_rxs, gg, b_gg, c0, nch, joint, nseg):
                outb, b_outb = gg, b_gg
                for t_ in range(4):
                    em.op("dve", lambda h, t_=t_, c=c: h.tensor_scalar(out=dg[:, t_ * 128:(t_ + 1) * 128], in0=ident[:, :],
                                                                        scalar1=pc[:, t_ * 12 + c: t_ * 12 + c + 1], scalar2=None,
                                                                        op0=ALU.mult), reads=[b_ident, b_pc], writes=[b_dg])
                if OPTS.get("a2_stage", 9) < 1:
                    return
                em.op("dve", lambda h: h.memset(rxs[0][:, 0:2], 0.0), writes=[b_rxs[0]])
                em.op("dve", lambda h, nseg=nseg: h.memset(rxs[nseg - 1][:, 2050:2052], 0.0), writes=[b_rxs[nseg - 1]])
                if joint:
                    em.op("dve", lambda h: h.tensor_scalar(out=rxs[0][:, 2050:2051], in0=rxs[1][:, 2:3], scalar1=fl[:, 0:1],
                                                           scalar2=None, op0=ALU.mult), reads=[b_rxs[1], b_fl], writes=[b_rxs[0]])
                    em.op("dve", lambda h: h.tensor_scalar(out=rxs[1][:, 0:2], in0=rxs[0][:, 2048:2050], scalar1=fl[:, 0:1],
                                                           scalar2=None, op0=ALU.mult), reads=[b_rxs[0], b_fl], writes=[b_rxs[1]])
                if OPTS.get("a2_stage", 9) < 2:
                    return
                for s in range(nseg):
                    for j in range(4):
                        ci = cnt["C"] % 2
                        cnt["C"] += 1
                        bank, bB = Cb[ci], b_Cb[ci]
                        em.mm_group([lambda h, bank=bank, t_=t_, s=s, j=j: h.matmul(
                            out=bank[:, :], lhsT=dg[:, t_ * 128:(t_ + 1) * 128], rhs=rxs[s][:, j * 512 + t_: j * 512 + t_ + 512],
                            start=(t_ == 0), stop=(t_ == 3)) for t_ in range(4)], reads=[b_dg, b_rxs[s]], writes=[bB])
                        em.op("act", lambda h, bank=bank, s=s, j=j, c=c: h.activation(
                            out=xc[s][:, j * 512:(j + 1) * 512], in_=bank[:, :], func=AF.Identity, bias=pc[:, 48 + c:49 + c]),
                            reads=[bB, b_pc], writes=[b_xc[s]])
                        em.op("dve", lambda h, s=s, j=j: h.tensor_copy(
                            out=xcb[s][:, j * 512:(j + 1) * 512], in_=xc[s][:, j * 512:(j + 1) * 512]),
                            reads=[b_xc[s]], writes=[b_xcb[s]])
                yield
                if OPTS.get("a2_stage", 9) < 3:
                    return
                for s in range(nseg):
                    ra, b_ra, iu, b_iu = gates_and_elem(0, s, c)
                    init = 0.0 if s == 0 else carry[:, 0:1]
                    em.op("dve", lambda h, s=s, init=init, ra=ra, iu=iu: h.tensor_tensor_scan(out=hf[s][:, :], data0=ra[:, :], data1=iu[:, :],
                                                                              initial=init, op0=ALU.mult, op1=ALU.add),
                          reads=[b_ra, b_iu, b_carry], writes=[b_hf[s]])
                    if joint and s == 0:
                        em.op("dve", lambda h: h.tensor_scalar(out=carry[:, 0:1], in0=hf[0][:, 2047:2048], scalar1=fl[:, 0:1],
                                                               scalar2=None, op0=ALU.mult), reads=[b_hf[0], b_fl], writes=[b_carry])
                    yield
                if OPTS.get("a2_stage", 9) < 4:
                    return
                for s in reversed(range(nseg)):
                    ra, b_ra, iu, b_iu = gates_and_elem(1, s, c)
                    init = 0.0 if s == nseg - 1 else carry[:, 1:2]
                    em.op("dve", lambda h, init=init, ra=ra, iu=iu: h.tensor_tensor_scan(out=hb[:, ::-1], data0=ra[:, ::-1], data1=iu[:, ::-1],
                                                                         initial=init, op0=ALU.mult, op1=ALU.add),
                          reads=[b_ra, b_iu, b_carry], writes=[b_hb])
                    if joint and s == 1:
                        em.op("dve", lambda h: h.tensor_scalar(out=carry[:, 1:2], in0=hb[:, 0:1], scalar1=fl[:, 0:1],
                                                               scalar2=None, op0=ALU.mult), reads=[b_hb, b_fl], writes=[b_carry])
                    em.op("dve", lambda h, s=s: h.tensor_tensor(out=hb[:, :], in0=hb[:, :], in1=hf[s][:, :], op=ALU.add),
                          reads=[b_hb, b_hf[s]], writes=[b_hb])
                    em.op("dve", lambda h, s=s: h.tensor_tensor(out=outb[:, s * 2048:(s + 1) * 2048], in0=hb[:, :],
                                                               in1=gg[:, s * 2048:(s + 1) * 2048], op=ALU.mult),
                          reads=[b_hb, b_gg], writes=[b_outb])
                    yield
                em.dma(art_d[c0:c0 + nch, :, :].rearrange("c p (f t) -> p c f t", t=512)[:, :, 4 + c, :],
                       outb[:, 0:nch * 512].rearrange("p (c t) -> p c t", t=512), reads=[b_outb], writes=[b_art], sembuf=b_outb, q=STQ)


            groups = []
            for (c0, nch, joint) in OPTS.get("a2_units", UNITS):
                for cg in range(OPTS.get("a2_chunks", 12) // NCG):
                    z = len(groups) % 2
                    groups.append((c0, nch, joint, [cg * NCG + k for k in range(NCG)], rxs_sets[z], b_rxs_sets[z], gg_sets[z], b_gg_sets[z]))

            def gen_lru_group(grp):
                (c0, nch, joint, chunks, rxs_all, b_rxs_all, gg_all, b_gg_all) = grp
                for k, c in enumerate(chunks):
                    yield from lru_chunk(c, rxs_all[k], b_rxs_all[k], gg_all[k], b_gg_all[k], c0, nch, joint, nch // 4)

            if groups:
                for _ in gen_inproj2(*groups[0]):
                    pass
            for i, grp in enumerate(groups):
                ga = gen_lru_group(grp)
                gi = gen_inproj2(*groups[i + 1]) if i + 1 < len(groups) else iter(())
                da = di = False
                while not (da and di):
                    if not da:
                        try:
                            next(ga)
                        except StopIteration:
                            da = True
                    if not di:
                        try:
                            next(gi)
                        except StopIteration:
                            di = True
            if debug and "art" in dbg:
                em.dma(dbg["art"][:, :, :], art_d[:, :, :], reads=[b_art], sembuf=b_art)
            em.flush(final_waits=[b_art])

        with contextlib.ExitStack() as st:
            gfin = sb(st, "gfin", [128, D], F32)
            b_gfin = em.buf("gfin")
            em.dma(gfin[:], gains_d[2].partition_broadcast(128), writes=[b_gfin])
            gmT = sb(st, "gmT", [128, 16], F32)
            b_gmT = em.buf("gmT")
            em.dma(gmT[:], gT_d[:, :], writes=[b_gmT])
            NS = 4
            ring = [sb(st, f"ring{i}", [128, 8192], BF16) for i in range(NS)]
            b_ring = [em.buf(f"ring{i}") for i in range(NS)]
            AU = [sb(st, f"AU{i}", [128, 8192], BF16) for i in range(2)]
            b_AU = [em.buf(f"AU{i}") for i in range(2)]
            arT = sb(st, "arT", [128, 8192], BF16)
            b_arT = em.buf("arT")
            MH = sb(st, "MH", [128, 8192], BF16)
            b_MH = em.buf("MH")
            x1 = sb(st, "x1", [128, 4 * D], F32)
            b_x1 = [em.buf(f"x1_{i}") for i in range(4)]
            u2 = [sb(st, f"u2_{i}", [128, D], BF16) for i in range(2)]
            b_u2 = [em.buf(f"u2_{i}") for i in range(2)]
            sg = [sb(st, f"sg{i}", [128, 512], BF16) for i in range(4)]
            b_sg = [em.buf(f"sg{i}") for i in range(4)]
            tm = [sb(st, f"tm{i}", [128, 512], F32) for i in range(4)]
            b_tm = [em.buf(f"tm{i}") for i in range(4)]
            stt2 = sb(st, "stt2", [128, 8], F32)
            b_stt2 = [em.buf(f"stt2_{i}") for i in range(4)]
            PB = [ps(st, f"PB{i}", [128, 512], F32) for i in range(8)]
            b_PB = [em.buf(f"ps_PB{i}") for i in range(8)]
            b_y = em.buf("y")

            seq = []
            for ci in range(NCH):
                for jb in range(4):
                    seq.append([(0, 8192, wG_d[jb, :, :])])
                    seq.append([(0, 8192, wG_d[4 + jb, :, :])])
                    seq.append([(0, 6144, wRO_d[jb, :, :]), (6144, 8192, wAO_d[0, :, jb * 2048:(jb + 1) * 2048])])
                for fb in range(4):
                    seq.append([(0, 8192, wO_d[fb, :, :])])
                for qd in range(4):
                    for jj in range(4):
                        seq.append([(0, 8192, w1_s[qd * 4 + jj, :, :])])
                    for fb in range(4):
                        seq.append([(0, 8192, w2_s[fb * 4 + qd, :, :])])
            if OPTS.get("b_chunks") is not None:
                seq = seq[:48 * OPTS["b_chunks"]]
            ss_ = {"issued": 0, "next": 0}

            def slab_issue_upto(n):
                while ss_["issued"] < min(n, len(seq)):
                    k = ss_["issued"]
                    for lo, hi, src in seq[k]:
                        em.dma(ring[k % NS][:, lo:hi], src, reads=[b_wd], writes=[b_ring[k % NS]])
                    ss_["issued"] += 1

            def slab_next():
                k = ss_["next"]
                ss_["next"] += 1
                slab_issue_upto(k + 1)
                return ring[k % NS], b_ring[k % NS]

            def slab_prefetch():
                slab_issue_upto(ss_["next"] + NS)

            def rms_stats(src_ap, k, rb):
                ssa, rsa, bS = stt2[:, 2 * k:2 * k + 1], stt2[:, 2 * k + 1:2 * k + 2], b_stt2[k]
                jk = u2[k % 2]
                em.op("act", lambda h: h.activation(out=jk[:, :], in_=src_ap, func=AF.Square, accum_out=ssa),
                      reads=[rb], writes=[b_u2[k % 2], bS])
                em.op("dve", lambda h: h.tensor_scalar(out=rsa, in0=ssa, scalar1=1.0 / D, scalar2=EPS, op0=ALU.mult, op1=ALU.add),
                      reads=[bS], writes=[bS])
                em.op("act", lambda h: h.activation(out=rsa, in_=rsa, func=AF.Sqrt), reads=[bS], writes=[bS])
                em.op("dve", lambda h: h.reciprocal(out=rsa, in_=rsa), reads=[bS], writes=[bS])
                return rsa, bS

            nchunks = OPTS.get("b_chunks", NCH)
            em.dma(AU[0][:, :], uT_d[0, :, :], reads=[b_uTd], writes=[b_AU[0]])
            em.dma(arT[:, :], art_d[0, :, :], reads=[b_art], writes=[b_arT])
            for ci in range(nchunks):
                UT, bUT = AU[ci % 2], b_AU[ci % 2]
                em.dma(x1[:, :].rearrange("p (t d) -> p t d", d=D), x_d[ci * 512:(ci + 1) * 512, :].rearrange("(t p) d -> p t d", p=128),
                       writes=b_x1)
                for jb in range(4):
                    GA, bGA = slab_next()
                    GR, bGR = slab_next()
                    RO, bRO = slab_next()
                    for cb in range(4):
                        j = jb * 4 + cb
                        pb = 4 * (j % 2)
                        em.mm_group([lambda h, kc=kc, pb=pb, cb=cb, GA=GA, UT=UT: h.matmul(
                            out=PB[pb][:, :], lhsT=GA[:, cb * 2048 + kc * 128: cb * 2048 + (kc + 1) * 128],
                            rhs=UT[:, kc * 512:(kc + 1) * 512], start=(kc == 0), stop=(kc == 15)) for kc in range(16)],
                            reads=[bGA, bUT], writes=[b_PB[pb]])
                        em.mm_group([lambda h, kc=kc, pb=pb, cb=cb, GR=GR, UT=UT: h.matmul(
                            out=PB[pb + 1][:, :], lhsT=GR[:, cb * 2048 + kc * 128: cb * 2048 + (kc + 1) * 128],
                            rhs=UT[:, kc * 512:(kc + 1) * 512], start=(kc == 0), stop=(kc == 15)) for kc in range(16)],
                            reads=[bGR, bUT], writes=[b_PB[pb + 1]])
                        em.mm_group([lambda h, kc=kc, pb=pb, cb=cb, RO=RO: h.matmul(
                            out=PB[pb + 2][:, :], lhsT=RO[:, 6144 + cb * 512 + kc * 128: 6144 + cb * 512 + (kc + 1) * 128],
                            rhs=arT[:, kc * 512:(kc + 1) * 512], start=(kc == 0), stop=(kc == 3)) for kc in range(4)],
                            reads=[bRO, b_arT], writes=[b_PB[pb + 2]])
                        em.mm_group([lambda h, kc=kc, pb=pb, cb=cb, RO=RO: h.matmul(
                            out=PB[pb + 3][:, :], lhsT=RO[:, cb * 1536 + kc * 128: cb * 1536 + (kc + 1) * 128],
                            rhs=arT[:, (4 + kc) * 512:(5 + kc) * 512], start=(kc == 0), stop=(kc == 11)) for kc in range(12)],
                            reads=[bRO, b_arT], writes=[b_PB[pb + 3]])
                        q2 = 2 * (j % 2)
                        em.op("act", lambda h, pb=pb, q2=q2: h.activation(out=sg[q2][:, :], in_=PB[pb][:, :], func=AF.Sigmoid),
                              reads=[b_PB[pb]], writes=[b_sg[q2]])
                        em.op("act", lambda h, pb=pb, q2=q2: h.activation(out=sg[q2 + 1][:, :], in_=PB[pb + 1][:, :], func=AF.Sigmoid),
                              reads=[b_PB[pb + 1]], writes=[b_sg[q2 + 1]])
                        em.op("dve", lambda h, pb=pb, q2=q2: h.tensor_tensor(out=tm[q2][:, :], in0=PB[pb + 2][:, :], in1=sg[q2][:, :],
                                                                            op=ALU.mult), reads=[b_PB[pb + 2], b_sg[q2]], writes=[b_tm[q2]])
                        em.op("dve", lambda h, pb=pb, q2=q2: h.tensor_tensor(out=tm[q2 + 1][:, :], in0=PB[pb + 3][:, :], in1=sg[q2 + 1][:, :],
                                                                            op=ALU.mult), reads=[b_PB[pb + 3], b_sg[q2 + 1]], writes=[b_tm[q2 + 1]])
                        em.op("dve", lambda h, j=j, q2=q2: h.tensor_tensor(out=MH[:, j * 512:(j + 1) * 512], in0=tm[q2][:, :], in1=tm[q2 + 1][:, :],
                                                                          op=ALU.add), reads=[b_tm[q2], b_tm[q2 + 1]], writes=[b_MH])
                    slab_prefetch()
                if "mT" in dbg and ci == 0:
                    em.dma(dbg["mT"][:, :], MH[:, :], reads=[b_MH], writes=[b_y], sembuf=b_MH)
                for fb in range(4):
                    WO, bWO = slab_next()
                    pb = 4 * (fb % 2)
                    for kc in range(16):
                        em.mm_group([lambda h, kc=kc, tt=tt, pb=pb, WO=WO: h.matmul(
                            out=PB[pb + tt][:, :], lhsT=MH[:, kc * 512 + tt * 128: kc * 512 + (tt + 1) * 128],
                            rhs=WO[:, kc * 512:(kc + 1) * 512], start=(kc == 0), stop=(kc == 15)) for tt in range(4)],
                            reads=[bWO, b_MH], writes=[b_PB[pb + tt] for tt in range(4)])
                    for tt in range(4):
                        dst = x1[:, tt * D + fb * 512: tt * D + (fb + 1) * 512]
                        em.op("dve", lambda h, dst=dst, pb=pb, tt=tt: h.tensor_tensor(out=dst, in0=PB[pb + tt][:, :], in1=dst, op=ALU.add),
                              reads=[b_PB[pb + tt], b_x1[tt]], writes=[b_x1[tt]])
                    slab_prefetch()
                if "x1" in dbg:
                    em.dma(dbg["x1"][ci * 512:(ci + 1) * 512, :].rearrange("(t p) d -> p t d", p=128),
                           x1[:, :].rearrange("p (t d) -> p t d", d=D), reads=b_x1, writes=[b_y], sembuf=b_x1[0])
                if ci + 1 < nchunks:
                    em.dma(AU[(ci + 1) % 2][:, :], uT_d[ci + 1, :, :], reads=[b_uTd], writes=[b_AU[(ci + 1) % 2]])
                    em.dma(arT[:, :], art_d[ci + 1, :, :], reads=[b_art], writes=[b_arT])
                for tt in range(4):
                    rsa, bS = rms_stats(x1[:, tt * D:(tt + 1) * D], tt, b_x1[tt])
                    U2, bU2 = u2[tt % 2], b_u2[tt % 2]
                    em.op("dve", lambda h, U2=U2, tt=tt, rsa=rsa: h.tensor_scalar(out=U2[:, :], in0=x1[:, tt * D:(tt + 1) * D], scalar1=rsa,
                                                                                scalar2=None, op0=ALU.mult), reads=[b_x1[tt], bS], writes=[bU2])
                    pbk = 2 * (tt % 2)
                    for half in range(2):
                        bank = PB[pbk + half][:, :].bitcast(BF16)
                        em.mm_group([lambda h, kc=kc, bank=bank, U2=U2, half=half: h.transpose(
                            out=bank[:, kc * 128:(kc + 1) * 128], in_=U2[:, (half * 8 + kc) * 128:(half * 8 + kc + 1) * 128], identity=ident[:])
                            for kc in range(8)], reads=[bU2, b_ident], writes=[b_PB[pbk + half]])
                        em.op("dve", lambda h, bank=bank, half=half, tt=tt, UT=UT: h.tensor_tensor(
                            out=UT[:, half * 4096:(half + 1) * 4096].rearrange("p (kc t) -> p kc t", t=512)[:, :, tt * 128:(tt + 1) * 128],
                            in0=bank.rearrange("p (kc t) -> p kc t", t=128),
                            in1=gmT[:, half * 8:(half + 1) * 8].unsqueeze(2).broadcast_to([128, 8, 128]), op=ALU.mult),
                            reads=[b_PB[pbk + half], b_gmT], writes=[bUT])
                for qd in range(4):
                    for jj in range(4):
                        W1, bW1 = slab_next()
                        for cb in range(4):
                            hk = jj * 4 + cb
                            pb = hk % 2
                            em.mm_group([lambda h, kc=kc, pb=pb, cb=cb, W1=W1, UT=UT: h.matmul(
                                out=PB[pb][:, :], lhsT=W1[:, cb * 2048 + kc * 128: cb * 2048 + (kc + 1) * 128],
                                rhs=UT[:, kc * 512:(kc + 1) * 512], start=(kc == 0), stop=(kc == 15)) for kc in range(16)],
                                reads=[bW1, bUT], writes=[b_PB[pb]])
                            q2 = hk % 4
                            em.op("act", lambda h, pb=pb, q2=q2: h.activation(out=tm[q2][:, :], in_=PB[pb][:, :], func=AF.Relu),
                                  reads=[b_PB[pb]], writes=[b_tm[q2]])
                            em.op("dve", lambda h, hk=hk, q2=q2: h.tensor_tensor(out=MH[:, hk * 512:(hk + 1) * 512], in0=tm[q2][:, :], in1=tm[q2][:, :],
                                                                                op=ALU.mult), reads=[b_tm[q2]], writes=[b_MH])
                        slab_prefetch()
                    for fb in range(4):
                        W2, bW2 = slab_next()
                        for kc in range(16):
                            em.mm_group([lambda h, kc=kc, tt=tt, W2=W2: h.matmul(
                                out=PB[4 + tt][:, :], lhsT=MH[:, kc * 512 + tt * 128: kc * 512 + (tt + 1) * 128],
                                rhs=W2[:, kc * 512:(kc + 1) * 512], start=(kc == 0), stop=(kc == 15)) for tt in range(4)],
                                reads=[bW2, b_MH], writes=[b_PB[4 + tt] for tt in range(4)])
                        for tt in range(4):
                            dst = x1[:, tt * D + fb * 512: tt * D + (fb + 1) * 512]
                            em.op("dve", lambda h, dst=dst, tt=tt: h.tensor_tensor(out=dst, in0=PB[4 + tt][:, :], in1=dst, op=ALU.add),
                                  reads=[b_PB[4 + tt], b_x1[tt]], writes=[b_x1[tt]])
                        slab_prefetch()
                for tt in range(4):
                    rsa, bS = rms_stats(x1[:, tt * D:(tt + 1) * D], tt, b_x1[tt])
                    em.op("dve", lambda h, tt=tt, rsa=rsa: h.scalar_tensor_tensor(out=x1[:, tt * D:(tt + 1) * D], in0=x1[:, tt * D:(tt + 1) * D],
                                                                                scalar=rsa, in1=gfin[:, :], op0=ALU.mult, op1=ALU.mult),
                          reads=[b_x1[tt], bS, b_gfin], writes=[b_x1[tt]])
                em.dma(y_d[ci * 512:(ci + 1) * 512, :].rearrange("(t p) d -> p t d", p=128), x1[:, :].rearrange("p (t d) -> p t d", d=D),
                       reads=b_x1, writes=[b_y], sembuf=b_x1[0], q=STQ)
            em.flush(final_waits=[b_y])
    return nc


def _t5_bucket_np(rel):
    import math
    rel = np.asarray(rel, np.int64)
    half, max_exact = 16, 8
    n = np.abs(rel)
    nf = np.maximum(n, 1).astype(np.float64)
    large = max_exact + (np.log(nf / max_exact) / math.log(1024 / max_exact) * (half - max_exact)).astype(np.int64)
    large = np.minimum(large, half - 1)
    return np.where(rel > 0, half, 0) + np.where(n < max_exact, n, large)


def _make_bt(rel_bias):
    k = np.arange(128)[:, None]
    j = np.arange(256)[None, :]
    rel = k - j + 64
    valid = np.abs(rel) <= 64
    bt = np.full((128, 12, 256), NEG, np.float32)
    for g, d in enumerate((1, 4, 16)):
        bk = _t5_bucket_np(np.clip(rel, -64, 64) * d)
        for h in range(4):
            head = g * 4 + h
            bt[:, head, :] = np.where(valid, rel_bias[bk, head], np.float32(NEG))
    return np.ascontiguousarray(bt.reshape(128, 12 * 256))


def _make_pc(inputs):
    f = lambda a: np.asarray(a, dtype=np.float32)
    pc = np.zeros((128, 136), np.float32)
    cw = f(inputs["conv_w"])[0]
    for t in range(4):
        pc[:, t * 12:(t + 1) * 12] = cw[t].reshape(12, 128).T
    pc[:, 48:60] = f(inputs["conv_b"])[0].reshape(12, 128).T
    for i, nm in enumerate(["lru_ba", "lru_bx", "lru_lambda"]):
        a = f(inputs[nm])[0]
        for dr in range(2):
            pc[:, 60 + i * 24 + dr * 12: 60 + i * 24 + (dr + 1) * 12] = a[dr].reshape(12, 128).T
    return pc


_NC_CACHE = {}


def kernel(**inputs):
    import ml_dtypes
    f = lambda a: np.ascontiguousarray(np.asarray(a, dtype=np.float32))
    xp = f(inputs["x_prompt"])
    xs = f(inputs["x_sample"])
    common = {
        "w_in": f(inputs["w_in"])[0],
        "w_attn_o": f(inputs["w_attn_o"])[0],
        "w_rnn_o": f(inputs["w_rnn_o"])[0],
        "w_out": f(inputs["w_out"])[0],
        "w_mlp_in": f(inputs["w_mlp_in"])[0],
        "w_mlp_out": f(inputs["w_mlp_out"])[0],
        "lru_wa": f(inputs["lru_wa"])[0].reshape(24, 128, 128),
        "lru_wx": f(inputs["lru_wx"])[0].reshape(24, 128, 128),
        "gains": np.ascontiguousarray(np.stack([f(inputs["norm_mix_g"])[0], f(inputs["norm_mlp_g"])[0], f(inputs["norm_final_g"])])),
        "gT": np.ascontiguousarray(f(inputs["norm_mlp_g"])[0].reshape(16, 128).T),
        "pc": _make_pc(inputs),
        "bt": _make_bt(f(inputs["rel_bias"])),
        "ident": np.eye(128, dtype=np.float32).astype(ml_dtypes.bfloat16),
    }
    in_maps = []
    for c in range(8):
        m = dict(common)
        if c < 4:
            m["x"] = np.ascontiguousarray(np.concatenate([xs[c], xp[c]], axis=0))
            m["flag"] = np.ones((128, 1), np.float32)
        else:
            b0 = 4 + 3 * (c - 4)
            m["x"] = np.ascontiguousarray(xp[b0:b0 + 3].reshape(T, D))
            m["flag"] = np.zeros((128, 1), np.float32)
        in_maps.append(m)
    if "nc" not in _NC_CACHE:
        _NC_CACHE["nc"] = build()
    res = run_bass_kernel_spmd(_NC_CACHE["nc"], in_maps, core_ids=list(range(8)))
    y_prompt = np.empty((16, 2048, D), np.float32)
    y_sample = np.empty((4, 4096, D), np.float32)
    for c in range(8):
        y = res.results[c]["y"]
        if c < 4:
            y_sample[c] = y[:4096]
            y_prompt[c] = y[4096:]
        else:
            b0 = 4 + 3 * (c - 4)
            y_prompt[b0:b0 + 3] = y.reshape(3, 2048, D)
    return (y_prompt, y_sample)
```

```python
import contextlib
import numpy as np
import concourse.bass as bass
import concourse.mybir as mybir
from concourse.bass_utils import run_bass_kernel_spmd

F32 = mybir.dt.float32
BF16 = mybir.dt.bfloat16
AF = mybir.ActivationFunctionType
ALU = mybir.AluOpType

D = 2048
T = 6144
NCH = 12
CH = 512
D_IN = 11776
DFF = 8192
NEG = -30000.0
EPS = 1e-6
SAME_ENGINE_WAITS = True
STQ = "pool"
OPTS = {}


class Buf:
    __slots__ = ("name", "w", "r", "dsem", "dcount", "excl")

    def __init__(self, name):
        self.name = name
        self.w = {}
        self.r = {}
        self.dsem = None
        self.dcount = 0
        self.excl = name.startswith("ps_")


class Eng:
    def __init__(self, name, sem):
        self.name = name
        self.sem = sem
        self.count = 0
        self.waited = {}
        self.prog = []


class Emitter:
    def __init__(self, nc, stack):
        self.nc = nc
        self.stack = stack
        self.engs = {}
        for n in ("pe", "act", "dve", "pool", "sp"):
            self.engs[n] = Eng(n, stack.enter_context(nc.semaphore("sem_" + n)))
        self.nbuf = 0

    def buf(self, name=None):
        self.nbuf += 1
        return Buf(name or f"b{self.nbuf}")

    def _dsem(self, b):
        if b.dsem is None:
            b.dsem = self.stack.enter_context(self.nc.semaphore("ds_" + b.name))
        return b.dsem

    def _deps(self, eng, reads, writes):
        deps = {}

        def add(tok):
            if tok is None:
                return
            s, v = tok
            if deps.get(s, 0) < v:
                deps[s] = v

        for b in reads:
            for s, v in b.w.items():
                add((s, v))
            if b.excl:
                for s, v in b.r.items():
                    if s is not eng.sem:
                        add((s, v))
        for b in writes:
            for s, v in b.w.items():
                add((s, v))
            for s, v in b.r.items():
                add((s, v))
        for s, v in deps.items():
            if s is eng.sem and (eng.name == "pe" or not SAME_ENGINE_WAITS):
                continue
            if eng.waited.get(s, 0) < v:
                eng.waited[s] = v
                eng.prog.append(lambda h, s=s, v=v: h.wait_ge(s, v))

    def _mark(self, tok, reads, writes):
        s, v = tok
        for b in reads:
            if b.r.get(s, 0) < v:
                b.r[s] = v
        for b in writes:
            if b.w.get(s, 0) < v:
                b.w[s] = v
            b.r = {}

    def op(self, engname, fn, reads=(), writes=()):
        eng = self.engs[engname]
        self._deps(eng, reads, writes)
        eng.count += 1
        cnt = eng.count
        sem = eng.sem
        eng.prog.append(lambda h, fn=fn, sem=sem: fn(h).then_inc(sem, 1))
        self._mark((sem, cnt), reads, writes)

    def mm_group(self, fns, reads=(), writes=()):
        eng = self.engs["pe"]
        self._deps(eng, reads, writes)
        eng.count += 1
        cnt = eng.count
        sem = eng.sem
        for f in fns[:-1]:
            eng.prog.append(lambda h, f=f: f(h))
        eng.prog.append(lambda h, f=fns[-1], sem=sem: f(h).then_inc(sem, 1))
        self._mark((sem, cnt), reads, writes)

    def dma(self, out, in_, reads=(), writes=(), sembuf=None, q="sp", **kw):
        eng = self.engs[q]
        self._deps(eng, reads, writes)
        b = sembuf or (writes[0] if writes else reads[0])
        s = self._dsem(b)
        b.dcount += 16
        v = b.dcount
        eng.prog.append(lambda h, out=out, in_=in_, s=s, kw=kw: h.dma_start(out=out, in_=in_, **kw).then_inc(s, 16))
        self._mark((s, v), reads, writes)

    def flush(self, final_waits=()):
        nc = self.nc
        progs = {n: e.prog for n, e in self.engs.items()}
        for e in self.engs.values():
            e.prog = []
        with nc.Block() as block:
            @block.tensor
            def _(h):
                for t in progs["pe"]:
                    t(h)

            @block.scalar
            def _(h):
                for t in progs["act"]:
                    t(h)

            @block.vector
            def _(h):
                for t in progs["dve"]:
                    t(h)

            @block.gpsimd
            def _(h):
                for t in progs["pool"]:
                    t(h)

            @block.sync
            def _(h):
                for t in progs["sp"]:
                    t(h)
                for b in final_waits:
                    for s, v in list(b.w.items()) + list(b.r.items()):
                        h.wait_ge(s, v)


def r3(ap, pat, **kw):
    return ap.rearrange(pat, **kw)


def build(debug=None):
    nc = bass.Bass("TRN2", target_bir_lowering=False)
    dt = nc.dram_tensor
    x_d = dt("x", [T, D], F32, kind="ExternalInput").ap()
    w_in_d = dt("w_in", [D, D_IN], F32, kind="ExternalInput").ap()
    w_ao_d = dt("w_attn_o", [512, D], F32, kind="ExternalInput").ap()
    w_ro_d = dt("w_rnn_o", [1536, D], F32, kind="ExternalInput").ap()
    w_out_d = dt("w_out", [D, D], F32, kind="ExternalInput").ap()
    w1_d = dt("w_mlp_in", [D, DFF], F32, kind="ExternalInput").ap()
    w2_d = dt("w_mlp_out", [DFF, D], F32, kind="ExternalInput").ap()
    lwa_d = dt("lru_wa", [24, 128, 128], F32, kind="ExternalInput").ap()
    lwx_d = dt("lru_wx", [24, 128, 128], F32, kind="ExternalInput").ap()
    gains_d = dt("gains", [3, D], F32, kind="ExternalInput").ap()
    pc_d = dt("pc", [128, 136], F32, kind="ExternalInput").ap()
    bt_d = dt("bt", [128, 12 * 256], F32, kind="ExternalInput").ap()
    ident_d = dt("ident", [128, 128], BF16, kind="ExternalInput").ap()
    flag_d = dt("flag", [128, 1], F32, kind="ExternalInput").ap()
    gT_d = dt("gT", [128, 16], F32, kind="ExternalInput").ap()
    y_d = dt("y", [T, D], F32, kind="ExternalOutput").ap()

    uT_d = dt("uT_s", [NCH, 128, 16 * CH], BF16, kind="Internal").ap()
    art_d = dt("art_s", [NCH, 128, 16 * CH], BF16, kind="Internal").ap()
    wA_d = dt("wA_s", [12, 128, 6144], BF16, kind="Internal").ap()
    wL_d = dt("wL_s", [12, 128, 4096], BF16, kind="Internal").ap()
    wG_d = dt("wG_s", [8, 128, 8192], BF16, kind="Internal").ap()
    wAO_d = dt("wAO_s", [1, 128, 8192], BF16, kind="Internal").ap()
    wRO_d = dt("wRO_s", [4, 128, 6144], BF16, kind="Internal").ap()
    wO_d = dt("wO_s", [4, 128, 8192], BF16, kind="Internal").ap()
    w1_s = dt("w1_s", [16, 128, 8192], BF16, kind="Internal").ap()
    w2_s = dt("w2_s", [16, 128, 8192], BF16, kind="Internal").ap()

    dbg = {}
    if debug:
        for name, shape, dty in debug:
            dbg[name] = dt("dbg_" + name, shape, dty, kind="ExternalOutput").ap()

    with contextlib.ExitStack() as gstack:
        em = Emitter(nc, gstack)
        sb = lambda st, name, shape, dty: st.enter_context(nc.sbuf_tensor("s_" + name, shape, dty))
        ps = lambda st, name, shape, dty: st.enter_context(nc.psum_tensor("p_" + name, shape, dty))

        ident = sb(gstack, "ident", [128, 128], BF16)
        pc = sb(gstack, "pc", [128, 136], F32)
        lw = sb(gstack, "lw", [128, 48 * 128], BF16)
        b_ident, b_pc, b_lw = em.buf("ident"), em.buf("pc"), em.buf("lw")
        em.dma(ident[:], ident_d[:, :], writes=[b_ident])
        em.dma(pc[:], pc_d[:, :], writes=[b_pc])
        fl = sb(gstack, "fl", [128, 2], F32)
        b_fl = em.buf("fl")
        em.dma(fl[:, 0:1], flag_d[:, :], writes=[b_fl])
        em.op("dve", lambda h: h.tensor_scalar(out=fl[:, 1:2], in0=fl[:, 0:1], scalar1=-1.0, scalar2=-NEG,
                                               op0=ALU.add, op1=ALU.mult), reads=[b_fl], writes=[b_fl])
        ones = sb(gstack, "ones", [128, 128], BF16)
        b_ones = em.buf("ones")
        em.op("dve", lambda h: h.memset(ones[:], 1.0), writes=[b_ones])

        with contextlib.ExitStack() as st:
            stg = [sb(st, f"stg{i}", [128, 8192], F32) for i in range(2)]
            slb = [sb(st, f"slb{i}", [128, 8192], BF16) for i in range(2)]
            b_stg = [em.buf(f"stg{i}") for i in range(2)]
            b_slb = [em.buf(f"slb{i}") for i in range(2)]
            b_wd = em.buf("wdram")
            b_wA = em.buf("wdramA")
            state = {"n": 0, "ci": 0}

            def cast(out, in_, rb, wb):
                state["ci"] += 1
                if state["ci"] % 2 == 0:
                    em.op("act", lambda h: h.activation(out=out, in_=in_, func=AF.Copy), reads=[rb], writes=[wb])
                else:
                    em.op("dve", lambda h: h.tensor_copy(out=out, in_=in_), reads=[rb], writes=[wb])

            def prep(loads, casts, nel, dst, sbuf_dst=None):
                i = state["n"] % 2
                state["n"] += 1
                S, L = stg[i], slb[i]
                for vf, src in loads:
                    em.dma(vf(S), src, writes=[b_stg[i]])
                if sbuf_dst is not None:
                    for of, inf in casts:
                        cast(of(sbuf_dst[0]), inf(S), b_stg[i], sbuf_dst[1])
                    return
                for of, inf in casts:
                    cast(of(L), inf(S), b_stg[i], b_slb[i])
                em.dma(dst, L[:, 0:nel], reads=[b_slb[i]], writes=[b_wA], sembuf=b_slb[i], q=STQ)

            def rows(w, nkc):
                return w.rearrange("(kc p) c -> p kc c", p=128)

            win_v = rows(w_in_d, 16)

            def stat_slab(src_v, nkc, c0, dst):
                n = nkc * 512
                loads = [(lambda S: S[:, 0:n].rearrange("p (kc c) -> p kc c", c=512), src_v[:, :, c0:c0 + 512])]
                casts = []
                for cb in range(4):
                    casts.append((
                        lambda L, cb=cb: L[:, cb * nkc * 128:(cb + 1) * nkc * 128].rearrange("p (kc c) -> p kc c", c=128),
                        lambda S, cb=cb: S[:, 0:n].rearrange("p (kc c) -> p kc c", c=512)[:, :, cb * 128:(cb + 1) * 128]))
                prep(loads, casts, n, dst)

            def mov_slab(src_v, c0, dst):
                loads = [(lambda S: S[:, :].rearrange("p (kc c) -> p kc c", c=512), src_v[:, :, c0:c0 + 512])]
                casts = [(lambda L, j=j: L[:, j * 2048:(j + 1) * 2048], lambda S, j=j: S[:, j * 2048:(j + 1) * 2048])
                         for j in range(4)]
                prep(loads, casts, 8192, dst)

            def cols_slab(col_offs, dst):
                nb = len(col_offs)
                loads = [(lambda S, j=j: S[:, j * 2048:(j + 1) * 2048].rearrange("p (kc c) -> p kc c", c=128),
                          win_v[:, :, c0:c0 + 128]) for j, c0 in enumerate(col_offs)]
                casts = [(lambda L, j=j: L[:, j * 2048:(j + 1) * 2048], lambda S, j=j: S[:, j * 2048:(j + 1) * 2048])
                         for j in range(nb)]
                prep(loads, casts, nb * 2048, dst)

            loads = [(lambda S: S[:, 0:3072].rearrange("p (n o) -> p n o", o=128), lwa_d.rearrange("n i o -> i n o")),
                     (lambda S: S[:, 3072:6144].rearrange("p (n o) -> p n o", o=128), lwx_d.rearrange("n i o -> i n o"))]
            casts = [(lambda L, j=j: L[:, j * 3072:(j + 1) * 3072], lambda S, j=j: S[:, j * 3072:(j + 1) * 3072])
                     for j in range(2)]
            prep(loads, casts, 6144, None, sbuf_dst=(lw, b_lw))

            for h in range(4):
                for g in range(3):
                    head = g * 4 + h
                    cols_slab([head * 128, 1536 + head * 128, 3072 + head * 128], wA_d[h * 3 + g, :, :])
            for c in range(12):
                cols_slab([4608 + c * 128, 6144 + c * 128], wL_d[c, :, :])
            ao_v = rows(w_ao_d, 4)
            ro_v = rows(w_ro_d, 12)
            wo_v = rows(w_out_d, 16)
            w1_v = rows(w1_d, 16)
            w2_v = w2_d.rearrange("(kg kc p) c -> kg p kc c", p=128, kc=16)

            def half_stat(src_v, nkc, c0, dst, hf):
                n = nkc * 256

                def job_load(S, bS, L, bL):
                    em.dma(S[:, 0:n].rearrange("p (kc c) -> p kc c", c=256), src_v[:, :, c0 + hf * 256: c0 + hf * 256 + 256],
                           writes=[bS], q="act")

                def job_fin(S, bS, L, bL):
                    for cb in range(2):
                        cast(L[:, cb * nkc * 128:(cb + 1) * nkc * 128].rearrange("p (kc c) -> p kc c", c=128),
                             S[:, 0:n].rearrange("p (kc c) -> p kc c", c=256)[:, :, cb * 128:(cb + 1) * 128], bS, bL)
                    em.dma(dst[:, hf * n:(hf + 1) * n], L[:, 0:n], reads=[bL], writes=[b_wd], sembuf=bL, q=STQ)
                return (job_load, job_fin)

            def half_mov(src_v, c0, dst, hf):
                def job_load(S, bS, L, bL):
                    em.dma(S[:, 0:4096].rearrange("p (kc c) -> p kc c", c=512), src_v[:, hf * 8:(hf + 1) * 8, c0:c0 + 512],
                           writes=[bS], q="act")

                def job_fin(S, bS, L, bL):
                    for j in range(2):
                        cast(L[:, j * 2048:(j + 1) * 2048], S[:, j * 2048:(j + 1) * 2048], bS, bL)
                    em.dma(dst[:, hf * 4096:(hf + 1) * 4096], L[:, 0:4096], reads=[bL], writes=[b_wd], sembuf=bL, q=STQ)
                return (job_load, job_fin)

            def half_ao(hf):
                def job_load(S, bS, L, bL):
                    em.dma(S[:, 0:4096].rearrange("p (kc c) -> p kc c", c=1024), ao_v[:, :, hf * 1024:(hf + 1) * 1024],
                           writes=[bS], q="act")

                def job_fin(S, bS, L, bL):
                    for q in range(2):
                        cast(L[:, q * 2048:(q + 1) * 2048].rearrange("p (cb kc c) -> p cb kc c", kc=4, c=128),
                             S[:, 0:4096].rearrange("p (kc c) -> p kc c", c=1024)[:, :, q * 512:(q + 1) * 512]
                             .rearrange("p kc (cb c) -> p cb kc c", c=128), bS, bL)
                    em.dma(wAO_d[0, :, hf * 4096:(hf + 1) * 4096], L[:, 0:4096], reads=[bL], writes=[b_wd], sembuf=bL, q=STQ)
                return (job_load, job_fin)

            bjobs = []
            for hf in range(2):
                bjobs.append(half_ao(hf))
            for jb in range(4):
                for hf in range(2):
                    bjobs.append(half_stat(win_v, 16, 7680 + jb * 512, wG_d[jb, :, :], hf))
                    bjobs.append(half_stat(win_v, 16, 9728 + jb * 512, wG_d[4 + jb, :, :], hf))
                    bjobs.append(half_stat(ro_v, 12, jb * 512, wRO_d[jb, :, :], hf))
            for fb in range(4):
                for hf in range(2):
                    bjobs.append(half_mov(wo_v, fb * 512, wO_d[fb, :, :], hf))
            for j in range(16):
                for hf in range(2):
                    bjobs.append(half_stat(w1_v, 16, j * 512, w1_s[j, :, :], hf))
            for fb in range(4):
                for kg in range(4):
                    for hf in range(2):
                        bjobs.append(half_mov(w2_v[kg], fb * 512, w2_s[fb * 4 + kg, :, :], hf))
            bj = {"next": 0, "bufs": None}

            def run_bjobs(k):
                for _ in range(k):
                    if bj.get("pending") is not None:
                        bj["pending"][1](*bj["bufs"])
                        bj["pending"] = None
                    if bj["next"] < len(bjobs):
                        bjobs[bj["next"]][0](*bj["bufs"])
                        bj["pending"] = bjobs[bj["next"]]
                        bj["next"] += 1

            gbc = sb(st, "gbc", [128, D], F32)
            b_gbc = em.buf("gbc")
            em.dma(gbc[:], gains_d[0].partition_broadcast(128), writes=[b_gbc])
            xt = [sb(st, f"xt{i}", [128, D], F32) for i in range(3)]
            b_xt = [em.buf(f"xt{i}") for i in range(3)]
            junk = sb(st, "junk", [128, D], BF16)
            b_junk = em.buf("junk")
            stt = sb(st, "stt", [128, 8], F32)
            b_stt = [em.buf(f"stt{i}") for i in range(4)]
            ub = [sb(st, f"ub{i}", [128, D], BF16) for i in range(2)]
            b_ub = [em.buf(f"ub{i}") for i in range(2)]
            tp = [ps(st, f"tp{i}", [128, D], BF16) for i in range(2)]
            b_tp = [em.buf(f"ps_tp{i}") for i in range(2)]
            uTc = [sb(st, f"uTc{i}", [128, 16 * CH], BF16) for i in range(2)]
            b_uTc = [em.buf(f"uTc{i}") for i in range(2)]
            b_uTd = em.buf("uTd")
            for i in range(T // 128):
                c, j = divmod(i, 4)
                X, bX = xt[i % 3], b_xt[i % 3]
                k = i % 4
                ssa, rsa, bS = stt[:, 2 * k:2 * k + 1], stt[:, 2 * k + 1:2 * k + 2], b_stt[k]
                U, bU = ub[i % 2], b_ub[i % 2]
                P, bP = tp[i % 2], b_tp[i % 2]
                UT, bUT = uTc[c % 2], b_uTc[c % 2]
                em.dma(X[:], x_d[i * 128:(i + 1) * 128, :], writes=[bX])
                em.op("act", lambda h, X=X, ssa=ssa: h.activation(out=junk[:], in_=X[:], func=AF.Square, accum_out=ssa),
                      reads=[bX], writes=[b_junk, bS])
                em.op("dve", lambda h, ssa=ssa, rsa=rsa: h.tensor_scalar(out=rsa, in0=ssa, scalar1=1.0 / D, scalar2=EPS,
                                                                       op0=ALU.mult, op1=ALU.add), reads=[bS], writes=[bS])
                em.op("act", lambda h, rsa=rsa: h.activation(out=rsa, in_=rsa, func=AF.Sqrt), reads=[bS], writes=[bS])
                em.op("dve", lambda h, rsa=rsa: h.reciprocal(out=rsa, in_=rsa), reads=[bS], writes=[bS])
                em.op("dve", lambda h, X=X, U=U, rsa=rsa: h.scalar_tensor_tensor(out=U[:], in0=X[:], scalar=rsa, in1=gbc[:],
                                                                                op0=ALU.mult, op1=ALU.mult),
                      reads=[bX, bS, b_gbc], writes=[bU])
                em.mm_group([lambda h, kc=kc, P=P, U=U: h.transpose(out=P[:, kc * 128:(kc + 1) * 128],
                                                                   in_=U[:, kc * 128:(kc + 1) * 128], identity=ident[:])
                             for kc in range(16)], reads=[bU, b_ident], writes=[bP])
                em.op("act", lambda h, P=P, UT=UT, j=j: h.activation(
                    out=UT[:, :].rearrange("p (kc t) -> p kc t", t=CH)[:, :, j * 128:(j + 1) * 128],
                    in_=P[:, :].rearrange("p (kc t) -> p kc t", t=128), func=AF.Copy), reads=[bP], writes=[bUT])
                if j == 3:
                    em.dma(uT_d[c, :, :], UT[:, :], reads=[bUT], writes=[b_uTd], sembuf=bUT, q=STQ)
            if debug and "uT" in dbg:
                em.dma(dbg["uT"][:, :, :], uT_d[:, :, :], reads=[b_uTd], sembuf=b_uTd)
            def finish_bjob():
                if bj.get("pending") is not None:
                    bj["pending"][1](*bj["bufs"])
                    bj["pending"] = None

            em.flush(final_waits=[b_wA, b_uTd])

        UNITS = [(0, 8, True), (8, 4, False)]
        DILS = (1, 4, 16)
        SCALE = 128.0 ** -0.5
        alt = {"n": 0}

        def evac_eng():
            alt["n"] += 1
            return "act" if alt["n"] % 2 == 0 else "dve"

        def copy_op(eng, out, in_):
            if eng == "act":
                return lambda h: h.activation(out=out, in_=in_, func=AF.Copy)
            return lambda h: h.tensor_copy(out=out, in_=in_)

        with contextlib.ExitStack() as st:
            BTs = [sb(st, f"BT{i}", [128, 3 * 256], F32) for i in range(2)]
            b_BTs = [em.buf(f"BT{i}") for i in range(2)]
            stgh = sb(st, "stgh1", [128, 4096], F32)
            slbh = sb(st, "slbh1", [128, 4096], BF16)
            bj["bufs"] = (stgh, em.buf("stgh1"), slbh, em.buf("slbh1"))
            wsl = [sb(st, f"wsl{i}", [128, 6144], BF16) for i in range(1)]
            b_wsl = [em.buf(f"wsl{i}") for i in range(1)]
            uTs = [sb(st, f"uTs{i}", [128, 8192], BF16) for i in range(2)]
            b_uTs = [em.buf(f"uTs{i}") for i in range(2)]
            qkv_sets = [[sb(st, f"qkv{z}_{i}", [128, 4096], BF16) for i in range(3)] for z in range(2)]
            b_qkv_sets = [[em.buf(f"qkv{z}_{i}") for i in range(3)] for z in range(2)]
            vtok_sets = [sb(st, f"vtok{z}", [128, 4096], BF16) for z in range(2)]
            b_vtok_sets = [em.buf(f"vtok{z}") for z in range(2)]
            acc = [sb(st, f"acc{i}", [128, 4096], F32) for i in range(2)]
            b_acc = [em.buf(f"acc{i}") for i in range(2)]
            PT = [sb(st, f"PT{i}", [128, 512], BF16) for i in range(4)]
            b_PT = [em.buf(f"PT{i}") for i in range(4)]
            tmpS = [sb(st, f"tmpS{i}", [128, 512], F32) for i in range(4)]
            b_tmpS = [em.buf(f"tmpS{i}") for i in range(4)]
            attnT = sb(st, "attnT", [128, 4096], BF16)
            b_attnT = em.buf("attnT")
            Wb = [ps(st, f"Wb{i}", [128, 512], F32) for i in range(4)]
            b_Wb = [em.buf(f"ps_Wb{i}") for i in range(4)]
            Ob = [ps(st, f"Ob{i}", [128, 512], F32) for i in range(2)]
            b_Ob = [em.buf(f"ps_Ob{i}") for i in range(2)]
            Lb = [ps(st, f"Lb{i}", [128, 512], F32) for i in range(2)]
            b_Lb = [em.buf(f"ps_Lb{i}") for i in range(2)]
            VTb = Wb[1][:, :].bitcast(BF16)
            b_VTb = b_Wb[1]
            b_art = em.buf("art")
            cnt = {"w": 0, "u": 0, "W": 0, "pt": 0, "ts": 0, "reg": 0, "ip": 0, "zq": 0}

            def a1_ctx(c0, nch, joint, hs, g, zq_i):
                zq = zq_i % 2
                d = DILS[g]
                ulen = nch * CH
                head = g * 4 + hs
                Lj = ulen // d
                par = (zq_i // 3) % 2
                return dict(BT=BTs[par], b_BT=b_BTs[par], c0=c0, nch=nch, joint=joint, hs=hs, g=g, d=d, ulen=ulen, head=head, Lj=Lj,
                            qkv=qkv_sets[zq], b_qkv=b_qkv_sets[zq], vtok=vtok_sets[zq], b_vtok=b_vtok_sets[zq], nkb=ulen // 128)

            def gen_inproj(cx):
                c0, nch, joint, hs, g, d, ulen, head, Lj = (cx[k] for k in ('c0', 'nch', 'joint', 'hs', 'g', 'd', 'ulen', 'head', 'Lj'))
                qkv, b_qkv, vtok, b_vtok, nkb = cx['qkv'], cx['b_qkv'], cx['vtok'], cx['b_vtok'], cx['nkb']
                WS, bWS = wsl[0], b_wsl[0]
                em.dma(WS[:, :], wA_d[hs * 3 + g, :, :], reads=[b_wA], writes=[bWS])
                if g == 0:
                    for g2 in range(3):
                        em.dma(cx['BT'][:, g2 * 256:(g2 + 1) * 256], bt_d[:, (g2 * 4 + hs) * 256:(g2 * 4 + hs + 1) * 256], writes=[cx['b_BT']])
                for cu in range(nch):
                    ui = cnt["u"] % 2
                    cnt["u"] += 1
                    UT, bUT = uTs[ui], b_uTs[ui]
                    em.dma(UT[:, :], uT_d[c0 + cu, :, :], reads=[b_uTd], writes=[bUT])
                    for blk in range(3):
                        bi = cnt["ip"] % 2
                        cnt["ip"] += 1
                        bank, bB = Wb[bi], b_Wb[bi]
                        em.mm_group([lambda h, kc=kc, bank=bank, WS=WS, UT=UT, blk=blk: h.matmul(
                            out=bank[:, :], lhsT=WS[:, blk * 2048 + kc * 128: blk * 2048 + (kc + 1) * 128],
                            rhs=UT[:, kc * 512:(kc + 1) * 512], start=(kc == 0), stop=(kc == 15))
                            for kc in range(16)], reads=[bWS, bUT], writes=[bB])
                        dst = qkv[blk][:, 0:ulen].rearrange("p (r l) -> p r l", l=Lj)[:, :, cu * (512 // d):(cu + 1) * (512 // d)]
                        src = bank[:, :].rearrange("p (l r) -> p r l", r=d)
                        e = evac_eng()
                        em.op(e, copy_op(e, dst, src), reads=[bB], writes=[b_qkv[blk]])
                    yield
                nkb = ulen // 128
                for b0 in range(0, nkb, 8):
                    em.mm_group([lambda h, b=b, b0=b0, Vt=qkv[2]: h.transpose(out=VTb[:, (b - b0) * 128:(b - b0 + 1) * 128],
                                                                 in_=Vt[:, b * 128:(b + 1) * 128], identity=ident[:])
                                 for b in range(b0, b0 + 8)], reads=[b_qkv[2], b_ident], writes=[b_VTb])
                    e = evac_eng()
                    em.op(e, copy_op(e, vtok[:, b0 * 128:(b0 + 8) * 128], VTb[:, :]), reads=[b_VTb], writes=[b_vtok])
                yield

            def gen_attn(cx):
                c0, nch, joint, hs, g, d, ulen, head, Lj = (cx[k] for k in ('c0', 'nch', 'joint', 'hs', 'g', 'd', 'ulen', 'head', 'Lj'))
                qkv, b_qkv, vtok, b_vtok, nkb = cx['qkv'], cx['b_qkv'], cx['vtok'], cx['b_vtok'], cx['nkb']
                Lseg = Lj // 2 if joint else Lj
                for R in range(ulen // 512):
                    ri = cnt["reg"] % 2
                    cnt["reg"] += 1
                    OB, bO, LB, bL = Ob[ri], b_Ob[ri], Lb[ri], b_Lb[ri]
                    pieces = []
                    for b in range(nkb):
                        r = (128 * b) // Lj
                        lb = 128 * b - r * Lj
                        qlo, qhi = max(0, lb - 64), min(Lj, lb + 192)
                        cuts = [qlo, qhi]
                        if joint and qlo < Lseg < qhi:
                            cuts = [qlo, Lseg, qhi]
                        for a_, b_ in zip(cuts[:-1], cuts[1:]):
                            fa, fb_ = r * Lj + a_, r * Lj + b_
                            fa2, fb2 = max(fa, 512 * R), min(fb_, 512 * R + 512)
                            if fa2 >= fb2:
                                continue
                            cross = joint and ((a_ >= Lseg) != (lb >= Lseg))
                            pieces.append((b, fa2, fb2, (fa2 - r * Lj) - (lb - 64), cross))
                    LOOK = 2
                    held = {}

                    def emit_S(pi):
                        (b, fa, fb_, j0, cross) = pieces[pi]
                        n = fb_ - fa
                        bi = cnt["W"] % 4
                        cnt["W"] += 1
                        bank, bB = Wb[bi], b_Wb[bi]
                        em.mm_group([lambda h, bank=bank, b=b, fa=fa, fb_=fb_, n=n, Kt=qkv[1], Qt=qkv[0]: h.matmul(
                            out=bank[:, 0:n], lhsT=Kt[:, b * 128:(b + 1) * 128], rhs=Qt[:, fa:fb_],
                            start=True, stop=True)], reads=[b_qkv[0], b_qkv[1]], writes=[bB])
                        ti = cnt["ts"] % 4
                        cnt["ts"] += 1
                        TS, bTS = tmpS[ti], b_tmpS[ti]
                        em.op("dve", lambda h, TS=TS, bank=bank, n=n, j0=j0, g=g, BT=cx['BT']: h.scalar_tensor_tensor(
                            out=TS[:, 0:n], in0=bank[:, 0:n], scalar=SCALE, in1=BT[:, g * 256 + j0: g * 256 + j0 + n],
                            op0=ALU.mult, op1=ALU.add), reads=[bB, cx['b_BT']], writes=[bTS])
                        pti = cnt["pt"] % 4
                        cnt["pt"] += 1
                        P_, bP_ = PT[pti], b_PT[pti]
                        if cross:
                            em.op("act", lambda h, P_=P_, TS=TS, n=n: h.activation(out=P_[:, 0:n], in_=TS[:, 0:n], func=AF.Exp,
                                                                                  bias=fl[:, 1:2]), reads=[bTS, b_fl], writes=[bP_])
                        else:
                            em.op("act", lambda h, P_=P_, TS=TS, n=n: h.activation(out=P_[:, 0:n], in_=TS[:, 0:n], func=AF.Exp),
                                  reads=[bTS], writes=[bP_])
                        held[pi] = (P_, bP_)

                    def emit_PV(pi):
                        (b, fa, fb_, j0, cross) = pieces[pi]
                        n = fb_ - fa
                        P_, bP_ = held.pop(pi)
                        o0 = fa - 512 * R
                        first, last = (pi == 0), (pi == len(pieces) - 1)
                        em.mm_group([
                            lambda h, OB=OB, b=b, P_=P_, n=n, o0=o0, first=first, last=last, vtok=vtok: h.matmul(
                                out=OB[:, o0:o0 + n], lhsT=vtok[:, b * 128:(b + 1) * 128], rhs=P_[:, 0:n],
                                start=first, stop=last, skip_group_check=True),
                            lambda h, LB=LB, P_=P_, n=n, o0=o0, first=first, last=last: h.matmul(
                                out=LB[:, o0:o0 + n], lhsT=ones[:, :], rhs=P_[:, 0:n],
                                start=first, stop=last, skip_group_check=True)],
                            reads=[b_vtok, bP_, b_ones], writes=[bO, bL])

                    for pi in range(len(pieces) + LOOK):
                        if pi < len(pieces):
                            emit_S(pi)
                        if pi - LOOK >= 0:
                            emit_PV(pi - LOOK)
                    for (A, bA, BK, bBK) in ((acc[0], b_acc[0], OB, bO), (acc[1], b_acc[1], LB, bL)):
                        Av = A[:, 0:ulen].rearrange("p (l r) -> p r l", r=d)
                        f = 512 * R
                        while f < 512 * R + 512:
                            r = f // Lj
                            l0 = f - r * Lj
                            n = min(Lj - l0, 512 * R + 512 - f)
                            dst = Av[:, r, l0:l0 + n]
                            src = BK[:, f - 512 * R: f - 512 * R + n]
                            if g == 0:
                                e = evac_eng()
                                em.op(e, copy_op(e, dst, src), reads=[bBK], writes=[bA])
                            else:
                                em.op("dve", lambda h, dst=dst, src=src: h.tensor_tensor(out=dst, in0=dst, in1=src, op=ALU.add),
                                      reads=[bBK, bA], writes=[bA])
                            f += n
                    yield
                if g == 2:
                    em.op("dve", lambda h, ulen=ulen: h.reciprocal(out=acc[1][:, 0:ulen], in_=acc[1][:, 0:ulen]),
                          reads=[b_acc[1]], writes=[b_acc[1]])
                    em.op("dve", lambda h, ulen=ulen: h.tensor_tensor(out=attnT[:, 0:ulen], in0=acc[0][:, 0:ulen], in1=acc[1][:, 0:ulen],
                                                                     op=ALU.mult), reads=[b_acc[0], b_acc[1]], writes=[b_attnT])
                    em.dma(art_d[c0:c0 + nch, :, :].rearrange("c p (f t) -> p c f t", t=512)[:, :, hs, :],
                           attnT[:, 0:ulen].rearrange("p (c t) -> p c t", t=512), reads=[b_attnT], writes=[b_art], sembuf=b_attnT, q=STQ)
                yield

            passes = []
            for (c0, nch, joint) in (UNITS if not OPTS.get('skip_a1') else []):
                for hs in range(4):
                    for g in range(3):
                        passes.append(a1_ctx(c0, nch, joint, hs, g, len(passes)))
            if passes:
                for _ in gen_inproj(passes[0]):
                    pass
            for i, cx in enumerate(passes):
                ga = gen_attn(cx)
                gi = gen_inproj(passes[i + 1]) if i + 1 < len(passes) else iter(())
                da = di = False
                while not (da and di):
                    if not da:
                        try:
                            next(ga)
                            bj["tick"] = bj.get("tick", 0) + 1
                            if bj["tick"] % OPTS.get("bj_every1", 2) == 0:
                                run_bjobs(1)
                        except StopIteration:
                            da = True
                    if not di:
                        try:
                            next(gi)
                        except StopIteration:
                            di = True
            finish_bjob()
            em.flush(final_waits=[b_art])

        with contextlib.ExitStack() as st:
            cf = sb(st, "cf", [128, 24], F32)
            b_cf = em.buf("cf")
            em.op("act", lambda h: h.activation(out=cf[:, :], in_=pc[:, 108:132], func=AF.Exp, scale=-1.0), reads=[b_pc], writes=[b_cf])
            em.op("act", lambda h: h.activation(out=cf[:, :], in_=cf[:, :], func=AF.Ln, bias=1.0), reads=[b_cf], writes=[b_cf])
            em.op("dve", lambda h: h.tensor_scalar(out=cf[:, :], in0=cf[:, :], scalar1=-8.0, scalar2=None, op0=ALU.mult),
                  reads=[b_cf], writes=[b_cf])
            stgh2 = sb(st, "stgh2", [128, 4096], F32)
            slbh2 = sb(st, "slbh2", [128, 4096], BF16)
            bj["bufs"] = (stgh2, em.buf("stgh2"), slbh2, em.buf("slbh2"))
            carry = sb(st, "carry", [128, 2], F32)
            b_carry = em.buf("carry")
            dg = sb(st, "dg", [128, 512], BF16)
            b_dg = em.buf("dg")
            NCG = OPTS.get("ncg", 1)
            wsl = [sb(st, f"wl{i}", [128, 4096], BF16) for i in range(NCG)]
            b_wsl = [em.buf(f"wl{i}") for i in range(NCG)]
            uTs = [sb(st, f"uTl{i}", [128, 8192], BF16) for i in range(2)]
            b_uTs = [em.buf(f"uTl{i}") for i in range(2)]
            rxs_sets = [[[sb(st, f"rxs{z}_{k}_{i}", [128, 2052], BF16) for i in range(2)] for k in range(NCG)] for z in range(2)]
            b_rxs_sets = [[[em.buf(f"rxs{z}_{k}_{i}") for i in range(2)] for k in range(NCG)] for z in range(2)]
            gg_sets = [[sb(st, f"gg{z}_{k}", [128, 4096], BF16) for k in range(NCG)] for z in range(2)]
            b_gg_sets = [[em.buf(f"gg{z}_{k}") for k in range(NCG)] for z in range(2)]
            xc = [sb(st, f"xc{i}", [128, 2048], F32) for i in range(2)]
            b_xc = [em.buf(f"xc{i}") for i in range(2)]
            xcb = [sb(st, f"xcb{i}", [128, 2048], BF16) for i in range(2)]
            b_xcb = [em.buf(f"xcb{i}") for i in range(2)]
            hf = [sb(st, f"hf{i}", [128, 2048], F32) for i in range(2)]
            b_hf = [em.buf(f"hf{i}") for i in range(2)]
            hb = sb(st, "hb", [128, 2048], F32)
            b_hb = em.buf("hb")
            NSET = OPTS.get("nset", 2)
            ra_l = [sb(st, f"ra{i}", [128, 2048], F32) for i in range(NSET)]
            b_ra_l = [em.buf(f"ra{i}") for i in range(NSET)]
            iu_l = [sb(st, f"iu{i}", [128, 2048], F32) for i in range(NSET)]
            b_iu_l = [em.buf(f"iu{i}") for i in range(NSET)]
            tt_l = [sb(st, f"tt{i}", [128, 2048], F32) for i in range(NSET)]
            b_tt_l = [em.buf(f"tt{i}") for i in range(NSET)]
            Wb = [ps(st, f"Wl{i}", [128, 512], F32) for i in range(2)]
            b_Wb = [em.buf(f"ps_Wl{i}") for i in range(2)]
            Cb = [ps(st, f"Cb{i}", [128, 512], F32) for i in range(2)]
            b_Cb = [em.buf(f"ps_Cb{i}") for i in range(2)]
            Gb = [ps(st, f"Gb{i}", [128, 512], F32) for i in range(2)]
            b_Gb = [em.buf(f"ps_Gb{i}") for i in range(2)]
            cnt = {"w": 0, "u": 0, "W": 0, "C": 0, "G": 0, "it": 0, "z": 0}

            def gates_and_elem(dr, s, c):
                si = cnt["it"] % NSET
                cnt["it"] += 1
                ra, b_ra, iu, b_iu, tt, b_tt = ra_l[si], b_ra_l[si], iu_l[si], b_iu_l[si], tt_l[si], b_tt_l[si]
                for j in range(4):
                    for which, dstT, bD, bcol in ((0, ra, b_ra, 60), (1, iu, b_iu, 84)):
                        gi = cnt["G"] % 2
                        cnt["G"] += 1
                        bank, bB = Gb[gi], b_Gb[gi]
                        w0 = (which * 24 + dr * 12 + c) * 128
                        em.mm_group([lambda h, bank=bank, w0=w0, s=s, j=j: h.matmul(
                            out=bank[:, :], lhsT=lw[:, w0:w0 + 128], rhs=xcb[s][:, j * 512:(j + 1) * 512], start=True, stop=True)],
                            reads=[b_lw, b_xcb[s]], writes=[bB])
                        col = bcol + dr * 12 + c
                        em.op("act", lambda h, bank=bank, dstT=dstT, j=j, col=col: h.activation(
                            out=dstT[:, j * 512:(j + 1) * 512], in_=bank[:, :], func=AF.Sigmoid, bias=pc[:, col:col + 1]),
                            reads=[bB, b_pc], writes=[bD])
                ccol = dr * 12 + c
                em.op("act", lambda h, ccol=ccol, ra=ra: h.activation(out=ra[:, :], in_=ra[:, :], func=AF.Exp, scale=cf[:, ccol:ccol + 1]),
                      reads=[b_ra, b_cf], writes=[b_ra])
                em.op("act", lambda h, ra=ra, tt=tt: h.activation(out=tt[:, :], in_=ra[:, :], func=AF.Square), reads=[b_ra], writes=[b_tt])
                em.op("act", lambda h, tt=tt: h.activation(out=tt[:, :], in_=tt[:, :], func=AF.Sqrt, scale=-1.0, bias=1.0),
                      reads=[b_tt], writes=[b_tt])
                em.op("dve", lambda h, s=s, iu=iu: h.tensor_tensor(out=iu[:, :], in0=iu[:, :], in1=xc[s][:, :], op=ALU.mult),
                      reads=[b_iu, b_xc[s]], writes=[b_iu])
                em.op("dve", lambda h, iu=iu, tt=tt: h.tensor_tensor(out=iu[:, :], in0=iu[:, :], in1=tt[:, :], op=ALU.mult),
                      reads=[b_iu, b_tt], writes=[b_iu])
                return ra, b_ra, iu, b_iu

            def gen_inproj2(c0, nch, joint, chunks, rxs_all, b_rxs_all, gg_all, b_gg_all):
                for k, c in enumerate(chunks):
                    em.dma(wsl[k][:, :], wL_d[c, :, :], reads=[b_wA], writes=[b_wsl[k]])
                for cu in range(nch):
                    ui = cnt["u"] % 2
                    cnt["u"] += 1
                    UT, bUT = uTs[ui], b_uTs[ui]
                    em.dma(UT[:, :], uT_d[c0 + cu, :, :], reads=[b_uTd], writes=[bUT])
                    s, j = divmod(cu, 4)
                    for k, c in enumerate(chunks):
                        WS, bWS = wsl[k], b_wsl[k]
                        for blk in range(2):
                            bi = cnt["W"] % 2
                            cnt["W"] += 1
                            bank, bB = Wb[bi], b_Wb[bi]
                            em.mm_group([lambda h, kc=kc, bank=bank, WS=WS, UT=UT, blk=blk: h.matmul(
                                out=bank[:, :], lhsT=WS[:, blk * 2048 + kc * 128: blk * 2048 + (kc + 1) * 128],
                                rhs=UT[:, kc * 512:(kc + 1) * 512], start=(kc == 0), stop=(kc == 15))
                                for kc in range(16)], reads=[bWS, bUT], writes=[bB])
                            if blk == 0:
                                em.op("dve", lambda h, bank=bank, j=j, dstR=rxs_all[k][s]: h.tensor_copy(
                                    out=dstR[:, 2 + j * 512: 2 + (j + 1) * 512], in_=bank[:, :]), reads=[bB], writes=[b_rxs_all[k][s]])
                            else:
                                em.op("act", lambda h, bank=bank, cu=cu, dstG=gg_all[k]: h.activation(
                                    out=dstG[:, cu * 512:(cu + 1) * 512], in_=bank[:, :], func=AF.Gelu_apprx_tanh), reads=[bB], writes=[b_gg_all[k]])
                    yield


            def lru_chunk(c, rxs, b_rxs, gg, b_gg, c0, nch, joint, nseg):
                outb, b_outb = gg, b_gg
                for t_ in range(4):
                    em.op("dve", lambda h, t_=t_, c=c: h.tensor_scalar(out=dg[:, t_ * 128:(t_ + 1) * 128], in0=ident[:, :],
                                                                        scalar1=pc[:, t_ * 12 + c: t_ * 12 + c + 1], scalar2=None,
                                                                        op0=ALU.mult), reads=[b_ident, b_pc], writes=[b_dg])
                if OPTS.get("a2_stage", 9) < 1:
                    return
                em.op("dve", lambda h: h.memset(rxs[0][:, 0:2], 0.0), writes=[b_rxs[0]])
                em.op("dve", lambda h, nseg=nseg: h.memset(rxs[nseg - 1][:, 2050:2052], 0.0), writes=[b_rxs[nseg - 1]])
                if joint:
                    em.op("dve", lambda h: h.tensor_scalar(out=rxs[0][:, 2050:2051], in0=rxs[1][:, 2:3], scalar1=fl[:, 0:1],
                                                           scalar2=None, op0=ALU.mult), reads=[b_rxs[1], b_fl], writes=[b_rxs[0]])
                    em.op("dve", lambda h: h.tensor_scalar(out=rxs[1][:, 0:2], in0=rxs[0][:, 2048:2050], scalar1=fl[:, 0:1],
                                                           scalar2=None, op0=ALU.mult), reads=[b_rxs[0], b_fl], writes=[b_rxs[1]])
                if OPTS.get("a2_stage", 9) < 2:
                    return
                for s in range(nseg):
                    for j in range(4):
                        ci = cnt["C"] % 2
                        cnt["C"] += 1
                        bank, bB = Cb[ci], b_Cb[ci]
                        em.mm_group([lambda h, bank=bank, t_=t_, s=s, j=j: h.matmul(
                            out=bank[:, :], lhsT=dg[:, t_ * 128:(t_ + 1) * 128], rhs=rxs[s][:, j * 512 + t_: j * 512 + t_ + 512],
                            start=(t_ == 0), stop=(t_ == 3)) for t_ in range(4)], reads=[b_dg, b_rxs[s]], writes=[bB])
                        em.op("act", lambda h, bank=bank, s=s, j=j, c=c: h.activation(
                            out=xc[s][:, j * 512:(j + 1) * 512], in_=bank[:, :], func=AF.Identity, bias=pc[:, 48 + c:49 + c]),
                            reads=[bB, b_pc], writes=[b_xc[s]])
                        em.op("dve", lambda h, s=s, j=j: h.tensor_copy(
                            out=xcb[s][:, j * 512:(j + 1) * 512], in_=xc[s][:, j * 512:(j + 1) * 512]),
                            reads=[b_xc[s]], writes=[b_xcb[s]])
                yield
                if OPTS.get("a2_stage", 9) < 3:
                    return
                for s in range(nseg):
                    ra, b_ra, iu, b_iu = gates_and_elem(0, s, c)
                    init = 0.0 if s == 0 else carry[:, 0:1]
                    em.op("dve", lambda h, s=s, init=init, ra=ra, iu=iu: h.tensor_tensor_scan(out=hf[s][:, :], data0=ra[:, :], data1=iu[:, :],
                                                                              initial=init, op0=ALU.mult, op1=ALU.add),
                          reads=[b_ra, b_iu, b_carry], writes=[b_hf[s]])
                    if joint and s == 0:
                        em.op("dve", lambda h: h.tensor_scalar(out=carry[:, 0:1], in0=hf[0][:, 2047:2048], scalar1=fl[:, 0:1],
                                                               scalar2=None, op0=ALU.mult), reads=[b_hf[0], b_fl], writes=[b_carry])
                    yield
                if OPTS.get("a2_stage", 9) < 4:
                    return
                for s in reversed(range(nseg)):
                    ra, b_ra, iu, b_iu = gates_and_elem(1, s, c)
                    init = 0.0 if s == nseg - 1 else carry[:, 1:2]
                    em.op("dve", lambda h, init=init, ra=ra, iu=iu: h.tensor_tensor_scan(out=hb[:, ::-1], data0=ra[:, ::-1], data1=iu[:, ::-1],
                                                                         initial=init, op0=ALU.mult, op1=ALU.add),
                          reads=[b_ra, b_iu, b_carry], writes=[b_hb])
                    if joint and s == 1:
                        em.op("dve", lambda h: h.tensor_scalar(out=carry[:, 1:2], in0=hb[:, 0:1], scalar1=fl[:, 0:1],
                                                               scalar2=None, op0=ALU.mult), reads=[b_hb, b_fl], writes=[b_carry])
                    em.op("dve", lambda h, s=s: h.tensor_tensor(out=hb[:, :], in0=hb[:, :], in1=hf[s][:, :], op=ALU.add),
                          reads=[b_hb, b_hf[s]], writes=[b_hb])
                    em.op("dve", lambda h, s=s: h.tensor_tensor(out=outb[:, s * 2048:(s + 1) * 2048], in0=hb[:, :],
                                                               in1=gg[:, s * 2048:(s + 1) * 2048], op=ALU.mult),
                          reads=[b_hb, b_gg], writes=[b_outb])
                    yield
                em.dma(art_d[c0:c0 + nch, :, :].rearrange("c p (f t) -> p c f t", t=512)[:, :, 4 + c, :],
                       outb[:, 0:nch * 512].rearrange("p (c t) -> p c t", t=512), reads=[b_outb], writes=[b_art], sembuf=b_outb, q=STQ)


            groups = []
            for (c0, nch, joint) in OPTS.get("a2_units", UNITS):
                for cg in range(OPTS.get("a2_chunks", 12) // NCG):
                    z = len(groups) % 2
                    groups.append((c0, nch, joint, [cg * NCG + k for k in range(NCG)], rxs_sets[z], b_rxs_sets[z], gg_sets[z], b_gg_sets[z]))

            def gen_lru_group(grp):
                (c0, nch, joint, chunks, rxs_all, b_rxs_all, gg_all, b_gg_all) = grp
                for k, c in enumerate(chunks):
                    yield from lru_chunk(c, rxs_all[k], b_rxs_all[k], gg_all[k], b_gg_all[k], c0, nch, joint, nch // 4)

            if groups:
                for _ in gen_inproj2(*groups[0]):
                    pass
            for i, grp in enumerate(groups):
                ga = gen_lru_group(grp)
                gi = gen_inproj2(*groups[i + 1]) if i + 1 < len(groups) else iter(())
                da = di = False
                while not (da and di):
                    if not da:
                        try:
                            next(ga)
                            bj["tick"] = bj.get("tick", 0) + 1
                            if bj["tick"] % OPTS.get("bj_every2", 3) == 0:
                                run_bjobs(1)
                        except StopIteration:
                            da = True
                    if not di:
                        try:
                            next(gi)
                        except StopIteration:
                            di = True
            run_bjobs(len(bjobs) + 1)
            finish_bjob()
            if debug and "art" in dbg:
                em.dma(dbg["art"][:, :, :], art_d[:, :, :], reads=[b_art], sembuf=b_art)
            em.flush(final_waits=[b_art, b_wd])

        with contextlib.ExitStack() as st:
            gfin = sb(st, "gfin", [128, D], F32)
            b_gfin = em.buf("gfin")
            em.dma(gfin[:], gains_d[2].partition_broadcast(128), writes=[b_gfin])
            gmT = sb(st, "gmT", [128, 16], F32)
            b_gmT = em.buf("gmT")
            em.dma(gmT[:], gT_d[:, :], writes=[b_gmT])
            NS = 4
            ring = [sb(st, f"ring{i}", [128, 8192], BF16) for i in range(NS)]
            b_ring = [em.buf(f"ring{i}") for i in range(NS)]
            AU = [sb(st, f"AU{i}", [128, 8192], BF16) for i in range(2)]
            b_AU = [em.buf(f"AU{i}") for i in range(2)]
            arT = sb(st, "arT", [128, 8192], BF16)
            b_arT = em.buf("arT")
            MH = sb(st, "MH", [128, 8192], BF16)
            b_MH = em.buf("MH")
            x1 = sb(st, "x1", [128, 4 * D], F32)
            b_x1 = [em.buf(f"x1_{i}") for i in range(4)]
            u2 = [sb(st, f"u2_{i}", [128, D], BF16) for i in range(2)]
            b_u2 = [em.buf(f"u2_{i}") for i in range(2)]
            sg = [sb(st, f"sg{i}", [128, 512], BF16) for i in range(4)]
            b_sg = [em.buf(f"sg{i}") for i in range(4)]
            tm = [sb(st, f"tm{i}", [128, 512], F32) for i in range(4)]
            b_tm = [em.buf(f"tm{i}") for i in range(4)]
            stt2 = sb(st, "stt2", [128, 8], F32)
            b_stt2 = [em.buf(f"stt2_{i}") for i in range(4)]
            PB = [ps(st, f"PB{i}", [128, 512], F32) for i in range(8)]
            b_PB = [em.buf(f"ps_PB{i}") for i in range(8)]
            b_y = em.buf("y")

            seq = []
            for ci in range(NCH):
                for jb in range(4):
                    seq.append([(0, 8192, wG_d[jb, :, :])])
                    seq.append([(0, 8192, wG_d[4 + jb, :, :])])
                    seq.append([(0, 6144, wRO_d[jb, :, :]), (6144, 8192, wAO_d[0, :, jb * 2048:(jb + 1) * 2048])])
                for fb in range(4):
                    seq.append([(0, 8192, wO_d[fb, :, :])])
                for qd in range(4):
                    for jj in range(4):
                        seq.append([(0, 8192, w1_s[qd * 4 + jj, :, :])])
                    for fb in range(4):
                        seq.append([(0, 8192, w2_s[fb * 4 + qd, :, :])])
            if OPTS.get("b_chunks") is not None:
                seq = seq[:48 * OPTS["b_chunks"]]
            ss_ = {"issued": 0, "next": 0}

            def slab_issue_upto(n):
                while ss_["issued"] < min(n, len(seq)):
                    k = ss_["issued"]
                    for lo, hi, src in seq[k]:
                        em.dma(ring[k % NS][:, lo:hi], src, reads=[b_wd], writes=[b_ring[k % NS]])
                    ss_["issued"] += 1

            def slab_next():
                k = ss_["next"]
                ss_["next"] += 1
                slab_issue_upto(k + 1)
                return ring[k % NS], b_ring[k % NS]

            def slab_prefetch():
                slab_issue_upto(ss_["next"] + NS)

            def rms_stats(src_ap, k, rb):
                ssa, rsa, bS = stt2[:, 2 * k:2 * k + 1], stt2[:, 2 * k + 1:2 * k + 2], b_stt2[k]
                jk = u2[k % 2]
                em.op("act", lambda h: h.activation(out=jk[:, :], in_=src_ap, func=AF.Square, accum_out=ssa),
                      reads=[rb], writes=[b_u2[k % 2], bS])
                em.op("dve", lambda h: h.tensor_scalar(out=rsa, in0=ssa, scalar1=1.0 / D, scalar2=EPS, op0=ALU.mult, op1=ALU.add),
                      reads=[bS], writes=[bS])
                em.op("act", lambda h: h.activation(out=rsa, in_=rsa, func=AF.Sqrt), reads=[bS], writes=[bS])
                em.op("dve", lambda h: h.reciprocal(out=rsa, in_=rsa), reads=[bS], writes=[bS])
                return rsa, bS

            nchunks = OPTS.get("b_chunks", NCH)
            em.dma(AU[0][:, :], uT_d[0, :, :], reads=[b_uTd], writes=[b_AU[0]])
            em.dma(arT[:, :], art_d[0, :, :], reads=[b_art], writes=[b_arT])
            for ci in range(nchunks):
                UT, bUT = AU[ci % 2], b_AU[ci % 2]
                em.dma(x1[:, :].rearrange("p (t d) -> p t d", d=D), x_d[ci * 512:(ci + 1) * 512, :].rearrange("(t p) d -> p t d", p=128),
                       writes=b_x1)
                for jb in range(4):
                    GA, bGA = slab_next()
                    GR, bGR = slab_next()
                    RO, bRO = slab_next()
                    for cb in range(4):
                        j = jb * 4 + cb
                        pb = 4 * (j % 2)
                        em.mm_group([lambda h, kc=kc, pb=pb, cb=cb, GA=GA, UT=UT: h.matmul(
                            out=PB[pb][:, :], lhsT=GA[:, cb * 2048 + kc * 128: cb * 2048 + (kc + 1) * 128],
                            rhs=UT[:, kc * 512:(kc + 1) * 512], start=(kc == 0), stop=(kc == 15)) for kc in range(16)],
                            reads=[bGA, bUT], writes=[b_PB[pb]])
                        em.mm_group([lambda h, kc=kc, pb=pb, cb=cb, GR=GR, UT=UT: h.matmul(
                            out=PB[pb + 1][:, :], lhsT=GR[:, cb * 2048 + kc * 128: cb * 2048 + (kc + 1) * 128],
                            rhs=UT[:, kc * 512:(kc + 1) * 512], start=(kc == 0), stop=(kc == 15)) for kc in range(16)],
                            reads=[bGR, bUT], writes=[b_PB[pb + 1]])
                        em.mm_group([lambda h, kc=kc, pb=pb, cb=cb, RO=RO: h.matmul(
                            out=PB[pb + 2][:, :], lhsT=RO[:, 6144 + cb * 512 + kc * 128: 6144 + cb * 512 + (kc + 1) * 128],
                            rhs=arT[:, kc * 512:(kc + 1) * 512], start=(kc == 0), stop=(kc == 3)) for kc in range(4)],
                            reads=[bRO, b_arT], writes=[b_PB[pb + 2]])
                        em.mm_group([lambda h, kc=kc, pb=pb, cb=cb, RO=RO: h.matmul(
                            out=PB[pb + 3][:, :], lhsT=RO[:, cb * 1536 + kc * 128: cb * 1536 + (kc + 1) * 128],
                            rhs=arT[:, (4 + kc) * 512:(5 + kc) * 512], start=(kc == 0), stop=(kc == 11)) for kc in range(12)],
                            reads=[bRO, b_arT], writes=[b_PB[pb + 3]])
                        q2 = 2 * (j % 2)
                        em.op("act", lambda h, pb=pb, q2=q2: h.activation(out=sg[q2][:, :], in_=PB[pb][:, :], func=AF.Sigmoid),
                              reads=[b_PB[pb]], writes=[b_sg[q2]])
                        em.op("act", lambda h, pb=pb, q2=q2: h.activation(out=sg[q2 + 1][:, :], in_=PB[pb + 1][:, :], func=AF.Sigmoid),
                              reads=[b_PB[pb + 1]], writes=[b_sg[q2 + 1]])
                        em.op("dve", lambda h, pb=pb, q2=q2: h.tensor_tensor(out=tm[q2][:, :], in0=PB[pb + 2][:, :], in1=sg[q2][:, :],
                                                                            op=ALU.mult), reads=[b_PB[pb + 2], b_sg[q2]], writes=[b_tm[q2]])
                        em.op("dve", lambda h, pb=pb, q2=q2: h.tensor_tensor(out=tm[q2 + 1][:, :], in0=PB[pb + 3][:, :], in1=sg[q2 + 1][:, :],
                                                                            op=ALU.mult), reads=[b_PB[pb + 3], b_sg[q2 + 1]], writes=[b_tm[q2 + 1]])
                        em.op("dve", lambda h, j=j, q2=q2: h.tensor_tensor(out=MH[:, j * 512:(j + 1) * 512], in0=tm[q2][:, :], in1=tm[q2 + 1][:, :],
                                                                          op=ALU.add), reads=[b_tm[q2], b_tm[q2 + 1]], writes=[b_MH])
                    slab_prefetch()
                if "mT" in dbg and ci == 0:
                    em.dma(dbg["mT"][:, :], MH[:, :], reads=[b_MH], writes=[b_y], sembuf=b_MH)
                for fb in range(4):
                    WO, bWO = slab_next()
                    pb = 4 * (fb % 2)
                    for kc in range(16):
                        em.mm_group([lambda h, kc=kc, tt=tt, pb=pb, WO=WO: h.matmul(
                            out=PB[pb + tt][:, :], lhsT=MH[:, kc * 512 + tt * 128: kc * 512 + (tt + 1) * 128],
                            rhs=WO[:, kc * 512:(kc + 1) * 512], start=(kc == 0), stop=(kc == 15)) for tt in range(4)],
                            reads=[bWO, b_MH], writes=[b_PB[pb + tt] for tt in range(4)])
                    for tt in range(4):
                        dst = x1[:, tt * D + fb * 512: tt * D + (fb + 1) * 512]
                        em.op("dve", lambda h, dst=dst, pb=pb, tt=tt: h.tensor_tensor(out=dst, in0=PB[pb + tt][:, :], in1=dst, op=ALU.add),
                              reads=[b_PB[pb + tt], b_x1[tt]], writes=[b_x1[tt]])
                    slab_prefetch()
                if "x1" in dbg:
                    em.dma(dbg["x1"][ci * 512:(ci + 1) * 512, :].rearrange("(t p) d -> p t d", p=128),
                           x1[:, :].rearrange("p (t d) -> p t d", d=D), reads=b_x1, writes=[b_y], sembuf=b_x1[0])
                if ci + 1 < nchunks:
                    em.dma(AU[(ci + 1) % 2][:, :], uT_d[ci + 1, :, :], reads=[b_uTd], writes=[b_AU[(ci + 1) % 2]])
                    em.dma(arT[:, :], art_d[ci + 1, :, :], reads=[b_art], writes=[b_arT])
                for tt in range(4):
                    rsa, bS = rms_stats(x1[:, tt * D:(tt + 1) * D], tt, b_x1[tt])
                    U2, bU2 = u2[tt % 2], b_u2[tt % 2]
                    em.op("dve", lambda h, U2=U2, tt=tt, rsa=rsa: h.tensor_scalar(out=U2[:, :], in0=x1[:, tt * D:(tt + 1) * D], scalar1=rsa,
                                                                                scalar2=None, op0=ALU.mult), reads=[b_x1[tt], bS], writes=[bU2])
                    pbk = 2 * (tt % 2)
                    for half in range(2):
                        bank = PB[pbk + half][:, :].bitcast(BF16)
                        em.mm_group([lambda h, kc=kc, bank=bank, U2=U2, half=half: h.transpose(
                            out=bank[:, kc * 128:(kc + 1) * 128], in_=U2[:, (half * 8 + kc) * 128:(half * 8 + kc + 1) * 128], identity=ident[:])
                            for kc in range(8)], reads=[bU2, b_ident], writes=[b_PB[pbk + half]])
                        em.op("dve", lambda h, bank=bank, half=half, tt=tt, UT=UT: h.tensor_tensor(
                            out=UT[:, half * 4096:(half + 1) * 4096].rearrange("p (kc t) -> p kc t", t=512)[:, :, tt * 128:(tt + 1) * 128],
                            in0=bank.rearrange("p (kc t) -> p kc t", t=128),
                            in1=gmT[:, half * 8:(half + 1) * 8].unsqueeze(2).broadcast_to([128, 8, 128]), op=ALU.mult),
                            reads=[b_PB[pbk + half], b_gmT], writes=[bUT])
                for qd in range(4):
                    for jj in range(4):
                        W1, bW1 = slab_next()
                        for cb in range(4):
                            hk = jj * 4 + cb
                            pb = hk % 2
                            em.mm_group([lambda h, kc=kc, pb=pb, cb=cb, W1=W1, UT=UT: h.matmul(
                                out=PB[pb][:, :], lhsT=W1[:, cb * 2048 + kc * 128: cb * 2048 + (kc + 1) * 128],
                                rhs=UT[:, kc * 512:(kc + 1) * 512], start=(kc == 0), stop=(kc == 15)) for kc in range(16)],
                                reads=[bW1, bUT], writes=[b_PB[pb]])
                            q2 = hk % 4
                            em.op("act", lambda h, pb=pb, q2=q2: h.activation(out=tm[q2][:, :], in_=PB[pb][:, :], func=AF.Relu),
                                  reads=[b_PB[pb]], writes=[b_tm[q2]])
                            em.op("dve", lambda h, hk=hk, q2=q2: h.tensor_tensor(out=MH[:, hk * 512:(hk + 1) * 512], in0=tm[q2][:, :], in1=tm[q2][:, :],
                                                                                op=ALU.mult), reads=[b_tm[q2]], writes=[b_MH])
                        slab_prefetch()
                    for fb in range(4):
                        W2, bW2 = slab_next()
                        for kc in range(16):
                            em.mm_group([lambda h, kc=kc, tt=tt, W2=W2: h.matmul(
                                out=PB[4 + tt][:, :], lhsT=MH[:, kc * 512 + tt * 128: kc * 512 + (tt + 1) * 128],
                                rhs=W2[:, kc * 512:(kc + 1) * 512], start=(kc == 0), stop=(kc == 15)) for tt in range(4)],
                                reads=[bW2, b_MH], writes=[b_PB[4 + tt] for tt in range(4)])
                        for tt in range(4):
                            dst = x1[:, tt * D + fb * 512: tt * D + (fb + 1) * 512]
                            em.op("dve", lambda h, dst=dst, tt=tt: h.tensor_tensor(out=dst, in0=PB[4 + tt][:, :], in1=dst, op=ALU.add),
                                  reads=[b_PB[4 + tt], b_x1[tt]], writes=[b_x1[tt]])
                        slab_prefetch()
                for tt in range(4):
                    rsa, bS = rms_stats(x1[:, tt * D:(tt + 1) * D], tt, b_x1[tt])
                    em.op("dve", lambda h, tt=tt, rsa=rsa: h.scalar_tensor_tensor(out=x1[:, tt * D:(tt + 1) * D], in0=x1[:, tt * D:(tt + 1) * D],
                                                                                scalar=rsa, in1=gfin[:, :], op0=ALU.mult, op1=ALU.mult),
                          reads=[b_x1[tt], bS, b_gfin], writes=[b_x1[tt]])
                em.dma(y_d[ci * 512:(ci + 1) * 512, :].rearrange("(t p) d -> p t d", p=128), x1[:, :].rearrange("p (t d) -> p t d", d=D),
                       reads=b_x1, writes=[b_y], sembuf=b_x1[0], q=STQ)
            em.flush(final_waits=[b_y])
    return nc


def _t5_bucket_np(rel):
    import math
    rel = np.asarray(rel, np.int64)
    half, max_exact = 16, 8
    n = np.abs(rel)
    nf = np.maximum(n, 1).astype(np.float64)
    large = max_exact + (np.log(nf / max_exact) / math.log(1024 / max_exact) * (half - max_exact)).astype(np.int64)
    large = np.minimum(large, half - 1)
    return np.where(rel > 0, half, 0) + np.where(n < max_exact, n, large)


def _make_bt(rel_bias):
    k = np.arange(128)[:, None]
    j = np.arange(256)[None, :]
    rel = k - j + 64
    valid = np.abs(rel) <= 64
    bt = np.full((128, 12, 256), NEG, np.float32)
    for g, d in enumerate((1, 4, 16)):
        bk = _t5_bucket_np(np.clip(rel, -64, 64) * d)
        for h in range(4):
            head = g * 4 + h
            bt[:, head, :] = np.where(valid, rel_bias[bk, head], np.float32(NEG))
    return np.ascontiguousarray(bt.reshape(128, 12 * 256))


def _make_pc(inputs):
    f = lambda a: np.asarray(a, dtype=np.float32)
    pc = np.zeros((128, 136), np.float32)
    cw = f(inputs["conv_w"])[0]
    for t in range(4):
        pc[:, t * 12:(t + 1) * 12] = cw[t].reshape(12, 128).T
    pc[:, 48:60] = f(inputs["conv_b"])[0].reshape(12, 128).T
    for i, nm in enumerate(["lru_ba", "lru_bx", "lru_lambda"]):
        a = f(inputs[nm])[0]
        for dr in range(2):
            pc[:, 60 + i * 24 + dr * 12: 60 + i * 24 + (dr + 1) * 12] = a[dr].reshape(12, 128).T
    return pc


_NC_CACHE = {}


def kernel(**inputs):
    import ml_dtypes
    f = lambda a: np.ascontiguousarray(np.asarray(a, dtype=np.float32))
    xp = f(inputs["x_prompt"])
    xs = f(inputs["x_sample"])
    common = {
        "w_in": f(inputs["w_in"])[0],
        "w_attn_o": f(inputs["w_attn_o"])[0],
        "w_rnn_o": f(inputs["w_rnn_o"])[0],
        "w_out": f(inputs["w_out"])[0],
        "w_mlp_in": f(inputs["w_mlp_in"])[0],
        "w_mlp_out": f(inputs["w_mlp_out"])[0],
        "lru_wa": f(inputs["lru_wa"])[0].reshape(24, 128, 128),
        "lru_wx": f(inputs["lru_wx"])[0].reshape(24, 128, 128),
        "gains": np.ascontiguousarray(np.stack([f(inputs["norm_mix_g"])[0], f(inputs["norm_mlp_g"])[0], f(inputs["norm_final_g"])])),
        "gT": np.ascontiguousarray(f(inputs["norm_mlp_g"])[0].reshape(16, 128).T),
        "pc": _make_pc(inputs),
        "bt": _make_bt(f(inputs["rel_bias"])),
        "ident": np.eye(128, dtype=np.float32).astype(ml_dtypes.bfloat16),
    }
    in_maps = []
    for c in range(8):
        m = dict(common)
        if c < 4:
            m["x"] = np.ascontiguousarray(np.concatenate([xs[c], xp[c]], axis=0))
            m["flag"] = np.ones((128, 1), np.float32)
        else:
            b0 = 4 + 3 * (c - 4)
            m["x"] = np.ascontiguousarray(xp[b0:b0 + 3].reshape(T, D))
            m["flag"] = np.zeros((128, 1), np.float32)
        in_maps.append(m)
    if "nc" not in _NC_CACHE:
        _NC_CACHE["nc"] = build()
    res = run_bass_kernel_spmd(_NC_CACHE["nc"], in_maps, core_ids=list(range(8)))
    y_prompt = np.empty((16, 2048, D), np.float32)
    y_sample = np.empty((4, 4096, D), np.float32)
    for c in range(8):
        y = res.results[c]["y"]
        if c < 4:
            y_sample[c] = y[:4096]
            y_prompt[c] = y[4096:]
        else:
            b0 = 4 + 3 * (c - 4)
            y_prompt[b0:b0 + 3] = y.reshape(3, 2048, D)
    return (y_prompt, y_sample)
```
